# Optimizing a Trainium2 kernel written in Bass

```python
import math
import jax, jax.numpy as jnp
from jax import lax
import numpy as np

D_MODEL = 1024
BATCH = 8
SEQ = 4096
DEPTH = 1

HEAD_DIM = 64
HEADS_PER_GROUP = 4
DILATION_PATTERNS = ((128, 1), (512, 4), (2048, 16))
N_ATTN_GROUPS = len(DILATION_PATTERNS)
N_ATTN_HEADS = N_ATTN_GROUPS * HEADS_PER_GROUP
ATTN_WIDTH = N_ATTN_HEADS * HEAD_DIM
ATTN_OUT_WIDTH = HEADS_PER_GROUP * HEAD_DIM
N_BUCKETS = 32
MAX_DISTANCE = 2048
SSM_WIDTH = D_MODEL // 2
SSM_GROUP = 16
SSM_GROUPS = SSM_WIDTH // SSM_GROUP
SSM_STATE = 64
DT_MIN = 1e-3
DT_MAX = 1e-1
D_FF = 4 * D_MODEL
ALPHA = (2.0 * DEPTH) ** 0.25
BETA = (8.0 * DEPTH) ** -0.25
LN_EPS = 1e-5
NEG_INF = -1e30
IN_WIDTH = 3 * ATTN_WIDTH + SSM_WIDTH + 2 * D_MODEL
SPLITS = (ATTN_WIDTH, 2 * ATTN_WIDTH, 3 * ATTN_WIDTH, 3 * ATTN_WIDTH + SSM_WIDTH)

kernel_name = "hybrid_s5_dilated_attn_gated_deepnorm"


def layer_norm(x, g, b):
    xf = x.astype(jnp.float32)
    mu = xf.mean(-1, keepdims=True)
    xc = xf - mu
    var = (xc * xc).mean(-1, keepdims=True)
    return (xc * lax.rsqrt(var + LN_EPS)).astype(x.dtype) * g + b


def t5_bucket(dist):
    max_exact = N_BUCKETS // 2
    d = jnp.maximum(dist, 1).astype(jnp.float32)
    large = max_exact + (jnp.log(d / max_exact) / math.log(MAX_DISTANCE / max_exact)
                         * (N_BUCKETS - max_exact)).astype(jnp.int32)
    large = jnp.minimum(large, N_BUCKETS - 1)
    return jnp.where(dist < max_exact, dist, large)


def dilated_window_attention(q, k, v, rel_bias, window, dilation):
    bsz, seqlen, nh, hd = q.shape
    span = window // dilation
    n = seqlen // dilation
    blk = min(span, n)
    nb = -(-n // blk)
    pad = nb * blk - n

    def to_blocks(t):
        t = t.reshape(bsz, n, dilation, nh, hd).transpose(0, 2, 3, 1, 4)
        t = jnp.pad(t, ((0, 0), (0, 0), (0, 0), (0, pad), (0, 0)))
        return t.reshape(bsz, dilation, nh, nb, blk, hd)

    def with_prev(t):
        prev = jnp.pad(t[:, :, :, :-1], ((0, 0), (0, 0), (0, 0), (1, 0), (0, 0), (0, 0)))
        return jnp.concatenate([prev, t], axis=4)

    qb, kb, vb = to_blocks(q), to_blocks(k), to_blocks(v)
    kw, vw = with_prev(kb), with_prev(vb)

    qi = jnp.arange(blk)[:, None]
    kj = jnp.arange(2 * blk)[None, :]
    rel = qi + blk - kj
    key_pos = jnp.arange(nb)[:, None, None] * blk - blk + kj[None]
    valid = (rel >= 0) & (rel <= span) & (key_pos >= 0)
    bias = rel_bias[t5_bucket(jnp.maximum(rel, 0) * dilation)]
    bias = bias.transpose(2, 0, 1)[:, None].astype(jnp.float32)

    logits = jnp.einsum('brhnqe,brhnke->brhnqk', qb, kw).astype(jnp.float32) * (hd ** -0.5) + bias
    logits = jnp.where(valid, logits, NEG_INF)
    m = logits.max(-1, keepdims=True)
    e = jnp.exp(logits - m)
    s = e.sum(-1, keepdims=True)
    out = jnp.einsum('brhnqk,brhnke->brhnqe', (e / s).astype(v.dtype), vw)
    lse = (m + jnp.log(s))[..., 0]

    out = out.reshape(bsz, dilation, nh, nb * blk, hd)[:, :, :, :n]
    out = out.transpose(0, 3, 1, 2, 4).reshape(bsz, seqlen, nh, hd)
    lse = lse.reshape(bsz, dilation, nh, nb * blk)[..., :n]
    lse = lse.transpose(0, 3, 1, 2).reshape(bsz, seqlen, nh)
    return out, lse


def s5_ssm(u, lambda_re, lambda_im, log_dt, b_re, b_im, c_re, c_im, d_skip):
    f32 = jnp.float32
    bsz, seqlen, _ = u.shape
    ug = u.reshape(bsz, seqlen, SSM_GROUPS, SSM_GROUP).astype(f32)
    dt = jnp.exp(log_dt.astype(f32))[:, None]
    lr, li = lambda_re.astype(f32), lambda_im.astype(f32)
    mag = jnp.exp(lr * dt)
    ab_re, ab_im = mag * jnp.cos(li * dt), mag * jnp.sin(li * dt)
    den = lr * lr + li * li
    nr = ab_re - 1.0
    k_re = (nr * lr + ab_im * li) / den
    k_im = (ab_im * lr - nr * li) / den
    bu_re = jnp.einsum('blgh,gph->blgp', ug, b_re.astype(f32))
    bu_im = jnp.einsum('blgh,gph->blgp', ug, b_im.astype(f32))
    x_re = k_re * bu_re - k_im * bu_im
    x_im = k_re * bu_im + k_im * bu_re
    a_re = jnp.broadcast_to(ab_re, x_re.shape)
    a_im = jnp.broadcast_to(ab_im, x_im.shape)

    def combine(left, right):
        a1r, a1i, b1r, b1i = left
        a2r, a2i, b2r, b2i = right
        return (a1r * a2r - a1i * a2i,
                a1r * a2i + a1i * a2r,
                a2r * b1r - a2i * b1i + b2r,
                a2r * b1i + a2i * b1r + b2i)

    _, _, h_re, h_im = lax.associative_scan(combine, (a_re, a_im, x_re, x_im), axis=1)
    y = (jnp.einsum('blgp,ghp->blgh', h_re, c_re.astype(f32))
         - jnp.einsum('blgp,ghp->blgh', h_im, c_im.astype(f32))
         + d_skip.astype(f32) * ug)
    return y.reshape(bsz, seqlen, SSM_WIDTH).astype(u.dtype)


def setup_inputs(seed: int = 0) -> dict:
    key = jax.random.key(seed)
    ks = jax.random.split(key, 24)
    nrm = jax.random.normal
    sd = D_MODEL ** -0.5
    x = nrm(ks[0], (BATCH, SEQ, D_MODEL), jnp.float32)
    w_in = jnp.concatenate([
        nrm(ks[1], (DEPTH, D_MODEL, 2 * ATTN_WIDTH)) * sd,
        nrm(ks[2], (DEPTH, D_MODEL, ATTN_WIDTH)) * (BETA * sd),
        nrm(ks[3], (DEPTH, D_MODEL, SSM_WIDTH)) * sd,
        nrm(ks[4], (DEPTH, D_MODEL, 2 * D_MODEL)) * sd,
    ], axis=-1)
    b_gate = 0.02 * nrm(ks[5], (DEPTH, 2 * D_MODEL))
    lambda_re = -0.5 + 0.01 * nrm(ks[6], (DEPTH, SSM_GROUPS, SSM_STATE))
    lambda_im = jnp.broadcast_to(math.pi * jnp.arange(SSM_STATE, dtype=jnp.float32),
                                 (DEPTH, SSM_GROUPS, SSM_STATE)) + 0.0
    log_dt = jax.random.uniform(ks[7], (DEPTH, SSM_GROUPS), jnp.float32,
                                math.log(DT_MIN), math.log(DT_MAX))
    bs = (2.0 * SSM_GROUP) ** -0.5
    ssm_b_re = nrm(ks[8], (DEPTH, SSM_GROUPS, SSM_STATE, SSM_GROUP)) * bs
    ssm_b_im = nrm(ks[9], (DEPTH, SSM_GROUPS, SSM_STATE, SSM_GROUP)) * bs
    ssm_c_re = nrm(ks[10], (DEPTH, SSM_GROUPS, SSM_GROUP, SSM_STATE)) * 0.5
    ssm_c_im = nrm(ks[11], (DEPTH, SSM_GROUPS, SSM_GROUP, SSM_STATE)) * 0.5
    ssm_d = nrm(ks[12], (DEPTH, SSM_GROUPS, SSM_GROUP))
    w_glu = nrm(ks[13], (DEPTH, SSM_WIDTH, 2 * SSM_WIDTH)) * SSM_WIDTH ** -0.5
    w_ssm_proj = nrm(ks[14], (DEPTH, SSM_WIDTH, D_MODEL)) * SSM_WIDTH ** -0.5
    rel_bias = 0.1 * nrm(ks[15], (N_BUCKETS, N_ATTN_HEADS))
    w_attn_proj = nrm(ks[16], (DEPTH, ATTN_OUT_WIDTH, D_MODEL)) * ATTN_OUT_WIDTH ** -0.5
    w_out = nrm(ks[17], (DEPTH, D_MODEL, D_MODEL)) * (BETA * sd)
    ln1_g = 1.0 + 0.02 * nrm(ks[18], (DEPTH, D_MODEL))
    ln1_b = 0.02 * nrm(ks[19], (DEPTH, D_MODEL))
    w_up = nrm(ks[20], (DEPTH, D_MODEL, D_FF)) * (BETA * sd)
    w_down = nrm(ks[21], (DEPTH, D_FF, D_MODEL)) * (BETA * D_FF ** -0.5)
    ln2_g = 1.0 + 0.02 * nrm(ks[22], (DEPTH, D_MODEL))
    ln2_b = 0.02 * nrm(ks[23], (DEPTH, D_MODEL))
    return {"x": x, "w_in": w_in, "b_gate": b_gate, "lambda_re": lambda_re,
            "lambda_im": lambda_im, "log_dt": log_dt, "ssm_b_re": ssm_b_re,
            "ssm_b_im": ssm_b_im, "ssm_c_re": ssm_c_re, "ssm_c_im": ssm_c_im,
            "ssm_d": ssm_d, "w_glu": w_glu, "w_ssm_proj": w_ssm_proj,
            "rel_bias": rel_bias, "w_attn_proj": w_attn_proj, "w_out": w_out,
            "ln1_g": ln1_g, "ln1_b": ln1_b, "w_up": w_up, "w_down": w_down,
            "ln2_g": ln2_g, "ln2_b": ln2_b}


def reference(x, w_in, b_gate, lambda_re, lambda_im, log_dt, ssm_b_re, ssm_b_im,
              ssm_c_re, ssm_c_im, ssm_d, w_glu, w_ssm_proj, rel_bias, w_attn_proj,
              w_out, ln1_g, ln1_b, w_up, w_down, ln2_g, ln2_b):
    bsz, seqlen, _ = x.shape
    h = x
    for l in range(DEPTH):
        z = h @ w_in[l]
        q, k, v, u, g = jnp.split(z, SPLITS, axis=-1)
        qh = q.reshape(bsz, seqlen, N_ATTN_GROUPS, HEADS_PER_GROUP, HEAD_DIM)
        kh = k.reshape(bsz, seqlen, N_ATTN_GROUPS, HEADS_PER_GROUP, HEAD_DIM)
        vh = v.reshape(bsz, seqlen, N_ATTN_GROUPS, HEADS_PER_GROUP, HEAD_DIM)

        outs, lses = [], []
        for gi, (window, dilation) in enumerate(DILATION_PATTERNS):
            o, s = dilated_window_attention(
                qh[:, :, gi], kh[:, :, gi], vh[:, :, gi],
                rel_bias[:, gi * HEADS_PER_GROUP:(gi + 1) * HEADS_PER_GROUP],
                window, dilation)
            outs.append(o)
            lses.append(s)
        wts = jax.nn.softmax(jnp.stack(lses), axis=0)
        y_attn = (wts[..., None] * jnp.stack(outs).astype(jnp.float32)).sum(0)
        y_attn = y_attn.astype(h.dtype).reshape(bsz, seqlen, ATTN_OUT_WIDTH)

        y_ssm = jax.nn.gelu(s5_ssm(u, lambda_re[l], lambda_im[l], log_dt[l], ssm_b_re[l],
                                   ssm_b_im[l], ssm_c_re[l], ssm_c_im[l], ssm_d[l]))
        glu_a, glu_b = jnp.split(y_ssm @ w_glu[l], 2, axis=-1)
        y_ssm = glu_a * jax.nn.sigmoid(glu_b)

        gate_ssm, gate_attn = jnp.split(jax.nn.sigmoid(g + b_gate[l]), 2, axis=-1)
        mix = (gate_ssm * (y_ssm @ w_ssm_proj[l]) + gate_attn * (y_attn @ w_attn_proj[l])) @ w_out[l]
        h = layer_norm(ALPHA * h + mix, ln1_g[l], ln1_b[l])

        ff = jnp.square(jax.nn.relu(h @ w_up[l])) @ w_down[l]
        h = layer_norm(ALPHA * h + ff, ln2_g[l], ln2_b[l])
    return h
```

```python
import math
from contextlib import ExitStack

import numpy as np
import concourse.bass as bass
import concourse.mybir as mybir
from concourse.bass_utils import run_bass_kernel_spmd

F32 = mybir.dt.float32
BF16 = mybir.dt.bfloat16
I32 = mybir.dt.int32
AF = mybir.ActivationFunctionType
ALU = mybir.AluOpType

D = 1024
L = 4096
NCORES = 8
DFF = 4096
INW = 4864
ALPHA = 2.0 ** 0.25
LN_EPS = 1e-5
NEG = -30000.0
TWO_PI = 6.283185
HALF_PI = 1.5707963
MAGIC = 12582912.0
DIL = (1, 4, 16)


class Res:
    __slots__ = ("w", "r", "rd", "excl")

    def __init__(self, excl=False):
        self.excl = excl
        self.w = None
        self.r = {}
        self.rd = []


class Op:
    __slots__ = ("eng", "emit", "deps", "dma", "sem", "val", "prev", "used", "seq")
    _seq = 0

    def __init__(self, eng, emit, dma):
        Op._seq += 1
        self.seq = Op._seq
        self.eng = eng
        self.emit = emit
        self.dma = dma
        self.deps = []
        self.sem = None
        self.val = 0
        self.prev = 0
        self.used = False


class Prog:
    ENGS = ("pe", "act", "dve", "pool", "sp")
    EPOCH = 20000

    def __init__(self):
        self.ops = {e: [] for e in self.ENGS}

    def add(self, eng, emit, reads=(), writes=(), dma=False):
        op = Op(eng, emit, dma)
        deps = {}
        ex = [r for r in reads if r.excl]
        if ex:
            writes = list(writes) + [r for r in ex if r not in writes]
            reads = [r for r in reads if not r.excl]

        def dep(o):
            if o is not None and o is not op:
                deps[id(o)] = o

        for r in reads:
            dep(r.w)
        for w in writes:
            if w.w is not None and (dma or w.w.dma or w.w.eng != eng or eng != "pe"):
                dep(w.w)
            for e, o in w.r.items():
                if dma or e != eng or eng != "pe":
                    dep(o)
            for o in w.rd:
                dep(o)
        for r in reads:
            if dma:
                r.rd.append(op)
            else:
                r.r[eng] = op
        for w in writes:
            w.w = op
            w.r = {}
            w.rd = []
        op.deps = list(deps.values())
        for o in op.deps:
            o.used = True
        self.ops[eng].append(op)
        return op

    def finalize(self, nc, es, ndma=12):
        self.sems = {}
        for e in self.ENGS:
            cnt = 0
            dcnt = 0
            dsems = []
            esems = []
            for op in self.ops[e]:
                if op.dma:
                    if len(dsems) < ndma:
                        dsems.append([es.enter_context(nc.semaphore(f"d_{e}_{len(dsems)}")), 0])
                    slot = dsems[dcnt % ndma]
                    dcnt += 1
                    op.prev = slot[1]
                    slot[1] += 16
                    op.sem = slot[0]
                    op.val = slot[1]
                elif op.used:
                    ep = cnt // self.EPOCH
                    if ep >= len(esems):
                        esems.append(es.enter_context(nc.semaphore(f"c_{e}_{ep}")))
                    op.sem = esems[ep]
                    op.val = cnt % self.EPOCH + 1
                    cnt += 1

    def emit_engine(self, e, eng):
        waited = {}

        def wait(sem, val):
            k = id(sem)
            if waited.get(k, 0) < val:
                eng.wait_ge(sem, val)
                waited[k] = val

        for op in self.ops[e]:
            for d in op.deps:
                wait(d.sem, d.val)
            if op.dma and op.prev > 0:
                wait(op.sem, op.prev)
            ins = op.emit(eng)
            if op.sem is not None:
                ins.then_inc(op.sem, 16 if op.dma else 1)
        last = {}
        for op in self.ops[e]:
            if op.dma:
                last[id(op.sem)] = (op.sem, op.val)
        for sem, val in last.values():
            wait(sem, val)


class Buf:
    def __init__(self, ap, nres=1, excl=False):
        self.ap = ap
        self.res = [Res(excl) for _ in range(nres)]

    @property
    def r0(self):
        return self.res[0]


class Arena:
    def __init__(self, nc, es, nbytes):
        self.t = es.enter_context(nc.sbuf_tensor("arena", [128, nbytes // 4], F32))
        self.cap = nbytes
        self.off = 0
        self.limit = nbytes
        self.hist = []

    def seek(self, off, limit=None):
        self.off = off
        self.limit = self.cap if limit is None else limit

    def alloc(self, shape, dtype, nres=1):
        n = int(np.prod(shape))
        nb = n * (2 if dtype == BF16 else 4)
        off = (self.off + 63) // 64 * 64
        assert off + nb <= self.limit, f"SBUF arena overflow {off + nb} > {self.limit}"
        self.off = off + nb
        v = self.t[:, off // 4:(off + nb + 3) // 4]
        if dtype != F32:
            v = v.bitcast(dtype)[:, 0:n]
        if len(shape) > 1:
            names = " ".join(f"a{i}" for i in range(len(shape)))
            kw = {f"a{i}": int(s) for i, s in enumerate(shape)}
            v = v.rearrange(f"p ({names}) -> p {names}", **kw)
        b = Buf(v, nres)
        seed_r, seed_rd = {}, []
        for (s0, s1, ob) in self.hist:
            if s0 < off + nb and off < s1:
                for rs in ob.res:
                    cands = list(rs.r.values()) + ([rs.w] if rs.w is not None else [])
                    for o in cands:
                        if o.dma:
                            seed_rd.append(o)
                        elif o.eng not in seed_r or seed_r[o.eng].seq < o.seq:
                            seed_r[o.eng] = o
                    seed_rd.extend(rs.rd)
        for rs in b.res:
            rs.r = dict(seed_r)
            rs.rd = list(seed_rd)
        self.hist.append((off, off + nb, b))
        return b


def _t5_bucket_np(dist):
    max_exact = 16
    d = np.maximum(dist, 1).astype(np.float32)
    large = max_exact + (np.log(d / np.float32(max_exact)) / np.float32(math.log(2048 / max_exact))
                         * np.float32(32 - max_exact)).astype(np.int32)
    large = np.minimum(large, 31)
    return np.where(dist < max_exact, dist, large)


def _layout(items):
    cols, off = {}, 0
    for name, n in items:
        cols[name] = (off, n)
        off += n
    return cols, off


SP_COLS, SP_N = _layout((("ident", 128), ("iota", 512), ("kvals", 16), ("sgn", 1), ("maskM", 128),
                         ("bgate", 16), ("lamre", 32), ("lamim", 32), ("logdt", 32)))
SB_COLS, SB_N = _layout((("lamre", 32), ("lamim", 32), ("logdt", 32), ("drep", 32),
                         ("b1", 512), ("b2raw", 512), ("c1raw", 512), ("c2", 512)))


def _put(a, cols, name, v):
    o, n = cols[name]
    a[:, o:o + n] = np.asarray(v, np.float32).reshape(128, n)


def build_smalls_p(b_gate, lambda_re, lambda_im, log_dt):
    a = np.zeros((128, SP_N), np.float32)
    _put(a, SP_COLS, "ident", np.eye(128, dtype=np.float32))
    _put(a, SP_COLS, "iota", np.broadcast_to(np.arange(512, dtype=np.float32)[None, :], (128, 512)))
    _put(a, SP_COLS, "kvals", np.broadcast_to(
        np.array([1, 2, 3, 4, 5, 6, 7, 8, 7, 6, 5, 4, 3, 2, 1, 0], np.float32)[None, :], (128, 16)))
    _put(a, SP_COLS, "sgn", np.concatenate([np.ones(64), -np.ones(64)])[:, None])
    s_idx = np.arange(128) // 16
    _put(a, SP_COLS, "maskM", (s_idx[None, :] >= s_idx[:, None]).astype(np.float32))
    _put(a, SP_COLS, "bgate", b_gate.reshape(16, 128).T)
    lr, li = lambda_re.T, lambda_im.T
    _put(a, SP_COLS, "lamre", np.concatenate([lr, lr], 0))
    _put(a, SP_COLS, "lamim", np.concatenate([li, li], 0))
    _put(a, SP_COLS, "logdt", np.broadcast_to(log_dt[None, :], (128, 32)))
    return a


def build_smalls_b(lambda_re, lambda_im, log_dt, b_re, b_im, c_re, c_im, d_skip):
    a = np.zeros((128, SB_N), np.float32)
    lr, li = lambda_re.T, lambda_im.T
    _put(a, SB_COLS, "lamre", np.concatenate([lr, lr], 0))
    _put(a, SB_COLS, "lamim", np.concatenate([li, li], 0))
    _put(a, SB_COLS, "logdt", np.broadcast_to(log_dt[None, :], (128, 32)))
    _put(a, SB_COLS, "drep", np.tile(d_skip.T, (8, 1)))
    bre, bim = b_re.transpose(1, 0, 2), b_im.transpose(1, 0, 2)
    _put(a, SB_COLS, "b1", np.concatenate([bre, bim], 0))
    _put(a, SB_COLS, "b2raw", np.concatenate([bim, bre], 0))
    cre, cim = c_re.transpose(2, 0, 1), c_im.transpose(2, 0, 1)
    _put(a, SB_COLS, "c1raw", np.concatenate([cre, cim], 0))
    _put(a, SB_COLS, "c2", np.concatenate([cim, cre], 0))
    return a


def build_bias_tables(rel_bias):
    out = np.full((6, 128, 2, 512), NEG, np.float32)
    j = np.arange(128)[:, None]
    a = np.arange(128)[None, :]
    for hp in range(2):
        for g in range(3):
            rnd = hp * 3 + g
            for h2 in range(2):
                head = g * 4 + hp * 2 + h2
                rel_prev = a + 128 - j
                rel_cur = a - j
                for kb, rel in ((0, rel_prev), (1, rel_cur)):
                    valid = (rel >= 0) & (rel <= 128)
                    idx = _t5_bucket_np(np.maximum(rel, 0) * DIL[g])
                    vals = np.where(valid, rel_bias[idx, head], np.float32(NEG))
                    c0 = h2 * 256 + kb * 128
                    out[rnd, :, 0, c0:c0 + 128] = vals
                    if kb == 1:
                        out[rnd, :, 1, c0:c0 + 128] = vals
    return out


KB = 1024


def build_program(debug=(), stages=("A", "U", "ATT", "SSM", "P3")):
    nc = bass.Bass("TRN2", target_bir_lowering=False)
    es = ExitStack()
    P = Prog()

    def dram_in(name, shape, dt=F32):
        return nc.dram_tensor(name, list(shape), dt, kind="ExternalInput").ap()

    x_d = dram_in("x", [L, D])
    w_in_d = dram_in("w_in", [D, INW])
    smP_d = dram_in("smalls_p", [128, SP_N])
    smB_d = dram_in("smalls_b", [128, SB_N])
    w_glu_d = dram_in("w_glu", [512, 1024])
    biasT_d = dram_in("bias_t", [6, 128, 2, 512])
    wsp_d = dram_in("w_ssm_proj", [512, D])
    wap_d = dram_in("w_attn_proj", [256, D])
    wout_d = dram_in("w_out", [D, D])
    wup_d = dram_in("w_up", [D, DFF])
    wdn_d = dram_in("w_down", [DFF, D])
    lnrows_d = dram_in("ln_rows", [4, 128, D])
    out_d = nc.dram_tensor("out", [L, D], F32, kind="ExternalOutput").ap()
    dbg = {}

    ar = Arena(nc, es, 206 * KB)
    ps_all = es.enter_context(nc.psum_tensor("ps", [128, 4096], F32))
    banks = [Buf(ps_all[:, b * 512:(b + 1) * 512], excl=True) for b in range(8)]

    def bank_bf(b):
        return banks[b].ap.bitcast(BF16)

    def R(*bufs):
        out = []
        for b in bufs:
            if isinstance(b, Buf):
                out.extend(b.res)
            elif isinstance(b, Res):
                out.append(b)
            else:
                out.extend(R(*b))
        return out

    def dma(eng, out, in_, reads, writes):
        return P.add(eng, lambda e: e.dma_start(out=out, in_=in_), R(*reads), R(*writes), dma=True)

    def mm(out, lhsT, rhs, start, stop, reads, writes, **kw):
        return P.add("pe", lambda e: e.matmul(out, lhsT, rhs, start=start, stop=stop, **kw),
                     R(*reads), R(*writes))

    def tr(out, in_, ident, reads, writes):
        return P.add("pe", lambda e: e.transpose(out, in_, ident), R(*reads), R(*writes))

    def act(out, in_, func, reads, writes, bias=0.0, scale=1.0):
        return P.add("act", lambda e: e.activation(out, in_, func, bias=bias, scale=scale),
                     R(*reads), R(*writes))

    def tt(eng, out, in0, in1, op, reads, writes):
        return P.add(eng, lambda e: e.tensor_tensor(out, in0, in1, op), R(*reads), R(*writes))

    def ts(eng, out, in0, s1, s2, op0, op1, reads, writes):
        if s2 is None:
            return P.add(eng, lambda e: e.tensor_scalar(out, in0, s1, None, op0), R(*reads), R(*writes))
        return P.add(eng, lambda e: e.tensor_scalar(out, in0, s1, s2, op0, op1), R(*reads), R(*writes))

    def stt(out, in0, scalar, in1, op0, op1, reads, writes):
        return P.add("dve", lambda e: e.scalar_tensor_tensor(out, in0, scalar, in1, op0, op1),
                     R(*reads), R(*writes))

    def cp(eng, out, in_, reads, writes):
        if eng == "act":
            return P.add("act", lambda e: e.copy(out, in_), R(*reads), R(*writes))
        return P.add(eng, lambda e: e.tensor_copy(out, in_), R(*reads), R(*writes))

    def scan(dst, d0, src, extra_reads):
        return P.add("dve", lambda e: e.tensor_tensor_scan(dst.ap, d0, src.ap, 0.0, ALU.mult, ALU.add),
                     R(src, *extra_reads), R(dst))

    def memset(eng, out, val, writes):
        return P.add(eng, lambda e: e.memset(out, val), (), R(*writes))

    def dump(name, buf_ap, shape, dt, reads):
        d = nc.dram_tensor("dbg_" + name, list(shape), dt, kind="ExternalOutput").ap()
        dbg[name] = d
        dma("sp", d, buf_ap, reads, [])

    ar.seek(0, 8 * KB)
    smP = ar.alloc([SP_N], F32)
    dma("sp", smP.ap, smP_d, [], [smP])

    def sp_(name):
        o, n = SP_COLS[name]
        return smP.ap[:, o:o + n]

    ident_f = sp_("ident")
    identb = ar.alloc([128], BF16)
    cp("dve", identb.ap, ident_f, [smP], [identb])
    onesAB = ar.alloc([2, 128], BF16)
    memset("pool", onesAB.ap, 0.0, [onesAB])
    memset("pool", onesAB.ap[:, 0, 0:64], 1.0, [onesAB])
    memset("pool", onesAB.ap[:, 1, 64:128], 1.0, [onesAB])

    rp = ar.alloc([32], F32)
    f8p = ar.alloc([32], F32)
    e_dt = ar.alloc([32], F32)
    e_a = ar.alloc([32], F32)
    e_b = ar.alloc([32], F32)
    act(e_dt.ap, sp_("logdt"), AF.Exp, [smP], [e_dt])
    tt("dve", e_a.ap, sp_("lamre"), e_dt.ap, ALU.mult, [smP, e_dt], [e_a])
    act(rp.ap, e_a.ap, AF.Exp, [e_a], [rp], scale=8.0)
    tt("dve", e_a.ap, sp_("lamim"), e_dt.ap, ALU.mult, [smP, e_dt], [e_a])
    ts("dve", e_a.ap, e_a.ap, 8.0, 1.0 / (2.0 * math.pi), ALU.mult, ALU.mult, [e_a], [e_a])
    ts("dve", e_b.ap, e_a.ap, MAGIC, -MAGIC, ALU.add, ALU.add, [e_a], [e_b])
    tt("dve", e_b.ap, e_a.ap, e_b.ap, ALU.subtract, [e_a, e_b], [e_b])
    stt(f8p.ap, e_b.ap, 0.0, e_b.ap, ALU.is_lt, ALU.add, [e_b], [f8p])
    tabs_d = nc.dram_tensor("tabs_scratch", [32, 128, 2, 512], F32, kind="Internal").ap()
    tabs_res = Buf(None, 16)
    xTs_d = nc.dram_tensor("xT_scratch", [128, 8, L], BF16, kind="Internal").ap()
    xTs_res = Buf(None, 4)

    ar.seek(8 * KB, 72 * KB)
    xT = ar.alloc([8, L], BF16, nres=32)
    ar.seek(72 * KB, 104 * KB)
    Ug = ar.alloc([32, 512], BF16, nres=32)
    ar.seek(104 * KB, 120 * KB)
    yattnT = ar.alloc([2, L], BF16, nres=2)
    T0, T1 = 120 * KB, 206 * KB

    ar.seek(160 * KB, T1)
    xs = [ar.alloc([D], F32) for _ in range(3)]
    pi = 0
    for tt_i in range(32):
        xb = xs[tt_i % 3]
        dma("sp", xb.ap, x_d[tt_i * 128:(tt_i + 1) * 128, :], [], [xb])
        for half in range(2):
            bk = banks[pi % 2]
            pi += 1
            for j in range(4):
                kc = half * 4 + j
                tr(bk.ap[:, j * 128:(j + 1) * 128], xb.ap[:, kc * 128:(kc + 1) * 128], ident_f,
                   [xb, smP], [bk])
            dst = xT.ap[:, half * 4:half * 4 + 4, tt_i * 128:(tt_i + 1) * 128]
            src = bk.ap.rearrange("p (j t) -> p j t", j=4)
            cp("dve" if half == 0 else "act", dst, src, [bk], [xT.res[tt_i]])
        if tt_i % 8 == 7:
            b4 = tt_i // 8
            dma("sp", xTs_d[:, :, b4 * 1024:(b4 + 1) * 1024], xT.ap[:, :, b4 * 1024:(b4 + 1) * 1024],
                [xT.res[b4 * 8 + i] for i in range(8)], [xTs_res.res[b4]])
    if "xT" in debug:
        dump("xT", xT.ap, [128, 8, L], BF16, [xT])

    if "U" in stages:
        ar.seek(T0, T1)
        wu = ar.alloc([8, 512], BF16)
        dma("pool", wu.ap, w_in_d[:, 2304:2816].rearrange("(kc p) j -> p kc j", p=128), [], [wu])
        Uc = ar.alloc([4, 32, 128], BF16, nres=4)
        xT5 = xT.ap.rearrange("p k (b c s) -> p k b c s", b=4, s=8)
        ar.seek(172 * KB, T1)
        gcs = [ar.alloc([2, 2, 512], F32) for _ in range(2)]
        gAs = [ar.alloc([2, 512], F32) for _ in range(2)]
        gFs = [ar.alloc([2, 512], F32) for _ in range(2)]
        iota = sp_("iota")

        def gen_tables(bt):
            cs_, tA, tF = gcs[bt % 2], gAs[bt % 2], gFs[bt % 2]
            tt("dve", tA.ap, f8p.ap[:, 2 * bt:2 * bt + 2, None].to_broadcast([128, 2, 512]),
               iota[:, None, :].to_broadcast([128, 2, 512]), ALU.mult, [f8p, smP], [tA])
            ts("dve", tF.ap, tA.ap, MAGIC, -MAGIC, ALU.add, ALU.add, [tA], [tF])
            tt("dve", tF.ap, tA.ap, tF.ap, ALU.subtract, [tA, tF], [tF])
            act(cs_.ap[:, :, 1, :], tF.ap, AF.Sin, [tF], [cs_], scale=TWO_PI)
            act(cs_.ap[:, :, 0, :], tF.ap, AF.Sin, [tF], [cs_], scale=TWO_PI / 2.0)
            act(cs_.ap[:, :, 0, :], cs_.ap[:, :, 0, :], AF.Square, [cs_], [cs_])
            ts("pool", cs_.ap[:, :, 0, :], cs_.ap[:, :, 0, :], -2.0, 1.0, ALU.mult, ALU.add, [cs_], [cs_])
            dma("sp", tabs_d[2 * bt:2 * bt + 2].rearrange("g p t c -> p g t c"), cs_.ap, [cs_],
                [tabs_res.res[bt]])

        for blk in range(4):
            for s in range(8):
                if s % 2 == 0:
                    gen_tables(blk * 4 + s // 2)
                bk = banks[pi % 2]
                pi += 1
                for kc in range(8):
                    mm(bk.ap, xT5[:, kc, blk, :, s], wu.ap[:, kc, :], kc == 0, kc == 7,
                       [xT.res[blk * 8 + i] for i in range(8)] + [wu], [bk])
                cp("act" if s % 2 else "dve", Uc.ap[:, blk, :, s * 16:(s + 1) * 16],
                   bk.ap.rearrange("p (g h) -> p g h", g=32), [bk], [Uc.res[blk]])
        for g in range(32):
            b = 2 + pi % 2
            pi += 1
            bkb = bank_bf(b)
            for blk in range(4):
                tr(bkb[:, blk * 128:(blk + 1) * 128], Uc.ap[:, blk, g, :], identb.ap,
                   [Uc.res[blk], identb], [banks[b]])
            cp("act" if g % 2 else "dve", Ug.ap[:, g, :], bkb[:, 0:512], [banks[b]], [Ug.res[g]])
        if "Ug" in debug:
            dump("Ug", Ug.ap, [128, 32, 512], BF16, [Ug])

    if "ATT" in stages:
        ar.seek(T0, T1)
        biasb = [ar.alloc([2, 512], BF16) for _ in range(1)]
        wq = [ar.alloc([8, 384], BF16) for _ in range(1)]
        qT = ar.alloc([2, L], BF16)
        kT = ar.alloc([L], BF16)
        memset("dve", qT.ap, 0.0, [qT])
        V = ar.alloc([32, 2, 128], BF16, nres=32)
        acc = ar.alloc([2, L], F32)
        PT = [ar.alloc([512], BF16) for _ in range(2)]
        memset("dve", V.ap, 0.0, [V])
        pti = 0
        si = 0
        for hp in range(2):
            for g in range(3):
                rnd = hp * 3 + g
                dil = DIL[g]
                nblk = 32 // dil
                bb = biasb[0]
                wb = wq[0]
                dma("pool", bb.ap, biasT_d[rnd], [], [bb])
                for j, c0 in enumerate((0, 768, 1536)):
                    col = c0 + g * 256 + hp * 128
                    dma("pool", wb.ap[:, :, j * 128:(j + 1) * 128],
                        w_in_d[:, col:col + 128].rearrange("(kc p) j -> p kc j", p=128), [], [wb])
                for which, dstT in ((0, qT), (1, kT)):
                    for nt in range(8):
                        bk = banks[pi % 2]
                        pi += 1
                        for kc in range(8):
                            mm(bk.ap, wb.ap[:, kc, which * 128:(which + 1) * 128],
                               xT.ap[:, kc, nt * 512:(nt + 1) * 512], kc == 0, kc == 7,
                               [wb] + [xT.res[nt * 4 + i] for i in range(4)], [bk])
                        cs_ = slice(nt * 512, (nt + 1) * 512)
                        if which == 0:
                            ts("dve", qT.ap[0:64, 0, cs_], bk.ap[0:64, :], 0.125, None, ALU.mult, None, [bk], [qT])
                            ts("dve", qT.ap[64:128, 1, cs_], bk.ap[64:128, :], 0.125, None, ALU.mult, None,
                               [bk], [qT])
                        else:
                            cp("dve", kT.ap[:, cs_], bk.ap, [bk], [kT])
                xTd = xT.ap.rearrange("p k (i a d) -> p k d i a", a=128, d=dil)
                for t4 in range(8):
                    b = 6 + t4 % 2
                    bk = banks[b]
                    for j in range(4):
                        tile = t4 * 4 + j
                        r, i = tile // nblk, tile % nblk
                        for kc in range(8):
                            mm(bk.ap[:, j * 128:(j + 1) * 128], xTd[:, kc, r, i, :], wb.ap[:, kc, 256:384],
                               kc == 0, kc == 7, [xT, wb], [bk])
                    src = bk.ap.rearrange("p (t c) -> p t c", t=4)
                    cp("dve", V.ap[:, t4 * 4:t4 * 4 + 4, 0, 0:64], src[:, :, 0:64], [bk],
                       [V.res[t4 * 4 + j] for j in range(4)])
                    cp("dve", V.ap[:, t4 * 4:t4 * 4 + 4, 1, 64:128], src[:, :, 64:128], [bk],
                       [V.res[t4 * 4 + j] for j in range(4)])
                qTd = qT.ap.rearrange("p h (i a d) -> p h d i a", a=128, d=dil)
                kTd = kT.ap.rearrange("p (i a d) -> p d i a", a=128, d=dil)
                accd = acc.ap.rearrange("p c (i a d) -> p c d i a", a=128, d=dil)
                tiles = [(r, i) for r in range(dil) for i in range(nblk)]

                def s_part(r, i, sb, pt):
                    first = (i == 0)
                    mm(sb.ap, identb.ap, bb.ap[:, 1 if first else 0, :], True, False, [identb, bb], [sb])
                    combos = [(h2, kb) for h2 in range(2) for kb in range(2) if not (first and kb == 0)]
                    for n_, (h2, kb) in enumerate(combos):
                        c0 = h2 * 256 + kb * 128
                        mm(sb.ap[:, c0:c0 + 128], kTd[:, r, i - 1 + kb, :], qTd[:, h2, r, i, :],
                           False, n_ == len(combos) - 1, [kT, qT], [sb])
                    act(pt.ap, sb.ap, AF.Exp, [sb], [pt])

                def pv_part(r, i, pb, pt):
                    first = (i == 0)
                    combos = [(h2, kb) for h2 in range(2) for kb in range(2) if not (first and kb == 0)]
                    for part in range(2):
                        for n_, (h2, kb) in enumerate(combos):
                            tl = r * nblk + i - 1 + kb
                            c0 = h2 * 256 + kb * 128
                            lhs = V.ap[:, tl, h2, :] if part == 0 else onesAB.ap[:, h2, :]
                            mm(pb.ap[:, part * 128:(part + 1) * 128], lhs, pt.ap[:, c0:c0 + 128],
                               n_ == 0, n_ == len(combos) - 1, [V.res[tl], onesAB, pt], [pb])
                    dsta = accd[:, :, r, i, :]
                    src = pb.ap[:, 0:256].rearrange("p (c a) -> p c a", c=2)
                    if g == 0:
                        cp("dve", dsta, src, [pb], [acc])
                    else:
                        tt("dve", dsta, dsta, src, ALU.add, [acc, pb], [acc])

                prev = None
                for (r, i) in tiles:
                    sb = banks[2 + si % 2]
                    pb = banks[4 + si % 2]
                    pt = PT[si % 2]
                    si += 1
                    s_part(r, i, sb, pt)
                    if prev is not None:
                        pv_part(*prev)
                    prev = (r, i, pb, pt)
                pv_part(*prev)
            P.add("dve", lambda e: e.reciprocal(acc.ap[:, 1, :], acc.ap[:, 1, :]), R(acc), R(acc))
            tt("dve", yattnT.ap[:, hp, :], acc.ap[:, 0, :], acc.ap[:, 1, :], ALU.mult, [acc], [yattnT.res[hp]])
        if "yattnT" in debug:
            dump("yattnT", yattnT.ap, [128, 2, L], BF16, [yattnT])

    if "SSM" in stages:
        ar.seek(8 * KB, 40 * KB)
        Yc = ar.alloc([4, 8, 512], BF16, nres=4)
        ar.seek(T0, 161 * KB)
        MT = ar.alloc([32, 128], BF16)
        L1b = ar.alloc([32, 128], BF16)
        L2b = ar.alloc([32, 128], BF16)
        WS1T = ar.alloc([32, 128], BF16)
        WS2T = ar.alloc([32, 128], BF16)
        ar.seek(161 * KB, T1)
        dt_ = ar.alloc([32], F32)
        lrdt = ar.alloc([32], F32)
        th = ar.alloc([32], F32)
        den = ar.alloc([32], F32)
        t32a = ar.alloc([32], F32)
        t32b = ar.alloc([32], F32)
        kre = ar.alloc([32], F32)
        kim = ar.alloc([32], F32)
        argm = ar.alloc([32, 16], F32)
        magp = ar.alloc([32, 16], F32)
        magn = ar.alloc([32, 16], F32)
        turns = ar.alloc([32, 16], F32)
        ti = ar.alloc([32, 16], I32)
        tf = argm
        fr = ar.alloc([32, 16], F32)
        msf = ar.alloc([32, 16], F32)
        frc = turns
        msc = msf
        c16 = ar.alloc([32, 16], F32)
        s16 = ar.alloc([32, 16], F32)
        Pc = ar.alloc([32, 16], F32)
        Ps = ar.alloc([32, 16], F32)
        Nc = ar.alloc([32, 16], F32)
        Ns = ar.alloc([32, 16], F32)
        b2 = c16
        c1 = s16
        Xb = ar.alloc([32, 16], F32)
        Yb = ar.alloc([32, 16], F32)
        tx = magn
        ar.seek(40 * KB, 72 * KB)
        smB = ar.alloc([SB_N], F32)
        dma("sp", smB.ap, smB_d, [], [smB])

        def sb_(name):
            o, n = SB_COLS[name]
            return smB.ap[:, o:o + n]

        E = "dve"
        lamre, lamim = sb_("lamre"), sb_("lamim")
        act(dt_.ap, sb_("logdt"), AF.Exp, [smB], [dt_])
        tt(E, lrdt.ap, lamre, dt_.ap, ALU.mult, [smB, dt_], [lrdt])
        tt(E, th.ap, lamim, dt_.ap, ALU.mult, [smB, dt_], [th])
        tt(E, t32a.ap, lamre, lamre, ALU.mult, [smB], [t32a])
        tt(E, t32b.ap, lamim, lamim, ALU.mult, [smB], [t32b])
        tt(E, den.ap, t32a.ap, t32b.ap, ALU.add, [t32a, t32b], [den])
        P.add("dve", lambda e: e.reciprocal(den.ap, den.ap), R(den), R(den))
        kv = sp_("kvals")
        kvb = kv[:, None, :].to_broadcast([128, 32, 16])
        tt(E, argm.ap, lrdt.ap[:, :, None].to_broadcast([128, 32, 16]), kvb, ALU.mult, [lrdt, smP], [argm])
        act(magp.ap, argm.ap, AF.Exp, [argm], [magp])
        act(magn.ap, argm.ap, AF.Exp, [argm], [magn], scale=-1.0)
        tt(E, turns.ap, th.ap[:, :, None].to_broadcast([128, 32, 16]), kvb, ALU.mult, [th, smP], [turns])
        ts(E, turns.ap, turns.ap, 1.0 / (2.0 * math.pi), None, ALU.mult, None, [turns], [turns])
        ts(E, tf.ap, turns.ap, MAGIC, -MAGIC, ALU.add, ALU.add, [turns], [tf])
        tt(E, fr.ap, turns.ap, tf.ap, ALU.subtract, [turns, tf], [fr])
        stt(msf.ap, fr.ap, 0.5, fr.ap, ALU.is_gt, ALU.subtract, [fr], [msf])
        act(s16.ap, msf.ap, AF.Sin, [msf], [s16], scale=-TWO_PI)
        ts(E, frc.ap, fr.ap, 0.25, None, ALU.add, None, [fr], [frc])
        stt(msc.ap, frc.ap, 0.5, frc.ap, ALU.is_gt, ALU.subtract, [frc], [msc])
        act(c16.ap, msc.ap, AF.Sin, [msc], [c16], scale=-TWO_PI)
        tt(E, Pc.ap, magp.ap, c16.ap, ALU.mult, [magp, c16], [Pc])
        tt(E, Ps.ap, magp.ap, s16.ap, ALU.mult, [magp, s16], [Ps])
        tt(E, Nc.ap, magn.ap, c16.ap, ALU.mult, [magn, c16], [Nc])
        tt(E, Ns.ap, magn.ap, s16.ap, ALU.mult, [magn, s16], [Ns])
        abre, abim = Pc.ap[:, :, 0], Ps.ap[:, :, 0]
        nr = t32a
        ts(E, nr.ap, abre, -1.0, None, ALU.add, None, [Pc], [nr])
        tt(E, t32b.ap, nr.ap, lamre, ALU.mult, [nr, smB], [t32b])
        tt(E, kre.ap, abim, lamim, ALU.mult, [Ps, smB], [kre])
        tt(E, kre.ap, kre.ap, t32b.ap, ALU.add, [kre, t32b], [kre])
        tt(E, kre.ap, kre.ap, den.ap, ALU.mult, [kre, den], [kre])
        tt(E, t32b.ap, nr.ap, lamim, ALU.mult, [nr, smB], [t32b])
        tt(E, kim.ap, abim, lamre, ALU.mult, [Ps, smB], [kim])
        tt(E, kim.ap, kim.ap, t32b.ap, ALU.subtract, [kim, t32b], [kim])
        tt(E, kim.ap, kim.ap, den.ap, ALU.mult, [kim, den], [kim])
        sgn = sp_("sgn")
        b1 = sb_("b1").rearrange("p (g h) -> p g h", g=32)
        b2raw = sb_("b2raw").rearrange("p (g h) -> p g h", g=32)
        c1raw = sb_("c1raw").rearrange("p (g h) -> p g h", g=32)
        c2 = sb_("c2").rearrange("p (g h) -> p g h", g=32)
        ts(E, b2.ap, b2raw, sgn, None, ALU.mult, None, [smB, smP], [b2])
        ts(E, c1.ap, c1raw, sgn, None, ALU.mult, None, [smB, smP], [c1])
        kreb = kre.ap[:, :, None].to_broadcast([128, 32, 16])
        kimb = kim.ap[:, :, None].to_broadcast([128, 32, 16])
        tt(E, Xb.ap, kreb, b1, ALU.mult, [kre, smB], [Xb])
        tt(E, tx.ap, kimb, b2.ap, ALU.mult, [kim, b2], [tx])
        tt(E, Xb.ap, Xb.ap, tx.ap, ALU.subtract, [Xb, tx], [Xb])
        tt(E, Yb.ap, kreb, b2.ap, ALU.mult, [kre, b2], [Yb])
        tt(E, tx.ap, kimb, b1, ALU.mult, [kim, smB], [tx])
        tt(E, Yb.ap, Yb.ap, tx.ap, ALU.add, [Yb, tx], [Yb])

        GB = 8
        bigA = ar.alloc([GB, 8, 16], F32)
        bigB = ar.alloc([GB, 8, 16], F32)
        AB1f = ar.alloc([GB, 8, 16], F32)
        L1f = ar.alloc([GB, 8, 16], F32)
        WSf = ar.alloc([GB, 8, 16], F32)
        mtmp = [ar.alloc([128], F32) for _ in range(2)]

        def outer(dst, tab, k0, vec, g0, eng, dst_res, op_reads):
            a = tab[:, g0:g0 + GB, k0:k0 + 8][:, :, :, None].to_broadcast([128, GB, 8, 16])
            b = vec[:, g0:g0 + GB, None, :].to_broadcast([128, GB, 8, 16])
            tt(eng, dst, a, b, ALU.mult, op_reads, dst_res)

        pbi = 0
        g3 = "p g s h -> p g (s h)"
        for gb in range(32 // GB):
            g0 = gb * GB
            outer(bigA.ap, Nc.ap, 0, Xb.ap, g0, "dve", [bigA], [Nc, Xb])
            outer(bigB.ap, Ns.ap, 0, Yb.ap, g0, "pool", [bigB], [Ns, Yb])
            tt("dve", AB1f.ap, bigA.ap, bigB.ap, ALU.add, [bigA, bigB], [AB1f])
            outer(bigA.ap, Pc.ap, 0, c1.ap, g0, "dve", [bigA], [Pc, c1])
            outer(bigB.ap, Ps.ap, 0, c2, g0, "pool", [bigB], [Ps, smB])
            tt("dve", L1f.ap, bigA.ap, bigB.ap, ALU.subtract, [bigA, bigB], [L1f])
            cp("pool", L1b.ap[:, g0:g0 + GB, :], L1f.ap.rearrange(g3), [L1f], [L1b])
            for g in range(GB):
                bk = banks[pbi % 2]
                mt = mtmp[pbi % 2]
                pbi += 1
                mm(bk.ap[:, 0:128], AB1f.ap[:, g].rearrange("p s h -> p (s h)"),
                   L1f.ap[:, g].rearrange("p s h -> p (s h)"), True, True, [AB1f, L1f], [bk])
                tt("dve", mt.ap, bk.ap[:, 0:128], sp_("maskM"), ALU.mult, [bk, smP], [mt])
                o_, _n = SB_COLS["drep"]
                stt(MT.ap[:, g0 + g, :], ident_f, smB.ap[:, o_ + g0 + g:o_ + g0 + g + 1], mt.ap,
                    ALU.mult, ALU.add, [smP, smB, mt], [MT])
            outer(bigA.ap, Ps.ap, 0, c1.ap, g0, "dve", [bigA], [Ps, c1])
            outer(bigB.ap, Pc.ap, 0, c2, g0, "pool", [bigB], [Pc, smB])
            stt(L2b.ap[:, g0:g0 + GB, :], bigA.ap.rearrange(g3), -1.0, bigB.ap.rearrange(g3),
                ALU.mult, ALU.subtract, [bigA, bigB], [L2b])
            for which in range(2):
                if which == 0:
                    outer(bigA.ap, Pc.ap, 8, Xb.ap, g0, "dve", [bigA], [Pc, Xb])
                    outer(bigB.ap, Ps.ap, 8, Yb.ap, g0, "pool", [bigB], [Ps, Yb])
                    tt("dve", WSf.ap, bigA.ap, bigB.ap, ALU.subtract, [bigA, bigB], [WSf])
                else:
                    outer(bigA.ap, Ps.ap, 8, Xb.ap, g0, "dve", [bigA], [Ps, Xb])
                    outer(bigB.ap, Pc.ap, 8, Yb.ap, g0, "pool", [bigB], [Pc, Yb])
                    tt("dve", WSf.ap, bigA.ap, bigB.ap, ALU.add, [bigA, bigB], [WSf])
                dstT = WS1T if which == 0 else WS2T
                for g4 in range(GB // 4):
                    bk = banks[pbi % 2]
                    pbi += 1
                    for j in range(4):
                        g = g4 * 4 + j
                        tr(bk.ap[:, j * 128:(j + 1) * 128], WSf.ap[:, g].rearrange("p s h -> p (s h)"),
                           ident_f, [WSf, smP], [bk])
                    cp("act", dstT.ap[:, g0 + g4 * 4:g0 + g4 * 4 + 4, :],
                       bk.ap.rearrange("p (j q) -> p j q", j=4), [bk], [dstT])
        if "ssmw" in debug:
            dump("MT", MT.ap, [128, 32, 128], BF16, [MT])
            dump("L1b", L1b.ap, [128, 32, 128], BF16, [L1b])
            dump("L2b", L2b.ap, [128, 32, 128], BF16, [L2b])
            dump("WS1T", WS1T.ap, [128, 32, 128], BF16, [WS1T])
            dump("WS2T", WS2T.ap, [128, 32, 128], BF16, [WS2T])
            dump("rp", rp.ap, [128, 32], F32, [rp])
            dump("f8p", f8p.ap, [128, 32], F32, [f8p])

        ar.seek(161 * KB, T1)
        tabb = [ar.alloc([2, 512], F32) for _ in range(4)]
        ar.seek(40 * KB, 72 * KB)
        t1s = [ar.alloc([512], F32) for _ in range(2)]
        t2s = [ar.alloc([512], F32) for _ in range(2)]
        cms = [ar.alloc([512], F32) for _ in range(2)]
        wss = [ar.alloc([512], F32) for _ in range(2)]
        V1s = [ar.alloc([520], BF16) for _ in range(2)]
        V2s = [ar.alloc([520], BF16) for _ in range(2)]
        for v in V1s + V2s:
            memset("pool", v.ap[:, 0:8], 0.0, [v])
        zb = [banks[0], banks[1], banks[2], banks[3]]
        yb = [banks[4], banks[5], banks[6], banks[7]]

        def load_tab(g):
            dma("sp", tabb[g % 4].ap, tabs_d[g], [tabs_res.res[g // 2]], [tabb[g % 4]])

        def z_part(g):
            z1, z2 = zb[(g * 2) % 4], zb[(g * 2 + 1) % 4]
            mm(z1.ap, WS1T.ap[:, g, :], Ug.ap[:, g, :], True, True, [WS1T, Ug.res[g]], [z1])
            mm(z2.ap, WS2T.ap[:, g, :], Ug.ap[:, g, :], True, True, [WS2T, Ug.res[g]], [z2])

        for g in range(3):
            load_tab(g)
        z_part(0)
        for g in range(32):
            if g + 3 < 32:
                load_tab(g + 3)
            if g + 1 < 32:
                z_part(g + 1)
            cosT, sinT = tabb[g % 4].ap[:, 0, :], tabb[g % 4].ap[:, 1, :]
            cosR = sinR = tabb[g % 4]
            z1, z2 = zb[(g * 2) % 4], zb[(g * 2 + 1) % 4]
            t1, t2, cm, ws = t1s[g % 2], t2s[g % 2], cms[g % 2], wss[g % 2]
            v1, v2 = V1s[g % 2], V2s[g % 2]
            tt("dve", t1.ap, z1.ap, cosT, ALU.mult, [z1, cosR], [t1])
            tt("dve", t2.ap, z2.ap, sinT, ALU.mult, [z2, sinR], [t2])
            tt("dve", cm.ap, t1.ap, t2.ap, ALU.add, [t1, t2], [cm])
            scan(ws, rp.ap[:, g:g + 1].to_broadcast([128, 512]), cm, [rp])
            tt("pool", v1.ap[:, 8:520], ws.ap, cosT, ALU.mult, [ws, cosR], [v1])
            tt("pool", v2.ap[:, 8:520], ws.ap, sinT, ALU.mult, [ws, sinR], [v2])
            for cb in range(4):
                o = yb[cb].ap[:, (g % 4) * 128:(g % 4 + 1) * 128]
                mm(o, Ug.ap[:, g, cb * 128:(cb + 1) * 128], MT.ap[:, g, :], True, False,
                   [Ug.res[g], MT], [yb[cb]])
                mm(o, v1.ap[:, 7 + cb * 128:7 + (cb + 1) * 128], L1b.ap[:, g, :], False, False,
                   [v1, L1b], [yb[cb]])
                mm(o, v2.ap[:, 7 + cb * 128:7 + (cb + 1) * 128], L2b.ap[:, g, :], False, True,
                   [v2, L2b], [yb[cb]])
            if g % 4 == 3:
                for cb in range(4):
                    yp = yb[cb]
                    g4 = g - 3
                    dst = Yc.ap[:, cb, :, g4 * 16:(g4 + 4) * 16].rearrange("p t (g h) -> p g t h", g=4)
                    act(dst, yp.ap.rearrange("p (g t h) -> p g t h", g=4, t=8), AF.Gelu_apprx_tanh,
                        [yp], [Yc.res[cb]])
        if "Yc" in debug:
            dump("Yc", Yc.ap, [128, 4, 8, 512], BF16, [Yc])
        ar.seek(72 * KB, 104 * KB)
        YT = ar.alloc([4, L], BF16)
        for cb in range(4):
            for j in range(4):
                b = pi % 2
                pi += 1
                bkb = bank_bf(b)
                for t in range(8):
                    tr(bkb[:, t * 128:(t + 1) * 128], Yc.ap[:, cb, t, j * 128:(j + 1) * 128], identb.ap,
                       [Yc.res[cb], identb], [banks[b]])
                dst = YT.ap[:, j, cb * 1024:(cb + 1) * 1024].rearrange("p (c t) -> p t c", t=8)
                cp("act" if j % 2 else "dve", dst, bkb.rearrange("p (t c) -> p t c", t=8),
                   [banks[b]], [YT])
        ar.seek(8 * KB, 40 * KB)
        Y2T = ar.alloc([4, L], BF16)
        ar.seek(161 * KB, T1)
        wg = ar.alloc([4, 1024], BF16)
        dma("pool", wg.ap, w_glu_d.rearrange("(kc p) j -> p kc j", p=128), [], [wg])
        sgs = [ar.alloc([512], F32) for _ in range(4)]
        gi_ = 0
        for m in range(4):
            for nt in range(8):
                ba, bb_ = banks[(gi_ * 2) % 8], banks[(gi_ * 2 + 1) % 8]
                sgb = sgs[gi_ % 4]
                gi_ += 1
                for kc in range(4):
                    mm(ba.ap, wg.ap[:, kc, m * 128:(m + 1) * 128], YT.ap[:, kc, nt * 512:(nt + 1) * 512],
                       kc == 0, kc == 3, [wg, YT], [ba])
                for kc in range(4):
                    mm(bb_.ap, wg.ap[:, kc, 512 + m * 128:512 + (m + 1) * 128],
                       YT.ap[:, kc, nt * 512:(nt + 1) * 512], kc == 0, kc == 3, [wg, YT], [bb_])
                act(sgb.ap, bb_.ap, AF.Sigmoid, [bb_], [sgb])
                tt("dve", Y2T.ap[:, m, nt * 512:(nt + 1) * 512], ba.ap, sgb.ap, ALU.mult, [ba, sgb], [Y2T])
        if "Y2T" in debug:
            dump("Y2T", Y2T.ap, [128, 4, L], BF16, [Y2T])

    if "P3" in stages:
        ar.seek(40 * KB, 104 * KB)
        xs3 = [ar.alloc([D], F32) for _ in range(8)]
        ring = [ar.alloc([4096], BF16) for _ in range(4)]
        ar.seek(T0, T1)
        lnr = [ar.alloc([D], F32) for _ in range(2)]
        xTb = ar.alloc([8, 1024], BF16, nres=8)
        aT = ar.alloc([16, 1024], BF16)
        mixin = Buf(aT.ap[:, 0:8, :], 1)
        mixin.res = aT.res
        wsp = ar.alloc([4, 1024], BF16)
        wap = ar.alloc([2, 1024], BF16)
        dma("pool", wsp.ap, wsp_d.rearrange("(kc p) j -> p kc j", p=128), [], [wsp])
        dma("pool", wap.ap, wap_d.rearrange("(kc p) j -> p kc j", p=128), [], [wap])
        gsb = [ar.alloc([512], F32) for _ in range(2)]
        gab = [ar.alloc([512], F32) for _ in range(2)]
        tm1 = [ar.alloc([512], F32) for _ in range(2)]
        tm2 = [ar.alloc([512], F32) for _ in range(2)]
        sqb = tm1
        st6 = ar.alloc([12], F32)
        mv = ar.alloc([2], F32)
        rstd = ar.alloc([1], F32)
        bgate = sp_("bgate")
        ri = 0
        bi = 0

        def nb():
            nonlocal bi
            b = banks[bi % 8]
            bi += 1
            return b

        def load_piece(src_aps):
            nonlocal ri
            slot = ring[ri % 4]
            ri += 1
            for (v, src) in src_aps:
                dma("pool", v(slot), src, [], [slot])
            return slot

        def layer_norm(xb, which):
            P.add("dve", lambda e: e.bn_stats(st6.ap[:, 0:6], xb.ap[:, 0:512]), R(xb), R(st6))
            P.add("dve", lambda e: e.bn_stats(st6.ap[:, 6:12], xb.ap[:, 512:1024]), R(xb, st6), R(st6))
            P.add("dve", lambda e: e.bn_aggr(mv.ap, st6.ap), R(st6), R(mv))
            act(rstd.ap, mv.ap[:, 1:2], AF.Sqrt, [mv], [rstd], bias=LN_EPS)
            P.add("dve", lambda e: e.reciprocal(rstd.ap, rstd.ap), R(rstd), R(rstd))
            stt(xb.ap, xb.ap, mv.ap[:, 0:1], lnr[0].ap, ALU.subtract, ALU.mult, [xb, mv, lnr[0]], [xb])
            stt(xb.ap, xb.ap, rstd.ap[:, 0:1], lnr[1].ap, ALU.mult, ALU.add, [xb, rstd, lnr[1]], [xb])

        def to_featmajor(xb, t):
            for half in range(2):
                bk = nb()
                for j in range(4):
                    kc = half * 4 + j
                    tr(bk.ap[:, j * 128:(j + 1) * 128], xb.ap[:, kc * 128:(kc + 1) * 128], ident_f,
                       [xb, smP], [bk])
                cp("act", xTb.ap[:, half * 4:half * 4 + 4, t * 128:(t + 1) * 128],
                   bk.ap.rearrange("p (j t) -> p j t", j=4), [bk], [xTb.res[t]])

        v3 = lambda kcn, c0, nc_: (lambda slot: slot.ap.rearrange("p (k c) -> p k c", k=kcn)[:, :, c0:c0 + nc_])
        for blk in range(4):
            t0 = blk * 1024
            dma("sp", xTb.ap, xTs_d[:, :, t0:t0 + 1024], [xTs_res.res[blk]], [xTb])
            for t in range(8):
                dma("sp", xs3[t].ap, x_d[t0 + t * 128:t0 + (t + 1) * 128, :], [], [xs3[t]])
            for mq in range(4):
                gcol = 2816 + mq * 256
                piece = load_piece([
                    (v3(8, 0, 256), w_in_d[:, gcol:gcol + 256].rearrange("(kc p) j -> p kc j", p=128)),
                    (v3(8, 256, 256), w_in_d[:, gcol + 1024:gcol + 1280].rearrange("(kc p) j -> p kc j", p=128)),
                ])
                pv = piece.ap.rearrange("p (k c) -> p k c", k=8)
                for m2 in range(2):
                    m = mq * 2 + m2
                    for nh in range(2):
                        cs = slice(nh * 512, (nh + 1) * 512)
                        gs_, ga_, t1_, t2_ = gsb[nh], gab[nh], tm1[nh], tm2[nh]
                        b_gs, b_ga, b_ps, b_pa = nb(), nb(), nb(), nb()
                        for kc in range(8):
                            mm(b_gs.ap, pv[:, kc, m2 * 128:(m2 + 1) * 128], xTb.ap[:, kc, cs], kc == 0, kc == 7,
                               [piece, xTb], [b_gs])
                        for kc in range(8):
                            mm(b_ga.ap, pv[:, kc, 256 + m2 * 128:256 + (m2 + 1) * 128], xTb.ap[:, kc, cs],
                               kc == 0, kc == 7, [piece, xTb], [b_ga])
                        for kc in range(4):
                            mm(b_ps.ap, wsp.ap[:, kc, m * 128:(m + 1) * 128],
                               Y2T.ap[:, kc, t0 + nh * 512:t0 + (nh + 1) * 512], kc == 0, kc == 3,
                               [wsp, Y2T], [b_ps])
                        for kc in range(2):
                            mm(b_pa.ap, wap.ap[:, kc, m * 128:(m + 1) * 128],
                               yattnT.ap[:, kc, t0 + nh * 512:t0 + (nh + 1) * 512], kc == 0, kc == 1,
                               [wap, yattnT], [b_pa])
                        act(gs_.ap, b_gs.ap, AF.Sigmoid, [b_gs, smP], [gs_], bias=bgate[:, m:m + 1])
                        act(ga_.ap, b_ga.ap, AF.Sigmoid, [b_ga, smP], [ga_], bias=bgate[:, 8 + m:9 + m])
                        tt("dve", t1_.ap, b_ps.ap, gs_.ap, ALU.mult, [b_ps, gs_], [t1_])
                        tt("dve", t2_.ap, b_pa.ap, ga_.ap, ALU.mult, [b_pa, ga_], [t2_])
                        tt("dve", mixin.ap[:, m, cs], t1_.ap, t2_.ap, ALU.add, [t1_, t2_], [mixin])
            wo = [load_piece([(v3(8, 0, 512), wout_d[:, oh * 512:(oh + 1) * 512]
                               .rearrange("(kc p) j -> p kc j", p=128))]) for oh in range(2)]
            for i, r_ in enumerate(lnr):
                dma("sp", r_.ap, lnrows_d[i], [], [r_])
            for t in range(8):
                for oh in range(2):
                    bk = nb()
                    for kc in range(8):
                        mm(bk.ap, mixin.ap[:, kc, t * 128:(t + 1) * 128],
                           wo[oh].ap.rearrange("p (k c) -> p k c", k=8)[:, kc, :],
                           kc == 0, kc == 7, [mixin, wo[oh]], [bk])
                    xo = xs3[t].ap[:, oh * 512:(oh + 1) * 512]
                    stt(xo, xo, ALPHA, bk.ap, ALU.mult, ALU.add, [xs3[t], bk], [xs3[t]])
                layer_norm(xs3[t], 0)
                if t >= 1:
                    to_featmajor(xs3[t - 1], t - 1)
            to_featmajor(xs3[7], 7)
            for h2 in range(2):
                for up in range(4):
                    c0 = h2 * 2048 + up * 512
                    piece = load_piece([(v3(8, 0, 512),
                                         wup_d[:, c0:c0 + 512].rearrange("(kc p) j -> p kc j", p=128))])
                    pv = piece.ap.rearrange("p (k c) -> p k c", k=8)
                    for m4 in range(4):
                        mloc = up * 4 + m4
                        for nh in range(2):
                            cs = slice(nh * 512, (nh + 1) * 512)
                            bk = nb()
                            for kc in range(8):
                                mm(bk.ap, pv[:, kc, m4 * 128:(m4 + 1) * 128], xTb.ap[:, kc, cs], kc == 0, kc == 7,
                                   [piece, xTb], [bk])
                            sq_ = sqb[nh]
                            act(sq_.ap, bk.ap, AF.Square, [bk], [sq_])
                            stt(aT.ap[:, mloc, cs], bk.ap, 0.0, sq_.ap, ALU.is_gt, ALU.mult, [bk, sq_], [aT])
                def dn_piece(oh, kh):
                    r0 = h2 * 2048 + kh * 1024
                    return load_piece([(v3(8, 0, 512), wdn_d[r0:r0 + 1024, oh * 512:(oh + 1) * 512]
                                        .rearrange("(kc p) j -> p kc j", p=128))])

                def dn_tile(t, oh, pcs):
                    bk = nb()
                    for kc in range(16):
                        pc_ = pcs[kc // 8]
                        mm(bk.ap, aT.ap[:, kc, t * 128:(t + 1) * 128],
                           pc_.ap.rearrange("p (k c) -> p k c", k=8)[:, kc % 8, :], kc == 0, kc == 15,
                           [aT, pc_], [bk])
                    xo = xs3[t].ap[:, oh * 512:(oh + 1) * 512]
                    if h2 == 0:
                        stt(xo, xo, ALPHA, bk.ap, ALU.mult, ALU.add, [xs3[t], bk], [xs3[t]])
                    else:
                        tt("dve", xo, xo, bk.ap, ALU.add, [xs3[t], bk], [xs3[t]])

                if h2 == 0:
                    for oh in range(2):
                        pcs = [dn_piece(oh, kh) for kh in range(2)]
                        for t in range(8):
                            dn_tile(t, oh, pcs)
                else:
                    pcs2 = [[dn_piece(oh, kh) for kh in range(2)] for oh in range(2)]
                    for i, r_ in enumerate(lnr):
                        dma("sp", r_.ap, lnrows_d[2 + i], [], [r_])
                    for t in range(8):
                        for oh in range(2):
                            dn_tile(t, oh, pcs2[oh])
                        layer_norm(xs3[t], 1)
                        dma("sp", out_d[t0 + t * 128:t0 + (t + 1) * 128, :], xs3[t].ap, [xs3[t]], [])

    P.finalize(nc, es)
    with nc.Block() as block:
        @block.tensor
        def _(e):
            P.emit_engine("pe", e)

        @block.scalar
        def _(e):
            P.emit_engine("act", e)

        @block.vector
        def _(e):
            P.emit_engine("dve", e)

        @block.gpsimd
        def _(e):
            P.emit_engine("pool", e)

        @block.sync
        def _(e):
            P.emit_engine("sp", e)
    es.close()
    nops = {k: len(v) for k, v in P.ops.items()}
    return nc, dbg, nops


def make_in_maps(inputs):
    g = lambda k: np.asarray(inputs[k], np.float32)
    smP = build_smalls_p(g("b_gate")[0], g("lambda_re")[0], g("lambda_im")[0], g("log_dt")[0])
    smB = build_smalls_b(g("lambda_re")[0], g("lambda_im")[0], g("log_dt")[0], g("ssm_b_re")[0],
                         g("ssm_b_im")[0], g("ssm_c_re")[0], g("ssm_c_im")[0], g("ssm_d")[0])
    biasT = build_bias_tables(g("rel_bias"))
    lnrows = np.stack([np.broadcast_to(g(k)[0][None, :], (128, D)) for k in ("ln1_g", "ln1_b", "ln2_g", "ln2_b")])
    lnrows = np.ascontiguousarray(lnrows, np.float32)
    shared = {
        "w_in": np.ascontiguousarray(g("w_in")[0]), "smalls_p": smP, "smalls_b": smB,
        "w_glu": np.ascontiguousarray(g("w_glu")[0]), "bias_t": biasT,
        "w_ssm_proj": np.ascontiguousarray(g("w_ssm_proj")[0]),
        "w_attn_proj": np.ascontiguousarray(g("w_attn_proj")[0]),
        "w_out": np.ascontiguousarray(g("w_out")[0]), "w_up": np.ascontiguousarray(g("w_up")[0]),
        "w_down": np.ascontiguousarray(g("w_down")[0]), "ln_rows": lnrows,
    }
    x = g("x")
    return [dict(shared, x=np.ascontiguousarray(x[b])) for b in range(x.shape[0])]


_CACHE = {}


def kernel(**inputs):
    in_maps = make_in_maps(inputs)
    if "nc" not in _CACHE:
        _CACHE["nc"] = build_program()[0]
    nc = _CACHE["nc"]
    res = run_bass_kernel_spmd(nc, in_maps, core_ids=list(range(NCORES)))
    return np.stack([np.asarray(r["out"], np.float32) for r in res.results], 0)
```

```python
import math
from contextlib import ExitStack

import numpy as np
import concourse.bass as bass
import concourse.mybir as mybir
from concourse.bass_utils import run_bass_kernel_spmd

F32 = mybir.dt.float32
BF16 = mybir.dt.bfloat16
I32 = mybir.dt.int32
AF = mybir.ActivationFunctionType
ALU = mybir.AluOpType

D = 1024
L = 4096
NCORES = 8
DFF = 4096
INW = 4864
ALPHA = 2.0 ** 0.25
LN_EPS = 1e-5
NEG = -30000.0
TWO_PI = 6.283185
HALF_PI = 1.5707963
MAGIC = 12582912.0
DIL = (1, 4, 16)


class Res:
    __slots__ = ("w", "r", "rd", "excl")

    def __init__(self, excl=False):
        self.excl = excl
        self.w = None
        self.r = {}
        self.rd = []


class Op:
    __slots__ = ("eng", "emit", "deps", "dma", "sem", "val", "prev", "used", "seq")
    _seq = 0

    def __init__(self, eng, emit, dma):
        Op._seq += 1
        self.seq = Op._seq
        self.eng = eng
        self.emit = emit
        self.dma = dma
        self.deps = []
        self.sem = None
        self.val = 0
        self.prev = 0
        self.used = False


class Prog:
    ENGS = ("pe", "act", "dve", "pool", "sp")
    EPOCH = 20000

    def __init__(self):
        self.ops = {e: [] for e in self.ENGS}

    def add(self, eng, emit, reads=(), writes=(), dma=False):
        op = Op(eng, emit, dma)
        deps = {}
        ex = [r for r in reads if r.excl]
        if ex:
            writes = list(writes) + [r for r in ex if r not in writes]
            reads = [r for r in reads if not r.excl]

        def dep(o):
            if o is not None and o is not op:
                deps[id(o)] = o

        for r in reads:
            dep(r.w)
        for w in writes:
            if w.w is not None and (dma or w.w.dma or w.w.eng != eng or eng != "pe"):
                dep(w.w)
            for e, o in w.r.items():
                if dma or e != eng or eng != "pe":
                    dep(o)
            for o in w.rd:
                dep(o)
        for r in reads:
            if dma:
                r.rd.append(op)
            else:
                r.r[eng] = op
        for w in writes:
            w.w = op
            w.r = {}
            w.rd = []
        op.deps = list(deps.values())
        for o in op.deps:
            o.used = True
        self.ops[eng].append(op)
        return op

    def finalize(self, nc, es, ndma=12):
        self.sems = {}
        for e in self.ENGS:
            cnt = 0
            dcnt = 0
            dsems = []
            esems = []
            for op in self.ops[e]:
                if op.dma:
                    if len(dsems) < ndma:
                        dsems.append([es.enter_context(nc.semaphore(f"d_{e}_{len(dsems)}")), 0])
                    slot = dsems[dcnt % ndma]
                    dcnt += 1
                    op.prev = slot[1]
                    slot[1] += 16
                    op.sem = slot[0]
                    op.val = slot[1]
                elif op.used:
                    ep = cnt // self.EPOCH
                    if ep >= len(esems):
                        esems.append(es.enter_context(nc.semaphore(f"c_{e}_{ep}")))
                    op.sem = esems[ep]
                    op.val = cnt % self.EPOCH + 1
                    cnt += 1

    def emit_engine(self, e, eng):
        waited = {}

        def wait(sem, val):
            k = id(sem)
            if waited.get(k, 0) < val:
                eng.wait_ge(sem, val)
                waited[k] = val

        for op in self.ops[e]:
            for d in op.deps:
                wait(d.sem, d.val)
            if op.dma and op.prev > 0:
                wait(op.sem, op.prev)
            ins = op.emit(eng)
            if op.sem is not None:
                ins.then_inc(op.sem, 16 if op.dma else 1)
        last = {}
        for op in self.ops[e]:
            if op.dma:
                last[id(op.sem)] = (op.sem, op.val)
        for sem, val in last.values():
            wait(sem, val)


class Buf:
    def __init__(self, ap, nres=1, excl=False):
        self.ap = ap
        self.res = [Res(excl) for _ in range(nres)]

    @property
    def r0(self):
        return self.res[0]


class Arena:
    def __init__(self, nc, es, nbytes):
        self.t = es.enter_context(nc.sbuf_tensor("arena", [128, nbytes // 4], F32))
        self.cap = nbytes
        self.off = 0
        self.limit = nbytes
        self.hist = []

    def seek(self, off, limit=None):
        self.off = off
        self.limit = self.cap if limit is None else limit

    def alloc(self, shape, dtype, nres=1):
        n = int(np.prod(shape))
        nb = n * (2 if dtype == BF16 else 4)
        off = (self.off + 63) // 64 * 64
        assert off + nb <= self.limit, f"SBUF arena overflow {off + nb} > {self.limit}"
        self.off = off + nb
        v = self.t[:, off // 4:(off + nb + 3) // 4]
        if dtype != F32:
            v = v.bitcast(dtype)[:, 0:n]
        if len(shape) > 1:
            names = " ".join(f"a{i}" for i in range(len(shape)))
            kw = {f"a{i}": int(s) for i, s in enumerate(shape)}
            v = v.rearrange(f"p ({names}) -> p {names}", **kw)
        b = Buf(v, nres)
        seed_r, seed_rd = {}, []
        for (s0, s1, ob) in self.hist:
            if s0 < off + nb and off < s1:
                for rs in ob.res:
                    cands = list(rs.r.values()) + ([rs.w] if rs.w is not None else [])
                    for o in cands:
                        if o.dma:
                            seed_rd.append(o)
                        elif o.eng not in seed_r or seed_r[o.eng].seq < o.seq:
                            seed_r[o.eng] = o
                    seed_rd.extend(rs.rd)
        for rs in b.res:
            rs.r = dict(seed_r)
            rs.rd = list(seed_rd)
        self.hist.append((off, off + nb, b))
        return b


def _t5_bucket_np(dist):
    max_exact = 16
    d = np.maximum(dist, 1).astype(np.float32)
    large = max_exact + (np.log(d / np.float32(max_exact)) / np.float32(math.log(2048 / max_exact))
                         * np.float32(32 - max_exact)).astype(np.int32)
    large = np.minimum(large, 31)
    return np.where(dist < max_exact, dist, large)


def _layout(items):
    cols, off = {}, 0
    for name, n in items:
        cols[name] = (off, n)
        off += n
    return cols, off


SP_COLS, SP_N = _layout((("ident", 128), ("iota", 512), ("kvals", 16), ("sgn", 1), ("maskM", 128),
                         ("bgate", 16), ("lamre", 32), ("lamim", 32), ("logdt", 32)))
SB_COLS, SB_N = _layout((("lamre", 32), ("lamim", 32), ("logdt", 32), ("drep", 32),
                         ("b1", 512), ("b2raw", 512), ("c1raw", 512), ("c2", 512)))


def _put(a, cols, name, v):
    o, n = cols[name]
    a[:, o:o + n] = np.asarray(v, np.float32).reshape(128, n)


def build_smalls_p(b_gate, lambda_re, lambda_im, log_dt):
    a = np.zeros((128, SP_N), np.float32)
    _put(a, SP_COLS, "ident", np.eye(128, dtype=np.float32))
    _put(a, SP_COLS, "iota", np.broadcast_to(np.arange(512, dtype=np.float32)[None, :], (128, 512)))
    _put(a, SP_COLS, "kvals", np.broadcast_to(
        np.array([1, 2, 3, 4, 5, 6, 7, 8, 7, 6, 5, 4, 3, 2, 1, 0], np.float32)[None, :], (128, 16)))
    _put(a, SP_COLS, "sgn", np.concatenate([np.ones(64), -np.ones(64)])[:, None])
    s_idx = np.arange(128) // 16
    _put(a, SP_COLS, "maskM", (s_idx[None, :] >= s_idx[:, None]).astype(np.float32))
    _put(a, SP_COLS, "bgate", b_gate.reshape(16, 128).T)
    lr, li = lambda_re.T, lambda_im.T
    _put(a, SP_COLS, "lamre", np.concatenate([lr, lr], 0))
    _put(a, SP_COLS, "lamim", np.concatenate([li, li], 0))
    _put(a, SP_COLS, "logdt", np.broadcast_to(log_dt[None, :], (128, 32)))
    return a


def build_smalls_b(lambda_re, lambda_im, log_dt, b_re, b_im, c_re, c_im, d_skip):
    a = np.zeros((128, SB_N), np.float32)
    lr, li = lambda_re.T, lambda_im.T
    _put(a, SB_COLS, "lamre", np.concatenate([lr, lr], 0))
    _put(a, SB_COLS, "lamim", np.concatenate([li, li], 0))
    _put(a, SB_COLS, "logdt", np.broadcast_to(log_dt[None, :], (128, 32)))
    _put(a, SB_COLS, "drep", np.tile(d_skip.T, (8, 1)))
    bre, bim = b_re.transpose(1, 0, 2), b_im.transpose(1, 0, 2)
    _put(a, SB_COLS, "b1", np.concatenate([bre, bim], 0))
    _put(a, SB_COLS, "b2raw", np.concatenate([bim, bre], 0))
    cre, cim = c_re.transpose(2, 0, 1), c_im.transpose(2, 0, 1)
    _put(a, SB_COLS, "c1raw", np.concatenate([cre, cim], 0))
    _put(a, SB_COLS, "c2", np.concatenate([cim, cre], 0))
    return a


def build_bias_tables(rel_bias):
    out = np.full((6, 128, 2, 512), NEG, np.float32)
    j = np.arange(128)[:, None]
    a = np.arange(128)[None, :]
    for hp in range(2):
        for g in range(3):
            rnd = hp * 3 + g
            for h2 in range(2):
                head = g * 4 + hp * 2 + h2
                rel_prev = a + 128 - j
                rel_cur = a - j
                for kb, rel in ((0, rel_prev), (1, rel_cur)):
                    valid = (rel >= 0) & (rel <= 128)
                    idx = _t5_bucket_np(np.maximum(rel, 0) * DIL[g])
                    vals = np.where(valid, rel_bias[idx, head], np.float32(NEG))
                    c0 = h2 * 256 + kb * 128
                    out[rnd, :, 0, c0:c0 + 128] = vals
                    if kb == 1:
                        out[rnd, :, 1, c0:c0 + 128] = vals
    return out


KB = 1024


def build_program(debug=(), stages=("A", "U", "ATT", "SSM", "P3")):
    nc = bass.Bass("TRN2", target_bir_lowering=False)
    es = ExitStack()
    P = Prog()

    def dram_in(name, shape, dt=F32):
        return nc.dram_tensor(name, list(shape), dt, kind="ExternalInput").ap()

    x_d = dram_in("x", [L, D])
    w_in_d = dram_in("w_in", [D, INW])
    smP_d = dram_in("smalls_p", [128, SP_N])
    smB_d = dram_in("smalls_b", [128, SB_N])
    w_glu_d = dram_in("w_glu", [512, 1024])
    biasT_d = dram_in("bias_t", [6, 128, 2, 512])
    wsp_d = dram_in("w_ssm_proj", [512, D])
    wap_d = dram_in("w_attn_proj", [256, D])
    wout_d = dram_in("w_out", [D, D])
    wup_d = dram_in("w_up", [D, DFF])
    wdn_d = dram_in("w_down", [DFF, D])
    lnrows_d = dram_in("ln_rows", [4, 128, D])
    out_d = nc.dram_tensor("out", [L, D], F32, kind="ExternalOutput").ap()
    dbg = {}

    ar = Arena(nc, es, 206 * KB)
    ps_all = es.enter_context(nc.psum_tensor("ps", [128, 4096], F32))
    banks = [Buf(ps_all[:, b * 512:(b + 1) * 512], excl=True) for b in range(8)]

    def bank_bf(b):
        return banks[b].ap.bitcast(BF16)

    def R(*bufs):
        out = []
        for b in bufs:
            if isinstance(b, Buf):
                out.extend(b.res)
            elif isinstance(b, Res):
                out.append(b)
            else:
                out.extend(R(*b))
        return out

    def dma(eng, out, in_, reads, writes):
        return P.add(eng, lambda e: e.dma_start(out=out, in_=in_), R(*reads), R(*writes), dma=True)

    def mm(out, lhsT, rhs, start, stop, reads, writes, **kw):
        return P.add("pe", lambda e: e.matmul(out, lhsT, rhs, start=start, stop=stop, **kw),
                     R(*reads), R(*writes))

    def tr(out, in_, ident, reads, writes):
        return P.add("pe", lambda e: e.transpose(out, in_, ident), R(*reads), R(*writes))

    def act(out, in_, func, reads, writes, bias=0.0, scale=1.0):
        return P.add("act", lambda e: e.activation(out, in_, func, bias=bias, scale=scale),
                     R(*reads), R(*writes))

    def tt(eng, out, in0, in1, op, reads, writes):
        return P.add(eng, lambda e: e.tensor_tensor(out, in0, in1, op), R(*reads), R(*writes))

    def ts(eng, out, in0, s1, s2, op0, op1, reads, writes):
        if s2 is None:
            return P.add(eng, lambda e: e.tensor_scalar(out, in0, s1, None, op0), R(*reads), R(*writes))
        return P.add(eng, lambda e: e.tensor_scalar(out, in0, s1, s2, op0, op1), R(*reads), R(*writes))

    def stt(out, in0, scalar, in1, op0, op1, reads, writes):
        return P.add("dve", lambda e: e.scalar_tensor_tensor(out, in0, scalar, in1, op0, op1),
                     R(*reads), R(*writes))

    def cp(eng, out, in_, reads, writes):
        if eng == "act":
            return P.add("act", lambda e: e.copy(out, in_), R(*reads), R(*writes))
        return P.add(eng, lambda e: e.tensor_copy(out, in_), R(*reads), R(*writes))

    def scan(dst, d0, src, extra_reads):
        return P.add("dve", lambda e: e.tensor_tensor_scan(dst.ap, d0, src.ap, 0.0, ALU.mult, ALU.add),
                     R(src, *extra_reads), R(dst))

    def memset(eng, out, val, writes):
        return P.add(eng, lambda e: e.memset(out, val), (), R(*writes))

    def dump(name, buf_ap, shape, dt, reads):
        d = nc.dram_tensor("dbg_" + name, list(shape), dt, kind="ExternalOutput").ap()
        dbg[name] = d
        dma("sp", d, buf_ap, reads, [])

    ar.seek(0, 8 * KB)
    smP = ar.alloc([SP_N], F32)
    dma("sp", smP.ap, smP_d, [], [smP])

    def sp_(name):
        o, n = SP_COLS[name]
        return smP.ap[:, o:o + n]

    ident_f = sp_("ident")
    identb = ar.alloc([128], BF16)
    cp("dve", identb.ap, ident_f, [smP], [identb])
    onesAB = ar.alloc([2, 128], BF16)
    memset("pool", onesAB.ap, 0.0, [onesAB])
    memset("pool", onesAB.ap[:, 0, 0:64], 1.0, [onesAB])
    memset("pool", onesAB.ap[:, 1, 64:128], 1.0, [onesAB])

    rp = ar.alloc([32], F32)
    f8p = ar.alloc([32], F32)
    e_dt = ar.alloc([32], F32)
    e_a = ar.alloc([32], F32)
    e_b = ar.alloc([32], F32)
    act(e_dt.ap, sp_("logdt"), AF.Exp, [smP], [e_dt])
    tt("dve", e_a.ap, sp_("lamre"), e_dt.ap, ALU.mult, [smP, e_dt], [e_a])
    act(rp.ap, e_a.ap, AF.Exp, [e_a], [rp], scale=8.0)
    tt("dve", e_a.ap, sp_("lamim"), e_dt.ap, ALU.mult, [smP, e_dt], [e_a])
    ts("dve", e_a.ap, e_a.ap, 8.0, 1.0 / (2.0 * math.pi), ALU.mult, ALU.mult, [e_a], [e_a])
    ts("dve", e_b.ap, e_a.ap, MAGIC, -MAGIC, ALU.add, ALU.add, [e_a], [e_b])
    tt("dve", e_b.ap, e_a.ap, e_b.ap, ALU.subtract, [e_a, e_b], [e_b])
    stt(f8p.ap, e_b.ap, 0.0, e_b.ap, ALU.is_lt, ALU.add, [e_b], [f8p])
    tabs_d = nc.dram_tensor("tabs_scratch", [32, 128, 2, 512], F32, kind="Internal").ap()
    tabs_res = Buf(None, 16)
    xTs_d = nc.dram_tensor("xT_scratch", [128, 8, L], BF16, kind="Internal").ap()
    xTs_res = Buf(None, 4)

    ar.seek(8 * KB, 72 * KB)
    xT = ar.alloc([8, L], BF16, nres=32)
    ar.seek(72 * KB, 104 * KB)
    Ug = ar.alloc([32, 512], BF16, nres=32)
    ar.seek(104 * KB, 120 * KB)
    yattnT = ar.alloc([2, L], BF16, nres=2)
    T0, T1 = 120 * KB, 206 * KB

    ar.seek(160 * KB, T1)
    xs = [ar.alloc([D], F32) for _ in range(3)]
    pi = 0
    for tt_i in range(32):
        xb = xs[tt_i % 3]
        dma("sp", xb.ap, x_d[tt_i * 128:(tt_i + 1) * 128, :], [], [xb])
        for half in range(2):
            bk = banks[pi % 2]
            pi += 1
            for j in range(4):
                kc = half * 4 + j
                tr(bk.ap[:, j * 128:(j + 1) * 128], xb.ap[:, kc * 128:(kc + 1) * 128], ident_f,
                   [xb, smP], [bk])
            dst = xT.ap[:, half * 4:half * 4 + 4, tt_i * 128:(tt_i + 1) * 128]
            src = bk.ap.rearrange("p (j t) -> p j t", j=4)
            cp("dve" if half == 0 else "act", dst, src, [bk], [xT.res[tt_i]])
        if tt_i % 8 == 7:
            b4 = tt_i // 8
            dma("sp", xTs_d[:, :, b4 * 1024:(b4 + 1) * 1024], xT.ap[:, :, b4 * 1024:(b4 + 1) * 1024],
                [xT.res[b4 * 8 + i] for i in range(8)], [xTs_res.res[b4]])
    if "xT" in debug:
        dump("xT", xT.ap, [128, 8, L], BF16, [xT])

    if "U" in stages:
        ar.seek(T0, T1)
        wu = ar.alloc([8, 512], BF16)
        dma("pool", wu.ap, w_in_d[:, 2304:2816].rearrange("(kc p) j -> p kc j", p=128), [], [wu])
        Uc = ar.alloc([4, 32, 128], BF16, nres=4)
        xT5 = xT.ap.rearrange("p k (b c s) -> p k b c s", b=4, s=8)
        ar.seek(172 * KB, T1)
        gcs = [ar.alloc([2, 2, 512], F32) for _ in range(2)]
        gAs = [ar.alloc([2, 512], F32) for _ in range(2)]
        gFs = [ar.alloc([2, 512], F32) for _ in range(2)]
        iota = sp_("iota")

        def gen_tables(bt):
            cs_, tA, tF = gcs[bt % 2], gAs[bt % 2], gFs[bt % 2]
            tt("dve", tA.ap, f8p.ap[:, 2 * bt:2 * bt + 2, None].to_broadcast([128, 2, 512]),
               iota[:, None, :].to_broadcast([128, 2, 512]), ALU.mult, [f8p, smP], [tA])
            ts("dve", tF.ap, tA.ap, MAGIC, -MAGIC, ALU.add, ALU.add, [tA], [tF])
            tt("dve", tF.ap, tA.ap, tF.ap, ALU.subtract, [tA, tF], [tF])
            act(cs_.ap[:, :, 1, :], tF.ap, AF.Sin, [tF], [cs_], scale=TWO_PI)
            act(cs_.ap[:, :, 0, :], tF.ap, AF.Sin, [tF], [cs_], scale=TWO_PI / 2.0)
            act(cs_.ap[:, :, 0, :], cs_.ap[:, :, 0, :], AF.Square, [cs_], [cs_])
            ts("pool", cs_.ap[:, :, 0, :], cs_.ap[:, :, 0, :], -2.0, 1.0, ALU.mult, ALU.add, [cs_], [cs_])
            dma("sp", tabs_d[2 * bt:2 * bt + 2].rearrange("g p t c -> p g t c"), cs_.ap, [cs_],
                [tabs_res.res[bt]])

        for blk in range(4):
            for s in range(8):
                if s % 2 == 0:
                    gen_tables(blk * 4 + s // 2)
                bk = banks[pi % 2]
                pi += 1
                for kc in range(8):
                    mm(bk.ap, xT5[:, kc, blk, :, s], wu.ap[:, kc, :], kc == 0, kc == 7,
                       [xT.res[blk * 8 + i] for i in range(8)] + [wu], [bk])
                cp("act" if s % 2 else "dve", Uc.ap[:, blk, :, s * 16:(s + 1) * 16],
                   bk.ap.rearrange("p (g h) -> p g h", g=32), [bk], [Uc.res[blk]])
        for g in range(32):
            b = 2 + pi % 2
            pi += 1
            bkb = bank_bf(b)
            for blk in range(4):
                tr(bkb[:, blk * 128:(blk + 1) * 128], Uc.ap[:, blk, g, :], identb.ap,
                   [Uc.res[blk], identb], [banks[b]])
            cp("act" if g % 2 else "dve", Ug.ap[:, g, :], bkb[:, 0:512], [banks[b]], [Ug.res[g]])
        if "Ug" in debug:
            dump("Ug", Ug.ap, [128, 32, 512], BF16, [Ug])

    if "ATT" in stages:
        ar.seek(T0, T1)
        biasb = [ar.alloc([2, 512], BF16) for _ in range(1)]
        wq = [ar.alloc([8, 384], BF16) for _ in range(1)]
        qT = ar.alloc([2, L], BF16)
        kT = ar.alloc([L], BF16)
        memset("dve", qT.ap, 0.0, [qT])
        V = ar.alloc([32, 2, 128], BF16, nres=32)
        acc = ar.alloc([2, L], F32)
        PT = [ar.alloc([512], BF16) for _ in range(2)]
        memset("dve", V.ap, 0.0, [V])
        pti = 0
        si = 0
        for hp in range(2):
            for g in range(3):
                rnd = hp * 3 + g
                dil = DIL[g]
                nblk = 32 // dil
                bb = biasb[0]
                wb = wq[0]
                dma("pool", bb.ap, biasT_d[rnd], [], [bb])
                for j, c0 in enumerate((0, 768, 1536)):
                    col = c0 + g * 256 + hp * 128
                    dma("pool", wb.ap[:, :, j * 128:(j + 1) * 128],
                        w_in_d[:, col:col + 128].rearrange("(kc p) j -> p kc j", p=128), [], [wb])
                for which, dstT in ((0, qT), (1, kT)):
                    for nt in range(8):
                        bk = banks[pi % 2]
                        pi += 1
                        for kc in range(8):
                            mm(bk.ap, wb.ap[:, kc, which * 128:(which + 1) * 128],
                               xT.ap[:, kc, nt * 512:(nt + 1) * 512], kc == 0, kc == 7,
                               [wb] + [xT.res[nt * 4 + i] for i in range(4)], [bk])
                        cs_ = slice(nt * 512, (nt + 1) * 512)
                        if which == 0:
                            ts("dve", qT.ap[0:64, 0, cs_], bk.ap[0:64, :], 0.125, None, ALU.mult, None, [bk], [qT])
                            ts("dve", qT.ap[64:128, 1, cs_], bk.ap[64:128, :], 0.125, None, ALU.mult, None,
                               [bk], [qT])
                        else:
                            cp("dve", kT.ap[:, cs_], bk.ap, [bk], [kT])
                xTd = xT.ap.rearrange("p k (i a d) -> p k d i a", a=128, d=dil)
                for t4 in range(8):
                    b = 6 + t4 % 2
                    bk = banks[b]
                    for j in range(4):
                        tile = t4 * 4 + j
                        r, i = tile // nblk, tile % nblk
                        for kc in range(8):
                            mm(bk.ap[:, j * 128:(j + 1) * 128], xTd[:, kc, r, i, :], wb.ap[:, kc, 256:384],
                               kc == 0, kc == 7, [xT, wb], [bk])
                    src = bk.ap.rearrange("p (t c) -> p t c", t=4)
                    cp("dve", V.ap[:, t4 * 4:t4 * 4 + 4, 0, 0:64], src[:, :, 0:64], [bk],
                       [V.res[t4 * 4 + j] for j in range(4)])
                    cp("dve", V.ap[:, t4 * 4:t4 * 4 + 4, 1, 64:128], src[:, :, 64:128], [bk],
                       [V.res[t4 * 4 + j] for j in range(4)])
                qTd = qT.ap.rearrange("p h (i a d) -> p h d i a", a=128, d=dil)
                kTd = kT.ap.rearrange("p (i a d) -> p d i a", a=128, d=dil)
                accd = acc.ap.rearrange("p c (i a d) -> p c d i a", a=128, d=dil)
                tiles = [(r, i) for r in range(dil) for i in range(nblk)]

                def s_part(r, i, sb, pt):
                    first = (i == 0)
                    mm(sb.ap, identb.ap, bb.ap[:, 1 if first else 0, :], True, False, [identb, bb], [sb])
                    combos = [(h2, kb) for h2 in range(2) for kb in range(2) if not (first and kb == 0)]
                    for n_, (h2, kb) in enumerate(combos):
                        c0 = h2 * 256 + kb * 128
                        mm(sb.ap[:, c0:c0 + 128], kTd[:, r, i - 1 + kb, :], qTd[:, h2, r, i, :],
                           False, n_ == len(combos) - 1, [kT, qT], [sb])
                    act(pt.ap, sb.ap, AF.Exp, [sb], [pt])

                def pv_part(r, i, pb, pt):
                    first = (i == 0)
                    combos = [(h2, kb) for h2 in range(2) for kb in range(2) if not (first and kb == 0)]
                    for part in range(2):
                        for n_, (h2, kb) in enumerate(combos):
                            tl = r * nblk + i - 1 + kb
                            c0 = h2 * 256 + kb * 128
                            lhs = V.ap[:, tl, h2, :] if part == 0 else onesAB.ap[:, h2, :]
                            mm(pb.ap[:, part * 128:(part + 1) * 128], lhs, pt.ap[:, c0:c0 + 128],
                               n_ == 0, n_ == len(combos) - 1, [V.res[tl], onesAB, pt], [pb])
                    dsta = accd[:, :, r, i, :]
                    src = pb.ap[:, 0:256].rearrange("p (c a) -> p c a", c=2)
                    if g == 0:
                        cp("dve", dsta, src, [pb], [acc])
                    else:
                        tt("dve", dsta, dsta, src, ALU.add, [acc, pb], [acc])

                prev = None
                for (r, i) in tiles:
                    sb = banks[2 + si % 2]
                    pb = banks[4 + si % 2]
                    pt = PT[si % 2]
                    si += 1
                    s_part(r, i, sb, pt)
                    if prev is not None:
                        pv_part(*prev)
                    prev = (r, i, pb, pt)
                pv_part(*prev)
            P.add("dve", lambda e: e.reciprocal(acc.ap[:, 1, :], acc.ap[:, 1, :]), R(acc), R(acc))
            tt("dve", yattnT.ap[:, hp, :], acc.ap[:, 0, :], acc.ap[:, 1, :], ALU.mult, [acc], [yattnT.res[hp]])
        if "yattnT" in debug:
            dump("yattnT", yattnT.ap, [128, 2, L], BF16, [yattnT])

    if "SSM" in stages:
        ar.seek(8 * KB, 40 * KB)
        Yc = ar.alloc([4, 8, 512], BF16, nres=4)
        ar.seek(T0, 161 * KB)
        MT = ar.alloc([32, 128], BF16)
        L1b = ar.alloc([32, 128], BF16)
        L2b = ar.alloc([32, 128], BF16)
        WS1T = ar.alloc([32, 128], BF16)
        WS2T = ar.alloc([32, 128], BF16)
        ar.seek(161 * KB, T1)
        dt_ = ar.alloc([32], F32)
        lrdt = ar.alloc([32], F32)
        th = ar.alloc([32], F32)
        den = ar.alloc([32], F32)
        t32a = ar.alloc([32], F32)
        t32b = ar.alloc([32], F32)
        kre = ar.alloc([32], F32)
        kim = ar.alloc([32], F32)
        argm = ar.alloc([32, 16], F32)
        magp = ar.alloc([32, 16], F32)
        magn = ar.alloc([32, 16], F32)
        turns = ar.alloc([32, 16], F32)
        ti = ar.alloc([32, 16], I32)
        tf = argm
        fr = ar.alloc([32, 16], F32)
        msf = ar.alloc([32, 16], F32)
        frc = turns
        msc = msf
        c16 = ar.alloc([32, 16], F32)
        s16 = ar.alloc([32, 16], F32)
        Pc = ar.alloc([32, 16], F32)
        Ps = ar.alloc([32, 16], F32)
        Nc = ar.alloc([32, 16], F32)
        Ns = ar.alloc([32, 16], F32)
        b2 = c16
        c1 = s16
        Xb = ar.alloc([32, 16], F32)
        Yb = ar.alloc([32, 16], F32)
        tx = magn
        ar.seek(40 * KB, 72 * KB)
        smB = ar.alloc([SB_N], F32)
        dma("sp", smB.ap, smB_d, [], [smB])

        def sb_(name):
            o, n = SB_COLS[name]
            return smB.ap[:, o:o + n]

        E = "dve"
        lamre, lamim = sb_("lamre"), sb_("lamim")
        act(dt_.ap, sb_("logdt"), AF.Exp, [smB], [dt_])
        tt(E, lrdt.ap, lamre, dt_.ap, ALU.mult, [smB, dt_], [lrdt])
        tt(E, th.ap, lamim, dt_.ap, ALU.mult, [smB, dt_], [th])
        tt(E, t32a.ap, lamre, lamre, ALU.mult, [smB], [t32a])
        tt(E, t32b.ap, lamim, lamim, ALU.mult, [smB], [t32b])
        tt(E, den.ap, t32a.ap, t32b.ap, ALU.add, [t32a, t32b], [den])
        P.add("dve", lambda e: e.reciprocal(den.ap, den.ap), R(den), R(den))
        kv = sp_("kvals")
        kvb = kv[:, None, :].to_broadcast([128, 32, 16])
        tt(E, argm.ap, lrdt.ap[:, :, None].to_broadcast([128, 32, 16]), kvb, ALU.mult, [lrdt, smP], [argm])
        act(magp.ap, argm.ap, AF.Exp, [argm], [magp])
        act(magn.ap, argm.ap, AF.Exp, [argm], [magn], scale=-1.0)
        tt(E, turns.ap, th.ap[:, :, None].to_broadcast([128, 32, 16]), kvb, ALU.mult, [th, smP], [turns])
        ts(E, turns.ap, turns.ap, 1.0 / (2.0 * math.pi), None, ALU.mult, None, [turns], [turns])
        ts(E, tf.ap, turns.ap, MAGIC, -MAGIC, ALU.add, ALU.add, [turns], [tf])
        tt(E, fr.ap, turns.ap, tf.ap, ALU.subtract, [turns, tf], [fr])
        stt(msf.ap, fr.ap, 0.5, fr.ap, ALU.is_gt, ALU.subtract, [fr], [msf])
        act(s16.ap, msf.ap, AF.Sin, [msf], [s16], scale=-TWO_PI)
        ts(E, frc.ap, fr.ap, 0.25, None, ALU.add, None, [fr], [frc])
        stt(msc.ap, frc.ap, 0.5, frc.ap, ALU.is_gt, ALU.subtract, [frc], [msc])
        act(c16.ap, msc.ap, AF.Sin, [msc], [c16], scale=-TWO_PI)
        tt(E, Pc.ap, magp.ap, c16.ap, ALU.mult, [magp, c16], [Pc])
        tt(E, Ps.ap, magp.ap, s16.ap, ALU.mult, [magp, s16], [Ps])
        tt(E, Nc.ap, magn.ap, c16.ap, ALU.mult, [magn, c16], [Nc])
        tt(E, Ns.ap, magn.ap, s16.ap, ALU.mult, [magn, s16], [Ns])
        abre, abim = Pc.ap[:, :, 0], Ps.ap[:, :, 0]
        nr = t32a
        ts(E, nr.ap, abre, -1.0, None, ALU.add, None, [Pc], [nr])
        tt(E, t32b.ap, nr.ap, lamre, ALU.mult, [nr, smB], [t32b])
        tt(E, kre.ap, abim, lamim, ALU.mult, [Ps, smB], [kre])
        tt(E, kre.ap, kre.ap, t32b.ap, ALU.add, [kre, t32b], [kre])
        tt(E, kre.ap, kre.ap, den.ap, ALU.mult, [kre, den], [kre])
        tt(E, t32b.ap, nr.ap, lamim, ALU.mult, [nr, smB], [t32b])
        tt(E, kim.ap, abim, lamre, ALU.mult, [Ps, smB], [kim])
        tt(E, kim.ap, kim.ap, t32b.ap, ALU.subtract, [kim, t32b], [kim])
        tt(E, kim.ap, kim.ap, den.ap, ALU.mult, [kim, den], [kim])
        sgn = sp_("sgn")
        b1 = sb_("b1").rearrange("p (g h) -> p g h", g=32)
        b2raw = sb_("b2raw").rearrange("p (g h) -> p g h", g=32)
        c1raw = sb_("c1raw").rearrange("p (g h) -> p g h", g=32)
        c2 = sb_("c2").rearrange("p (g h) -> p g h", g=32)
        ts(E, b2.ap, b2raw, sgn, None, ALU.mult, None, [smB, smP], [b2])
        ts(E, c1.ap, c1raw, sgn, None, ALU.mult, None, [smB, smP], [c1])
        kreb = kre.ap[:, :, None].to_broadcast([128, 32, 16])
        kimb = kim.ap[:, :, None].to_broadcast([128, 32, 16])
        tt(E, Xb.ap, kreb, b1, ALU.mult, [kre, smB], [Xb])
        tt(E, tx.ap, kimb, b2.ap, ALU.mult, [kim, b2], [tx])
        tt(E, Xb.ap, Xb.ap, tx.ap, ALU.subtract, [Xb, tx], [Xb])
        tt(E, Yb.ap, kreb, b2.ap, ALU.mult, [kre, b2], [Yb])
        tt(E, tx.ap, kimb, b1, ALU.mult, [kim, smB], [tx])
        tt(E, Yb.ap, Yb.ap, tx.ap, ALU.add, [Yb, tx], [Yb])

        GB = 8
        AB1f = ar.alloc([GB, 8, 16], F32)
        L1f = ar.alloc([GB, 8, 16], F32)
        WSf = ar.alloc([GB, 8, 16], F32)
        mtmp = [ar.alloc([128], F32) for _ in range(2)]
        bigAs = [ar.alloc([GB, 8, 16], F32)]
        bigBs = [ar.alloc([GB, 8, 16], F32)]
        ar.seek(192 * KB, T1)
        bigAs.append(ar.alloc([GB, 8, 16], F32))
        bigBs.append(ar.alloc([GB, 8, 16], F32))
        obi = [0]

        def outer2(tabA, kA, vecA, tabB, kB, vecB, g0, readsA, readsB):
            bigA, bigB = bigAs[obi[0] % 2], bigBs[obi[0] % 2]
            obi[0] += 1
            for dst, tab, k0, vec, eng, rd in ((bigA, tabA, kA, vecA, "dve", readsA),
                                               (bigB, tabB, kB, vecB, "pool", readsB)):
                a = tab[:, g0:g0 + GB, k0:k0 + 8][:, :, :, None].to_broadcast([128, GB, 8, 16])
                b = vec[:, g0:g0 + GB, None, :].to_broadcast([128, GB, 8, 16])
                tt(eng, dst.ap, a, b, ALU.mult, rd, [dst])
            return bigA, bigB

        pbi = 0
        g3 = "p g s h -> p g (s h)"
        for gb in range(32 // GB):
            g0 = gb * GB
            bigA, bigB = outer2(Nc.ap, 0, Xb.ap, Ns.ap, 0, Yb.ap, g0, [Nc, Xb], [Ns, Yb])
            tt("dve", AB1f.ap, bigA.ap, bigB.ap, ALU.add, [bigA, bigB], [AB1f])
            bigA, bigB = outer2(Pc.ap, 0, c1.ap, Ps.ap, 0, c2, g0, [Pc, c1], [Ps, smB])
            tt("dve", L1f.ap, bigA.ap, bigB.ap, ALU.subtract, [bigA, bigB], [L1f])
            cp("act", L1b.ap[:, g0:g0 + GB, :], L1f.ap.rearrange(g3), [L1f], [L1b])
            for g in range(GB):
                bk = banks[pbi % 2]
                mt = mtmp[pbi % 2]
                pbi += 1
                mm(bk.ap[:, 0:128], AB1f.ap[:, g].rearrange("p s h -> p (s h)"),
                   L1f.ap[:, g].rearrange("p s h -> p (s h)"), True, True, [AB1f, L1f], [bk])
                tt("dve", mt.ap, bk.ap[:, 0:128], sp_("maskM"), ALU.mult, [bk, smP], [mt])
                o_, _n = SB_COLS["drep"]
                stt(MT.ap[:, g0 + g, :], ident_f, smB.ap[:, o_ + g0 + g:o_ + g0 + g + 1], mt.ap,
                    ALU.mult, ALU.add, [smP, smB, mt], [MT])
            bigA, bigB = outer2(Ps.ap, 0, c1.ap, Pc.ap, 0, c2, g0, [Ps, c1], [Pc, smB])
            stt(L2b.ap[:, g0:g0 + GB, :], bigA.ap.rearrange(g3), -1.0, bigB.ap.rearrange(g3),
                ALU.mult, ALU.subtract, [bigA, bigB], [L2b])
            for which in range(2):
                if which == 0:
                    bigA, bigB = outer2(Pc.ap, 8, Xb.ap, Ps.ap, 8, Yb.ap, g0, [Pc, Xb], [Ps, Yb])
                    tt("dve", WSf.ap, bigA.ap, bigB.ap, ALU.subtract, [bigA, bigB], [WSf])
                else:
                    bigA, bigB = outer2(Ps.ap, 8, Xb.ap, Pc.ap, 8, Yb.ap, g0, [Ps, Xb], [Pc, Yb])
                    tt("dve", WSf.ap, bigA.ap, bigB.ap, ALU.add, [bigA, bigB], [WSf])
                dstT = WS1T if which == 0 else WS2T
                for g4 in range(GB // 4):
                    bk = banks[pbi % 2]
                    pbi += 1
                    for j in range(4):
                        g = g4 * 4 + j
                        tr(bk.ap[:, j * 128:(j + 1) * 128], WSf.ap[:, g].rearrange("p s h -> p (s h)"),
                           ident_f, [WSf, smP], [bk])
                    cp("act", dstT.ap[:, g0 + g4 * 4:g0 + g4 * 4 + 4, :],
                       bk.ap.rearrange("p (j q) -> p j q", j=4), [bk], [dstT])
        if "ssmw" in debug:
            dump("MT", MT.ap, [128, 32, 128], BF16, [MT])
            dump("L1b", L1b.ap, [128, 32, 128], BF16, [L1b])
            dump("L2b", L2b.ap, [128, 32, 128], BF16, [L2b])
            dump("WS1T", WS1T.ap, [128, 32, 128], BF16, [WS1T])
            dump("WS2T", WS2T.ap, [128, 32, 128], BF16, [WS2T])
            dump("rp", rp.ap, [128, 32], F32, [rp])
            dump("f8p", f8p.ap, [128, 32], F32, [f8p])

        ar.seek(161 * KB, T1)
        tabb = [ar.alloc([2, 512], F32) for _ in range(4)]
        ar.seek(40 * KB, 72 * KB)
        t1s = [ar.alloc([512], F32) for _ in range(2)]
        t2s = [ar.alloc([512], F32) for _ in range(2)]
        cms = [ar.alloc([512], F32) for _ in range(2)]
        wss = [ar.alloc([512], F32) for _ in range(2)]
        V1s = [ar.alloc([520], BF16) for _ in range(2)]
        V2s = [ar.alloc([520], BF16) for _ in range(2)]
        for v in V1s + V2s:
            memset("pool", v.ap[:, 0:8], 0.0, [v])
        zb = [banks[0], banks[1], banks[2], banks[3]]
        yb = [banks[4], banks[5], banks[6], banks[7]]

        def load_tab(g):
            dma("sp", tabb[g % 4].ap, tabs_d[g], [tabs_res.res[g // 2]], [tabb[g % 4]])

        def z_part(g):
            z1, z2 = zb[(g * 2) % 4], zb[(g * 2 + 1) % 4]
            mm(z1.ap, WS1T.ap[:, g, :], Ug.ap[:, g, :], True, True, [WS1T, Ug.res[g]], [z1])
            mm(z2.ap, WS2T.ap[:, g, :], Ug.ap[:, g, :], True, True, [WS2T, Ug.res[g]], [z2])

        for g in range(3):
            load_tab(g)
        z_part(0)
        for g in range(32):
            if g + 3 < 32:
                load_tab(g + 3)
            if g + 1 < 32:
                z_part(g + 1)
            cosT, sinT = tabb[g % 4].ap[:, 0, :], tabb[g % 4].ap[:, 1, :]
            cosR = sinR = tabb[g % 4]
            z1, z2 = zb[(g * 2) % 4], zb[(g * 2 + 1) % 4]
            t1, t2, cm, ws = t1s[g % 2], t2s[g % 2], cms[g % 2], wss[g % 2]
            v1, v2 = V1s[g % 2], V2s[g % 2]
            tt("dve", t1.ap, z1.ap, cosT, ALU.mult, [z1, cosR], [t1])
            tt("dve", t2.ap, z2.ap, sinT, ALU.mult, [z2, sinR], [t2])
            tt("dve", cm.ap, t1.ap, t2.ap, ALU.add, [t1, t2], [cm])
            scan(ws, rp.ap[:, g:g + 1].to_broadcast([128, 512]), cm, [rp])
            tt("pool", v1.ap[:, 8:520], ws.ap, cosT, ALU.mult, [ws, cosR], [v1])
            tt("pool", v2.ap[:, 8:520], ws.ap, sinT, ALU.mult, [ws, sinR], [v2])
            for cb in range(4):
                o = yb[cb].ap[:, (g % 4) * 128:(g % 4 + 1) * 128]
                mm(o, Ug.ap[:, g, cb * 128:(cb + 1) * 128], MT.ap[:, g, :], True, False,
                   [Ug.res[g], MT], [yb[cb]])
                mm(o, v1.ap[:, 7 + cb * 128:7 + (cb + 1) * 128], L1b.ap[:, g, :], False, False,
                   [v1, L1b], [yb[cb]])
                mm(o, v2.ap[:, 7 + cb * 128:7 + (cb + 1) * 128], L2b.ap[:, g, :], False, True,
                   [v2, L2b], [yb[cb]])
            if g % 4 == 3:
                for cb in range(4):
                    yp = yb[cb]
                    g4 = g - 3
                    dst = Yc.ap[:, cb, :, g4 * 16:(g4 + 4) * 16].rearrange("p t (g h) -> p g t h", g=4)
                    act(dst, yp.ap.rearrange("p (g t h) -> p g t h", g=4, t=8), AF.Gelu_apprx_tanh,
                        [yp], [Yc.res[cb]])
        if "Yc" in debug:
            dump("Yc", Yc.ap, [128, 4, 8, 512], BF16, [Yc])
        ar.seek(72 * KB, 104 * KB)
        YT = ar.alloc([4, L], BF16)
        for cb in range(4):
            for j in range(4):
                b = pi % 2
                pi += 1
                bkb = bank_bf(b)
                for t in range(8):
                    tr(bkb[:, t * 128:(t + 1) * 128], Yc.ap[:, cb, t, j * 128:(j + 1) * 128], identb.ap,
                       [Yc.res[cb], identb], [banks[b]])
                dst = YT.ap[:, j, cb * 1024:(cb + 1) * 1024].rearrange("p (c t) -> p t c", t=8)
                cp("act" if j % 2 else "dve", dst, bkb.rearrange("p (t c) -> p t c", t=8),
                   [banks[b]], [YT])
        ar.seek(8 * KB, 40 * KB)
        Y2T = ar.alloc([4, L], BF16)
        ar.seek(161 * KB, T1)
        wg = ar.alloc([4, 1024], BF16)
        dma("pool", wg.ap, w_glu_d.rearrange("(kc p) j -> p kc j", p=128), [], [wg])
        sgs = [ar.alloc([512], F32) for _ in range(4)]
        gi_ = 0
        for m in range(4):
            for nt in range(8):
                ba, bb_ = banks[(gi_ * 2) % 8], banks[(gi_ * 2 + 1) % 8]
                sgb = sgs[gi_ % 4]
                gi_ += 1
                for kc in range(4):
                    mm(ba.ap, wg.ap[:, kc, m * 128:(m + 1) * 128], YT.ap[:, kc, nt * 512:(nt + 1) * 512],
                       kc == 0, kc == 3, [wg, YT], [ba])
                for kc in range(4):
                    mm(bb_.ap, wg.ap[:, kc, 512 + m * 128:512 + (m + 1) * 128],
                       YT.ap[:, kc, nt * 512:(nt + 1) * 512], kc == 0, kc == 3, [wg, YT], [bb_])
                act(sgb.ap, bb_.ap, AF.Sigmoid, [bb_], [sgb])
                tt("dve", Y2T.ap[:, m, nt * 512:(nt + 1) * 512], ba.ap, sgb.ap, ALU.mult, [ba, sgb], [Y2T])
        if "Y2T" in debug:
            dump("Y2T", Y2T.ap, [128, 4, L], BF16, [Y2T])

    if "P3" in stages:
        ar.seek(40 * KB, 104 * KB)
        xs3 = [ar.alloc([D], F32) for _ in range(8)]
        ring = [ar.alloc([4096], BF16) for _ in range(4)]
        ar.seek(T0, T1)
        lnr = [ar.alloc([D], F32) for _ in range(2)]
        xTb = ar.alloc([8, 1024], BF16, nres=8)
        aT = ar.alloc([16, 1024], BF16)
        mixin = Buf(aT.ap[:, 0:8, :], 1)
        mixin.res = aT.res
        wsp = ar.alloc([4, 1024], BF16)
        wap = ar.alloc([2, 1024], BF16)
        dma("pool", wsp.ap, wsp_d.rearrange("(kc p) j -> p kc j", p=128), [], [wsp])
        dma("pool", wap.ap, wap_d.rearrange("(kc p) j -> p kc j", p=128), [], [wap])
        gsb = [ar.alloc([512], F32) for _ in range(2)]
        gab = [ar.alloc([512], F32) for _ in range(2)]
        tm1 = [ar.alloc([512], F32) for _ in range(2)]
        tm2 = [ar.alloc([512], F32) for _ in range(2)]
        sqb = tm1
        st6 = ar.alloc([12], F32)
        mv = ar.alloc([2], F32)
        rstd = ar.alloc([1], F32)
        bgate = sp_("bgate")
        ri = 0
        bi = 0

        def nb():
            nonlocal bi
            b = banks[bi % 8]
            bi += 1
            return b

        def load_piece(src_aps):
            nonlocal ri
            slot = ring[ri % 4]
            ri += 1
            for (v, src) in src_aps:
                dma("pool", v(slot), src, [], [slot])
            return slot

        def layer_norm(xb, which):
            P.add("dve", lambda e: e.bn_stats(st6.ap[:, 0:6], xb.ap[:, 0:512]), R(xb), R(st6))
            P.add("dve", lambda e: e.bn_stats(st6.ap[:, 6:12], xb.ap[:, 512:1024]), R(xb, st6), R(st6))
            P.add("dve", lambda e: e.bn_aggr(mv.ap, st6.ap), R(st6), R(mv))
            act(rstd.ap, mv.ap[:, 1:2], AF.Sqrt, [mv], [rstd], bias=LN_EPS)
            P.add("dve", lambda e: e.reciprocal(rstd.ap, rstd.ap), R(rstd), R(rstd))
            stt(xb.ap, xb.ap, mv.ap[:, 0:1], lnr[0].ap, ALU.subtract, ALU.mult, [xb, mv, lnr[0]], [xb])
            stt(xb.ap, xb.ap, rstd.ap[:, 0:1], lnr[1].ap, ALU.mult, ALU.add, [xb, rstd, lnr[1]], [xb])

        def to_featmajor(xb, t):
            for half in range(2):
                bk = nb()
                for j in range(4):
                    kc = half * 4 + j
                    tr(bk.ap[:, j * 128:(j + 1) * 128], xb.ap[:, kc * 128:(kc + 1) * 128], ident_f,
                       [xb, smP], [bk])
                cp("act", xTb.ap[:, half * 4:half * 4 + 4, t * 128:(t + 1) * 128],
                   bk.ap.rearrange("p (j t) -> p j t", j=4), [bk], [xTb.res[t]])

        v3 = lambda kcn, c0, nc_: (lambda slot: slot.ap.rearrange("p (k c) -> p k c", k=kcn)[:, :, c0:c0 + nc_])
        for blk in range(4):
            t0 = blk * 1024
            dma("sp", xTb.ap, xTs_d[:, :, t0:t0 + 1024], [xTs_res.res[blk]], [xTb])
            for t in range(8):
                dma("sp", xs3[t].ap, x_d[t0 + t * 128:t0 + (t + 1) * 128, :], [], [xs3[t]])
            for mq in range(4):
                gcol = 2816 + mq * 256
                piece = load_piece([
                    (v3(8, 0, 256), w_in_d[:, gcol:gcol + 256].rearrange("(kc p) j -> p kc j", p=128)),
                    (v3(8, 256, 256), w_in_d[:, gcol + 1024:gcol + 1280].rearrange("(kc p) j -> p kc j", p=128)),
                ])
                pv = piece.ap.rearrange("p (k c) -> p k c", k=8)
                for m2 in range(2):
                    m = mq * 2 + m2
                    for nh in range(2):
                        cs = slice(nh * 512, (nh + 1) * 512)
                        gs_, ga_, t1_, t2_ = gsb[nh], gab[nh], tm1[nh], tm2[nh]
                        b_gs, b_ga, b_ps, b_pa = nb(), nb(), nb(), nb()
                        for kc in range(8):
                            mm(b_gs.ap, pv[:, kc, m2 * 128:(m2 + 1) * 128], xTb.ap[:, kc, cs], kc == 0, kc == 7,
                               [piece, xTb], [b_gs])
                        for kc in range(8):
                            mm(b_ga.ap, pv[:, kc, 256 + m2 * 128:256 + (m2 + 1) * 128], xTb.ap[:, kc, cs],
                               kc == 0, kc == 7, [piece, xTb], [b_ga])
                        for kc in range(4):
                            mm(b_ps.ap, wsp.ap[:, kc, m * 128:(m + 1) * 128],
                               Y2T.ap[:, kc, t0 + nh * 512:t0 + (nh + 1) * 512], kc == 0, kc == 3,
                               [wsp, Y2T], [b_ps])
                        for kc in range(2):
                            mm(b_pa.ap, wap.ap[:, kc, m * 128:(m + 1) * 128],
                               yattnT.ap[:, kc, t0 + nh * 512:t0 + (nh + 1) * 512], kc == 0, kc == 1,
                               [wap, yattnT], [b_pa])
                        act(gs_.ap, b_gs.ap, AF.Sigmoid, [b_gs, smP], [gs_], bias=bgate[:, m:m + 1])
                        act(ga_.ap, b_ga.ap, AF.Sigmoid, [b_ga, smP], [ga_], bias=bgate[:, 8 + m:9 + m])
                        tt("dve", t1_.ap, b_ps.ap, gs_.ap, ALU.mult, [b_ps, gs_], [t1_])
                        tt("dve", t2_.ap, b_pa.ap, ga_.ap, ALU.mult, [b_pa, ga_], [t2_])
                        tt("dve", mixin.ap[:, m, cs], t1_.ap, t2_.ap, ALU.add, [t1_, t2_], [mixin])
            wo = [load_piece([(v3(8, 0, 512), wout_d[:, oh * 512:(oh + 1) * 512]
                               .rearrange("(kc p) j -> p kc j", p=128))]) for oh in range(2)]
            for i, r_ in enumerate(lnr):
                dma("sp", r_.ap, lnrows_d[i], [], [r_])
            for t in range(8):
                for oh in range(2):
                    bk = nb()
                    for kc in range(8):
                        mm(bk.ap, mixin.ap[:, kc, t * 128:(t + 1) * 128],
                           wo[oh].ap.rearrange("p (k c) -> p k c", k=8)[:, kc, :],
                           kc == 0, kc == 7, [mixin, wo[oh]], [bk])
                    xo = xs3[t].ap[:, oh * 512:(oh + 1) * 512]
                    stt(xo, xo, ALPHA, bk.ap, ALU.mult, ALU.add, [xs3[t], bk], [xs3[t]])
                layer_norm(xs3[t], 0)
                if t >= 1:
                    to_featmajor(xs3[t - 1], t - 1)
            to_featmajor(xs3[7], 7)
            for h2 in range(2):
                for up in range(4):
                    c0 = h2 * 2048 + up * 512
                    piece = load_piece([(v3(8, 0, 512),
                                         wup_d[:, c0:c0 + 512].rearrange("(kc p) j -> p kc j", p=128))])
                    pv = piece.ap.rearrange("p (k c) -> p k c", k=8)
                    for m4 in range(4):
                        mloc = up * 4 + m4
                        for nh in range(2):
                            cs = slice(nh * 512, (nh + 1) * 512)
                            bk = nb()
                            for kc in range(8):
                                mm(bk.ap, pv[:, kc, m4 * 128:(m4 + 1) * 128], xTb.ap[:, kc, cs], kc == 0, kc == 7,
                                   [piece, xTb], [bk])
                            sq_ = sqb[nh]
                            act(sq_.ap, bk.ap, AF.Square, [bk], [sq_])
                            stt(aT.ap[:, mloc, cs], bk.ap, 0.0, sq_.ap, ALU.is_gt, ALU.mult, [bk, sq_], [aT])
                def dn_piece(oh, kh):
                    r0 = h2 * 2048 + kh * 1024
                    return load_piece([(v3(8, 0, 512), wdn_d[r0:r0 + 1024, oh * 512:(oh + 1) * 512]
                                        .rearrange("(kc p) j -> p kc j", p=128))])

                def dn_tile(t, oh, pcs):
                    bk = nb()
                    for kc in range(16):
                        pc_ = pcs[kc // 8]
                        mm(bk.ap, aT.ap[:, kc, t * 128:(t + 1) * 128],
                           pc_.ap.rearrange("p (k c) -> p k c", k=8)[:, kc % 8, :], kc == 0, kc == 15,
                           [aT, pc_], [bk])
                    xo = xs3[t].ap[:, oh * 512:(oh + 1) * 512]
                    if h2 == 0:
                        stt(xo, xo, ALPHA, bk.ap, ALU.mult, ALU.add, [xs3[t], bk], [xs3[t]])
                    else:
                        tt("dve", xo, xo, bk.ap, ALU.add, [xs3[t], bk], [xs3[t]])

                if h2 == 0:
                    for oh in range(2):
                        pcs = [dn_piece(oh, kh) for kh in range(2)]
                        for t in range(8):
                            dn_tile(t, oh, pcs)
                else:
                    pcs2 = [[dn_piece(oh, kh) for kh in range(2)] for oh in range(2)]
                    for i, r_ in enumerate(lnr):
                        dma("sp", r_.ap, lnrows_d[2 + i], [], [r_])
                    for t in range(8):
                        for oh in range(2):
                            dn_tile(t, oh, pcs2[oh])
                        layer_norm(xs3[t], 1)
                        dma("sp", out_d[t0 + t * 128:t0 + (t + 1) * 128, :], xs3[t].ap, [xs3[t]], [])

    P.finalize(nc, es)
    with nc.Block() as block:
        @block.tensor
        def _(e):
            P.emit_engine("pe", e)

        @block.scalar
        def _(e):
            P.emit_engine("act", e)

        @block.vector
        def _(e):
            P.emit_engine("dve", e)

        @block.gpsimd
        def _(e):
            P.emit_engine("pool", e)

        @block.sync
        def _(e):
            P.emit_engine("sp", e)
    es.close()
    nops = {k: len(v) for k, v in P.ops.items()}
    return nc, dbg, nops


def make_in_maps(inputs):
    g = lambda k: np.asarray(inputs[k], np.float32)
    smP = build_smalls_p(g("b_gate")[0], g("lambda_re")[0], g("lambda_im")[0], g("log_dt")[0])
    smB = build_smalls_b(g("lambda_re")[0], g("lambda_im")[0], g("log_dt")[0], g("ssm_b_re")[0],
                         g("ssm_b_im")[0], g("ssm_c_re")[0], g("ssm_c_im")[0], g("ssm_d")[0])
    biasT = build_bias_tables(g("rel_bias"))
    lnrows = np.stack([np.broadcast_to(g(k)[0][None, :], (128, D)) for k in ("ln1_g", "ln1_b", "ln2_g", "ln2_b")])
    lnrows = np.ascontiguousarray(lnrows, np.float32)
    shared = {
        "w_in": np.ascontiguousarray(g("w_in")[0]), "smalls_p": smP, "smalls_b": smB,
        "w_glu": np.ascontiguousarray(g("w_glu")[0]), "bias_t": biasT,
        "w_ssm_proj": np.ascontiguousarray(g("w_ssm_proj")[0]),
        "w_attn_proj": np.ascontiguousarray(g("w_attn_proj")[0]),
        "w_out": np.ascontiguousarray(g("w_out")[0]), "w_up": np.ascontiguousarray(g("w_up")[0]),
        "w_down": np.ascontiguousarray(g("w_down")[0]), "ln_rows": lnrows,
    }
    x = g("x")
    return [dict(shared, x=np.ascontiguousarray(x[b])) for b in range(x.shape[0])]


_CACHE = {}


def kernel(**inputs):
    in_maps = make_in_maps(inputs)
    if "nc" not in _CACHE:
        _CACHE["nc"] = build_program()[0]
    nc = _CACHE["nc"]
    res = run_bass_kernel_spmd(nc, in_maps, core_ids=list(range(NCORES)))
    return np.stack([np.asarray(r["out"], np.float32) for r in res.results], 0)
```

```python
import math
from contextlib import ExitStack

import numpy as np
import concourse.bass as bass
import concourse.mybir as mybir
from concourse.bass_utils import run_bass_kernel_spmd

F32 = mybir.dt.float32
BF16 = mybir.dt.bfloat16
I32 = mybir.dt.int32
AF = mybir.ActivationFunctionType
ALU = mybir.AluOpType

D = 1024
L = 4096
NCORES = 8
DFF = 4096
INW = 4864
ALPHA = 2.0 ** 0.25
LN_EPS = 1e-5
NEG = -30000.0
TWO_PI = 6.283185
HALF_PI = 1.5707963
MAGIC = 12582912.0
DIL = (1, 4, 16)


class Res:
    __slots__ = ("w", "r", "rd", "excl")

    def __init__(self, excl=False):
        self.excl = excl
        self.w = None
        self.r = {}
        self.rd = []


class Op:
    __slots__ = ("eng", "emit", "deps", "dma", "sem", "val", "prev", "used", "seq")
    _seq = 0

    def __init__(self, eng, emit, dma):
        Op._seq += 1
        self.seq = Op._seq
        self.eng = eng
        self.emit = emit
        self.dma = dma
        self.deps = []
        self.sem = None
        self.val = 0
        self.prev = 0
        self.used = False


class Prog:
    ENGS = ("pe", "act", "dve", "pool", "sp")
    EPOCH = 20000

    def __init__(self):
        self.ops = {e: [] for e in self.ENGS}

    def add(self, eng, emit, reads=(), writes=(), dma=False):
        op = Op(eng, emit, dma)
        deps = {}
        ex = [r for r in reads if r.excl]
        if ex:
            writes = list(writes) + [r for r in ex if r not in writes]
            reads = [r for r in reads if not r.excl]

        def dep(o):
            if o is not None and o is not op:
                deps[id(o)] = o

        for r in reads:
            dep(r.w)
        for w in writes:
            if w.w is not None and (dma or w.w.dma or w.w.eng != eng or eng != "pe"):
                dep(w.w)
            for e, o in w.r.items():
                if dma or e != eng or eng != "pe":
                    dep(o)
            for o in w.rd:
                dep(o)
        for r in reads:
            if dma:
                r.rd.append(op)
            else:
                r.r[eng] = op
        for w in writes:
            w.w = op
            w.r = {}
            w.rd = []
        op.deps = list(deps.values())
        for o in op.deps:
            o.used = True
        self.ops[eng].append(op)
        return op

    def finalize(self, nc, es, ndma=12):
        self.sems = {}
        for e in self.ENGS:
            cnt = 0
            dcnt = 0
            dsems = []
            esems = []
            for op in self.ops[e]:
                if op.dma:
                    if len(dsems) < ndma:
                        dsems.append([es.enter_context(nc.semaphore(f"d_{e}_{len(dsems)}")), 0])
                    slot = dsems[dcnt % ndma]
                    dcnt += 1
                    op.prev = slot[1]
                    slot[1] += 16
                    op.sem = slot[0]
                    op.val = slot[1]
                elif op.used:
                    ep = cnt // self.EPOCH
                    if ep >= len(esems):
                        esems.append(es.enter_context(nc.semaphore(f"c_{e}_{ep}")))
                    op.sem = esems[ep]
                    op.val = cnt % self.EPOCH + 1
                    cnt += 1

    def emit_engine(self, e, eng):
        waited = {}

        def wait(sem, val):
            k = id(sem)
            if waited.get(k, 0) < val:
                eng.wait_ge(sem, val)
                waited[k] = val

        for op in self.ops[e]:
            for d in op.deps:
                wait(d.sem, d.val)
            if op.dma and op.prev > 0:
                wait(op.sem, op.prev)
            ins = op.emit(eng)
            if op.sem is not None:
                ins.then_inc(op.sem, 16 if op.dma else 1)
        last = {}
        for op in self.ops[e]:
            if op.dma:
                last[id(op.sem)] = (op.sem, op.val)
        for sem, val in last.values():
            wait(sem, val)


class Buf:
    def __init__(self, ap, nres=1, excl=False):
        self.ap = ap
        self.res = [Res(excl) for _ in range(nres)]

    @property
    def r0(self):
        return self.res[0]


class Arena:
    def __init__(self, nc, es, nbytes):
        self.t = es.enter_context(nc.sbuf_tensor("arena", [128, nbytes // 4], F32))
        self.cap = nbytes
        self.off = 0
        self.limit = nbytes
        self.hist = []

    def seek(self, off, limit=None):
        self.off = off
        self.limit = self.cap if limit is None else limit

    def alloc(self, shape, dtype, nres=1):
        n = int(np.prod(shape))
        nb = n * (2 if dtype == BF16 else 4)
        off = (self.off + 63) // 64 * 64
        assert off + nb <= self.limit, f"SBUF arena overflow {off + nb} > {self.limit}"
        self.off = off + nb
        v = self.t[:, off // 4:(off + nb + 3) // 4]
        if dtype != F32:
            v = v.bitcast(dtype)[:, 0:n]
        if len(shape) > 1:
            names = " ".join(f"a{i}" for i in range(len(shape)))
            kw = {f"a{i}": int(s) for i, s in enumerate(shape)}
            v = v.rearrange(f"p ({names}) -> p {names}", **kw)
        b = Buf(v, nres)
        seed_r, seed_rd = {}, []
        for (s0, s1, ob) in self.hist:
            if s0 < off + nb and off < s1:
                for rs in ob.res:
                    cands = list(rs.r.values()) + ([rs.w] if rs.w is not None else [])
                    for o in cands:
                        if o.dma:
                            seed_rd.append(o)
                        elif o.eng not in seed_r or seed_r[o.eng].seq < o.seq:
                            seed_r[o.eng] = o
                    seed_rd.extend(rs.rd)
        for rs in b.res:
            rs.r = dict(seed_r)
            rs.rd = list(seed_rd)
        self.hist.append((off, off + nb, b))
        return b


def _t5_bucket_np(dist):
    max_exact = 16
    d = np.maximum(dist, 1).astype(np.float32)
    large = max_exact + (np.log(d / np.float32(max_exact)) / np.float32(math.log(2048 / max_exact))
                         * np.float32(32 - max_exact)).astype(np.int32)
    large = np.minimum(large, 31)
    return np.where(dist < max_exact, dist, large)


def _layout(items):
    cols, off = {}, 0
    for name, n in items:
        cols[name] = (off, n)
        off += n
    return cols, off


SP_COLS, SP_N = _layout((("ident", 128), ("iota", 512), ("kvals", 16), ("sgn", 1), ("maskM", 128),
                         ("bgate", 16), ("lamre", 32), ("lamim", 32), ("logdt", 32)))
SB_COLS, SB_N = _layout((("lamre", 32), ("lamim", 32), ("logdt", 32), ("drep", 32),
                         ("b1", 512), ("b2raw", 512), ("c1raw", 512), ("c2", 512)))


def _put(a, cols, name, v):
    o, n = cols[name]
    a[:, o:o + n] = np.asarray(v, np.float32).reshape(128, n)


def build_smalls_p(b_gate, lambda_re, lambda_im, log_dt):
    a = np.zeros((128, SP_N), np.float32)
    _put(a, SP_COLS, "ident", np.eye(128, dtype=np.float32))
    _put(a, SP_COLS, "iota", np.broadcast_to(np.arange(512, dtype=np.float32)[None, :], (128, 512)))
    _put(a, SP_COLS, "kvals", np.broadcast_to(
        np.array([1, 2, 3, 4, 5, 6, 7, 8, 7, 6, 5, 4, 3, 2, 1, 0], np.float32)[None, :], (128, 16)))
    _put(a, SP_COLS, "sgn", np.concatenate([np.ones(64), -np.ones(64)])[:, None])
    s_idx = np.arange(128) // 16
    _put(a, SP_COLS, "maskM", (s_idx[None, :] >= s_idx[:, None]).astype(np.float32))
    _put(a, SP_COLS, "bgate", b_gate.reshape(16, 128).T)
    lr, li = lambda_re.T, lambda_im.T
    _put(a, SP_COLS, "lamre", np.concatenate([lr, lr], 0))
    _put(a, SP_COLS, "lamim", np.concatenate([li, li], 0))
    _put(a, SP_COLS, "logdt", np.broadcast_to(log_dt[None, :], (128, 32)))
    return a


def build_smalls_b(lambda_re, lambda_im, log_dt, b_re, b_im, c_re, c_im, d_skip):
    a = np.zeros((128, SB_N), np.float32)
    lr, li = lambda_re.T, lambda_im.T
    _put(a, SB_COLS, "lamre", np.concatenate([lr, lr], 0))
    _put(a, SB_COLS, "lamim", np.concatenate([li, li], 0))
    _put(a, SB_COLS, "logdt", np.broadcast_to(log_dt[None, :], (128, 32)))
    _put(a, SB_COLS, "drep", np.tile(d_skip.T, (8, 1)))
    bre, bim = b_re.transpose(1, 0, 2), b_im.transpose(1, 0, 2)
    _put(a, SB_COLS, "b1", np.concatenate([bre, bim], 0))
    _put(a, SB_COLS, "b2raw", np.concatenate([bim, bre], 0))
    cre, cim = c_re.transpose(2, 0, 1), c_im.transpose(2, 0, 1)
    _put(a, SB_COLS, "c1raw", np.concatenate([cre, cim], 0))
    _put(a, SB_COLS, "c2", np.concatenate([cim, cre], 0))
    return a


def build_bias_tables(rel_bias):
    out = np.full((6, 128, 2, 512), NEG, np.float32)
    j = np.arange(128)[:, None]
    a = np.arange(128)[None, :]
    for hp in range(2):
        for g in range(3):
            rnd = hp * 3 + g
            for h2 in range(2):
                head = g * 4 + hp * 2 + h2
                rel_prev = a + 128 - j
                rel_cur = a - j
                for kb, rel in ((0, rel_prev), (1, rel_cur)):
                    valid = (rel >= 0) & (rel <= 128)
                    idx = _t5_bucket_np(np.maximum(rel, 0) * DIL[g])
                    vals = np.where(valid, rel_bias[idx, head], np.float32(NEG))
                    c0 = h2 * 256 + kb * 128
                    out[rnd, :, 0, c0:c0 + 128] = vals
                    if kb == 1:
                        out[rnd, :, 1, c0:c0 + 128] = vals
    return out


KB = 1024


def build_program(debug=(), stages=("A", "U", "ATT", "SSM", "P3")):
    nc = bass.Bass("TRN2", target_bir_lowering=False)
    es = ExitStack()
    P = Prog()

    def dram_in(name, shape, dt=F32):
        return nc.dram_tensor(name, list(shape), dt, kind="ExternalInput").ap()

    x_d = dram_in("x", [L, D])
    w_in_d = dram_in("w_in", [D, INW])
    smP_d = dram_in("smalls_p", [128, SP_N])
    smB_d = dram_in("smalls_b", [128, SB_N])
    w_glu_d = dram_in("w_glu", [512, 1024])
    biasT_d = dram_in("bias_t", [6, 128, 2, 512])
    wsp_d = dram_in("w_ssm_proj", [512, D])
    wap_d = dram_in("w_attn_proj", [256, D])
    wout_d = dram_in("w_out", [D, D])
    wup_d = dram_in("w_up", [D, DFF])
    wdn_d = dram_in("w_down", [DFF, D])
    lnrows_d = dram_in("ln_rows", [4, 128, D])
    out_d = nc.dram_tensor("out", [L, D], F32, kind="ExternalOutput").ap()
    dbg = {}

    ar = Arena(nc, es, 206 * KB)
    ps_all = es.enter_context(nc.psum_tensor("ps", [128, 4096], F32))
    banks = [Buf(ps_all[:, b * 512:(b + 1) * 512], excl=True) for b in range(8)]

    def bank_bf(b):
        return banks[b].ap.bitcast(BF16)

    def R(*bufs):
        out = []
        for b in bufs:
            if isinstance(b, Buf):
                out.extend(b.res)
            elif isinstance(b, Res):
                out.append(b)
            else:
                out.extend(R(*b))
        return out

    def dma(eng, out, in_, reads, writes):
        return P.add(eng, lambda e: e.dma_start(out=out, in_=in_), R(*reads), R(*writes), dma=True)

    def mm(out, lhsT, rhs, start, stop, reads, writes, **kw):
        return P.add("pe", lambda e: e.matmul(out, lhsT, rhs, start=start, stop=stop, **kw),
                     R(*reads), R(*writes))

    def tr(out, in_, ident, reads, writes):
        return P.add("pe", lambda e: e.transpose(out, in_, ident), R(*reads), R(*writes))

    def act(out, in_, func, reads, writes, bias=0.0, scale=1.0):
        return P.add("act", lambda e: e.activation(out, in_, func, bias=bias, scale=scale),
                     R(*reads), R(*writes))

    def tt(eng, out, in0, in1, op, reads, writes):
        return P.add(eng, lambda e: e.tensor_tensor(out, in0, in1, op), R(*reads), R(*writes))

    def ts(eng, out, in0, s1, s2, op0, op1, reads, writes):
        if s2 is None:
            return P.add(eng, lambda e: e.tensor_scalar(out, in0, s1, None, op0), R(*reads), R(*writes))
        return P.add(eng, lambda e: e.tensor_scalar(out, in0, s1, s2, op0, op1), R(*reads), R(*writes))

    def stt(out, in0, scalar, in1, op0, op1, reads, writes):
        return P.add("dve", lambda e: e.scalar_tensor_tensor(out, in0, scalar, in1, op0, op1),
                     R(*reads), R(*writes))

    def cp(eng, out, in_, reads, writes):
        if eng == "act":
            return P.add("act", lambda e: e.copy(out, in_), R(*reads), R(*writes))
        return P.add(eng, lambda e: e.tensor_copy(out, in_), R(*reads), R(*writes))

    def scan(dst, d0, src, extra_reads):
        return P.add("dve", lambda e: e.tensor_tensor_scan(dst.ap, d0, src.ap, 0.0, ALU.mult, ALU.add),
                     R(src, *extra_reads), R(dst))

    def memset(eng, out, val, writes):
        return P.add(eng, lambda e: e.memset(out, val), (), R(*writes))

    def dump(name, buf_ap, shape, dt, reads):
        d = nc.dram_tensor("dbg_" + name, list(shape), dt, kind="ExternalOutput").ap()
        dbg[name] = d
        dma("sp", d, buf_ap, reads, [])

    ar.seek(0, 8 * KB)
    smP = ar.alloc([SP_N], F32)
    dma("sp", smP.ap, smP_d, [], [smP])

    def sp_(name):
        o, n = SP_COLS[name]
        return smP.ap[:, o:o + n]

    ident_f = sp_("ident")
    identb = ar.alloc([128], BF16)
    cp("dve", identb.ap, ident_f, [smP], [identb])
    onesAB = ar.alloc([2, 128], BF16)
    memset("pool", onesAB.ap, 0.0, [onesAB])
    memset("pool", onesAB.ap[:, 0, 0:64], 1.0, [onesAB])
    memset("pool", onesAB.ap[:, 1, 64:128], 1.0, [onesAB])

    rp = ar.alloc([32], F32)
    f8p = ar.alloc([32], F32)
    e_dt = ar.alloc([32], F32)
    e_a = ar.alloc([32], F32)
    e_b = ar.alloc([32], F32)
    act(e_dt.ap, sp_("logdt"), AF.Exp, [smP], [e_dt])
    tt("dve", e_a.ap, sp_("lamre"), e_dt.ap, ALU.mult, [smP, e_dt], [e_a])
    act(rp.ap, e_a.ap, AF.Exp, [e_a], [rp], scale=8.0)
    tt("dve", e_a.ap, sp_("lamim"), e_dt.ap, ALU.mult, [smP, e_dt], [e_a])
    ts("dve", e_a.ap, e_a.ap, 8.0, 1.0 / (2.0 * math.pi), ALU.mult, ALU.mult, [e_a], [e_a])
    ts("dve", e_b.ap, e_a.ap, MAGIC, -MAGIC, ALU.add, ALU.add, [e_a], [e_b])
    tt("dve", e_b.ap, e_a.ap, e_b.ap, ALU.subtract, [e_a, e_b], [e_b])
    stt(f8p.ap, e_b.ap, 0.0, e_b.ap, ALU.is_lt, ALU.add, [e_b], [f8p])
    tabs_d = nc.dram_tensor("tabs_scratch", [32, 128, 2, 512], F32, kind="Internal").ap()
    tabs_res = Buf(None, 16)
    xTs_d = nc.dram_tensor("xT_scratch", [128, 8, L], BF16, kind="Internal").ap()
    xTs_res = Buf(None, 4)

    ar.seek(8 * KB, 72 * KB)
    xT = ar.alloc([8, L], BF16, nres=32)
    ar.seek(72 * KB, 104 * KB)
    Ug = ar.alloc([32, 512], BF16, nres=32)
    ar.seek(104 * KB, 120 * KB)
    yattnT = ar.alloc([2, L], BF16, nres=2)
    T0, T1 = 120 * KB, 206 * KB

    ar.seek(160 * KB, T1)
    xs = [ar.alloc([D], F32) for _ in range(3)]
    pi = 0
    for tt_i in range(32):
        xb = xs[tt_i % 3]
        dma("sp", xb.ap, x_d[tt_i * 128:(tt_i + 1) * 128, :], [], [xb])
        for half in range(2):
            bk = banks[pi % 2]
            pi += 1
            for j in range(4):
                kc = half * 4 + j
                tr(bk.ap[:, j * 128:(j + 1) * 128], xb.ap[:, kc * 128:(kc + 1) * 128], ident_f,
                   [xb, smP], [bk])
            dst = xT.ap[:, half * 4:half * 4 + 4, tt_i * 128:(tt_i + 1) * 128]
            src = bk.ap.rearrange("p (j t) -> p j t", j=4)
            cp("dve" if half == 0 else "act", dst, src, [bk], [xT.res[tt_i]])
        if tt_i % 8 == 7:
            b4 = tt_i // 8
            dma("sp", xTs_d[:, :, b4 * 1024:(b4 + 1) * 1024], xT.ap[:, :, b4 * 1024:(b4 + 1) * 1024],
                [xT.res[b4 * 8 + i] for i in range(8)], [xTs_res.res[b4]])
    if "xT" in debug:
        dump("xT", xT.ap, [128, 8, L], BF16, [xT])

    if "U" in stages:
        ar.seek(T0, T1)
        wu = ar.alloc([8, 512], BF16)
        dma("pool", wu.ap, w_in_d[:, 2304:2816].rearrange("(kc p) j -> p kc j", p=128), [], [wu])
        Uc = ar.alloc([4, 32, 128], BF16, nres=4)
        xT5 = xT.ap.rearrange("p k (b c s) -> p k b c s", b=4, s=8)
        ar.seek(172 * KB, T1)
        gcs = [ar.alloc([2, 2, 512], F32) for _ in range(2)]
        gAs = [ar.alloc([2, 512], F32) for _ in range(2)]
        gFs = [ar.alloc([2, 512], F32) for _ in range(2)]
        iota = sp_("iota")

        def gen_tables(bt):
            cs_, tA, tF = gcs[bt % 2], gAs[bt % 2], gFs[bt % 2]
            tt("dve", tA.ap, f8p.ap[:, 2 * bt:2 * bt + 2, None].to_broadcast([128, 2, 512]),
               iota[:, None, :].to_broadcast([128, 2, 512]), ALU.mult, [f8p, smP], [tA])
            ts("dve", tF.ap, tA.ap, MAGIC, -MAGIC, ALU.add, ALU.add, [tA], [tF])
            tt("dve", tF.ap, tA.ap, tF.ap, ALU.subtract, [tA, tF], [tF])
            act(cs_.ap[:, :, 1, :], tF.ap, AF.Sin, [tF], [cs_], scale=TWO_PI)
            act(cs_.ap[:, :, 0, :], tF.ap, AF.Sin, [tF], [cs_], scale=TWO_PI / 2.0)
            act(cs_.ap[:, :, 0, :], cs_.ap[:, :, 0, :], AF.Square, [cs_], [cs_])
            ts("pool", cs_.ap[:, :, 0, :], cs_.ap[:, :, 0, :], -2.0, 1.0, ALU.mult, ALU.add, [cs_], [cs_])
            dma("sp", tabs_d[2 * bt:2 * bt + 2].rearrange("g p t c -> p g t c"), cs_.ap, [cs_],
                [tabs_res.res[bt]])

        for blk in range(4):
            for s in range(8):
                if s % 2 == 0:
                    gen_tables(blk * 4 + s // 2)
                bk = banks[pi % 2]
                pi += 1
                for kc in range(8):
                    mm(bk.ap, xT5[:, kc, blk, :, s], wu.ap[:, kc, :], kc == 0, kc == 7,
                       [xT.res[blk * 8 + i] for i in range(8)] + [wu], [bk])
                cp("act" if s % 2 else "dve", Uc.ap[:, blk, :, s * 16:(s + 1) * 16],
                   bk.ap.rearrange("p (g h) -> p g h", g=32), [bk], [Uc.res[blk]])
        for g in range(32):
            b = 2 + pi % 2
            pi += 1
            bkb = bank_bf(b)
            for blk in range(4):
                tr(bkb[:, blk * 128:(blk + 1) * 128], Uc.ap[:, blk, g, :], identb.ap,
                   [Uc.res[blk], identb], [banks[b]])
            cp("act" if g % 2 else "dve", Ug.ap[:, g, :], bkb[:, 0:512], [banks[b]], [Ug.res[g]])
        if "Ug" in debug:
            dump("Ug", Ug.ap, [128, 32, 512], BF16, [Ug])

    if "ATT" in stages:
        ar.seek(T0, T1)
        biasb = [ar.alloc([2, 512], BF16) for _ in range(1)]
        wq = [ar.alloc([8, 384], BF16) for _ in range(1)]
        qT = ar.alloc([2, L], BF16)
        kT = ar.alloc([L], BF16)
        memset("dve", qT.ap, 0.0, [qT])
        V = ar.alloc([32, 2, 128], BF16, nres=32)
        acc = ar.alloc([2, L], F32)
        PT = [ar.alloc([512], BF16) for _ in range(2)]
        memset("dve", V.ap, 0.0, [V])
        pti = 0
        si = 0
        for hp in range(2):
            for g in range(3):
                rnd = hp * 3 + g
                dil = DIL[g]
                nblk = 32 // dil
                bb = biasb[0]
                wb = wq[0]
                dma("pool", bb.ap, biasT_d[rnd], [], [bb])
                for j, c0 in enumerate((0, 768, 1536)):
                    col = c0 + g * 256 + hp * 128
                    dma("pool", wb.ap[:, :, j * 128:(j + 1) * 128],
                        w_in_d[:, col:col + 128].rearrange("(kc p) j -> p kc j", p=128), [], [wb])
                for which, dstT in ((0, qT), (1, kT)):
                    for nt in range(8):
                        bk = banks[pi % 2]
                        pi += 1
                        for kc in range(8):
                            mm(bk.ap, wb.ap[:, kc, which * 128:(which + 1) * 128],
                               xT.ap[:, kc, nt * 512:(nt + 1) * 512], kc == 0, kc == 7,
                               [wb] + [xT.res[nt * 4 + i] for i in range(4)], [bk])
                        cs_ = slice(nt * 512, (nt + 1) * 512)
                        if which == 0:
                            ts("dve", qT.ap[0:64, 0, cs_], bk.ap[0:64, :], 0.125, None, ALU.mult, None, [bk], [qT])
                            ts("dve", qT.ap[64:128, 1, cs_], bk.ap[64:128, :], 0.125, None, ALU.mult, None,
                               [bk], [qT])
                        else:
                            cp("dve", kT.ap[:, cs_], bk.ap, [bk], [kT])
                xTd = xT.ap.rearrange("p k (i a d) -> p k d i a", a=128, d=dil)
                for t4 in range(8):
                    b = 6 + t4 % 2
                    bk = banks[b]
                    for j in range(4):
                        tile = t4 * 4 + j
                        r, i = tile // nblk, tile % nblk
                        for kc in range(8):
                            mm(bk.ap[:, j * 128:(j + 1) * 128], xTd[:, kc, r, i, :], wb.ap[:, kc, 256:384],
                               kc == 0, kc == 7, [xT, wb], [bk])
                    src = bk.ap.rearrange("p (t c) -> p t c", t=4)
                    cp("dve", V.ap[:, t4 * 4:t4 * 4 + 4, 0, 0:64], src[:, :, 0:64], [bk],
                       [V.res[t4 * 4 + j] for j in range(4)])
                    cp("dve", V.ap[:, t4 * 4:t4 * 4 + 4, 1, 64:128], src[:, :, 64:128], [bk],
                       [V.res[t4 * 4 + j] for j in range(4)])
                qTd = qT.ap.rearrange("p h (i a d) -> p h d i a", a=128, d=dil)
                kTd = kT.ap.rearrange("p (i a d) -> p d i a", a=128, d=dil)
                accd = acc.ap.rearrange("p c (i a d) -> p c d i a", a=128, d=dil)
                tiles = [(r, i) for r in range(dil) for i in range(nblk)]

                def s_part(r, i, sb, pt):
                    first = (i == 0)
                    mm(sb.ap, identb.ap, bb.ap[:, 1 if first else 0, :], True, False, [identb, bb], [sb])
                    combos = [(h2, kb) for h2 in range(2) for kb in range(2) if not (first and kb == 0)]
                    for n_, (h2, kb) in enumerate(combos):
                        c0 = h2 * 256 + kb * 128
                        mm(sb.ap[:, c0:c0 + 128], kTd[:, r, i - 1 + kb, :], qTd[:, h2, r, i, :],
                           False, n_ == len(combos) - 1, [kT, qT], [sb])
                    act(pt.ap, sb.ap, AF.Exp, [sb], [pt])

                def pv_part(r, i, pb, pt):
                    first = (i == 0)
                    combos = [(h2, kb) for h2 in range(2) for kb in range(2) if not (first and kb == 0)]
                    for part in range(2):
                        for n_, (h2, kb) in enumerate(combos):
                            tl = r * nblk + i - 1 + kb
                            c0 = h2 * 256 + kb * 128
                            lhs = V.ap[:, tl, h2, :] if part == 0 else onesAB.ap[:, h2, :]
                            mm(pb.ap[:, part * 128:(part + 1) * 128], lhs, pt.ap[:, c0:c0 + 128],
                               n_ == 0, n_ == len(combos) - 1, [V.res[tl], onesAB, pt], [pb])
                    dsta = accd[:, :, r, i, :]
                    src = pb.ap[:, 0:256].rearrange("p (c a) -> p c a", c=2)
                    if g == 0:
                        cp("dve", dsta, src, [pb], [acc])
                    else:
                        tt("dve", dsta, dsta, src, ALU.add, [acc, pb], [acc])

                prev = None
                for (r, i) in tiles:
                    sb = banks[2 + si % 2]
                    pb = banks[4 + si % 2]
                    pt = PT[si % 2]
                    si += 1
                    s_part(r, i, sb, pt)
                    if prev is not None:
                        pv_part(*prev)
                    prev = (r, i, pb, pt)
                pv_part(*prev)
            P.add("dve", lambda e: e.reciprocal(acc.ap[:, 1, :], acc.ap[:, 1, :]), R(acc), R(acc))
            tt("dve", yattnT.ap[:, hp, :], acc.ap[:, 0, :], acc.ap[:, 1, :], ALU.mult, [acc], [yattnT.res[hp]])
        if "yattnT" in debug:
            dump("yattnT", yattnT.ap, [128, 2, L], BF16, [yattnT])

    if "SSM" in stages:
        ar.seek(8 * KB, 40 * KB)
        Yc = ar.alloc([4, 8, 512], BF16, nres=4)
        ar.seek(T0, 161 * KB)
        MT = ar.alloc([32, 128], BF16)
        L1b = ar.alloc([32, 128], BF16)
        L2b = ar.alloc([32, 128], BF16)
        WS1T = ar.alloc([32, 128], BF16)
        WS2T = ar.alloc([32, 128], BF16)
        ar.seek(161 * KB, T1)
        dt_ = ar.alloc([32], F32)
        lrdt = ar.alloc([32], F32)
        th = ar.alloc([32], F32)
        den = ar.alloc([32], F32)
        t32a = ar.alloc([32], F32)
        t32b = ar.alloc([32], F32)
        kre = ar.alloc([32], F32)
        kim = ar.alloc([32], F32)
        argm = ar.alloc([32, 16], F32)
        magp = ar.alloc([32, 16], F32)
        magn = ar.alloc([32, 16], F32)
        turns = ar.alloc([32, 16], F32)
        ti = ar.alloc([32, 16], I32)
        tf = argm
        fr = ar.alloc([32, 16], F32)
        msf = ar.alloc([32, 16], F32)
        frc = turns
        msc = msf
        c16 = ar.alloc([32, 16], F32)
        s16 = ar.alloc([32, 16], F32)
        Pc = ar.alloc([32, 16], F32)
        Ps = ar.alloc([32, 16], F32)
        Nc = ar.alloc([32, 16], F32)
        Ns = ar.alloc([32, 16], F32)
        b2 = c16
        c1 = s16
        Xb = ar.alloc([32, 16], F32)
        Yb = ar.alloc([32, 16], F32)
        tx = magn
        ar.seek(40 * KB, 72 * KB)
        smB = ar.alloc([SB_N], F32)
        dma("sp", smB.ap, smB_d, [], [smB])

        def sb_(name):
            o, n = SB_COLS[name]
            return smB.ap[:, o:o + n]

        E = "dve"
        lamre, lamim = sb_("lamre"), sb_("lamim")
        act(dt_.ap, sb_("logdt"), AF.Exp, [smB], [dt_])
        tt(E, lrdt.ap, lamre, dt_.ap, ALU.mult, [smB, dt_], [lrdt])
        tt(E, th.ap, lamim, dt_.ap, ALU.mult, [smB, dt_], [th])
        tt(E, t32a.ap, lamre, lamre, ALU.mult, [smB], [t32a])
        tt(E, t32b.ap, lamim, lamim, ALU.mult, [smB], [t32b])
        tt(E, den.ap, t32a.ap, t32b.ap, ALU.add, [t32a, t32b], [den])
        P.add("dve", lambda e: e.reciprocal(den.ap, den.ap), R(den), R(den))
        kv = sp_("kvals")
        kvb = kv[:, None, :].to_broadcast([128, 32, 16])
        tt(E, argm.ap, lrdt.ap[:, :, None].to_broadcast([128, 32, 16]), kvb, ALU.mult, [lrdt, smP], [argm])
        act(magp.ap, argm.ap, AF.Exp, [argm], [magp])
        act(magn.ap, argm.ap, AF.Exp, [argm], [magn], scale=-1.0)
        tt(E, turns.ap, th.ap[:, :, None].to_broadcast([128, 32, 16]), kvb, ALU.mult, [th, smP], [turns])
        ts(E, turns.ap, turns.ap, 1.0 / (2.0 * math.pi), None, ALU.mult, None, [turns], [turns])
        ts(E, tf.ap, turns.ap, MAGIC, -MAGIC, ALU.add, ALU.add, [turns], [tf])
        tt(E, fr.ap, turns.ap, tf.ap, ALU.subtract, [turns, tf], [fr])
        stt(msf.ap, fr.ap, 0.5, fr.ap, ALU.is_gt, ALU.subtract, [fr], [msf])
        act(s16.ap, msf.ap, AF.Sin, [msf], [s16], scale=-TWO_PI)
        ts(E, frc.ap, fr.ap, 0.25, None, ALU.add, None, [fr], [frc])
        stt(msc.ap, frc.ap, 0.5, frc.ap, ALU.is_gt, ALU.subtract, [frc], [msc])
        act(c16.ap, msc.ap, AF.Sin, [msc], [c16], scale=-TWO_PI)
        tt(E, Pc.ap, magp.ap, c16.ap, ALU.mult, [magp, c16], [Pc])
        tt(E, Ps.ap, magp.ap, s16.ap, ALU.mult, [magp, s16], [Ps])
        tt(E, Nc.ap, magn.ap, c16.ap, ALU.mult, [magn, c16], [Nc])
        tt(E, Ns.ap, magn.ap, s16.ap, ALU.mult, [magn, s16], [Ns])
        abre, abim = Pc.ap[:, :, 0], Ps.ap[:, :, 0]
        nr = t32a
        ts(E, nr.ap, abre, -1.0, None, ALU.add, None, [Pc], [nr])
        tt(E, t32b.ap, nr.ap, lamre, ALU.mult, [nr, smB], [t32b])
        tt(E, kre.ap, abim, lamim, ALU.mult, [Ps, smB], [kre])
        tt(E, kre.ap, kre.ap, t32b.ap, ALU.add, [kre, t32b], [kre])
        tt(E, kre.ap, kre.ap, den.ap, ALU.mult, [kre, den], [kre])
        tt(E, t32b.ap, nr.ap, lamim, ALU.mult, [nr, smB], [t32b])
        tt(E, kim.ap, abim, lamre, ALU.mult, [Ps, smB], [kim])
        tt(E, kim.ap, kim.ap, t32b.ap, ALU.subtract, [kim, t32b], [kim])
        tt(E, kim.ap, kim.ap, den.ap, ALU.mult, [kim, den], [kim])
        sgn = sp_("sgn")
        b1 = sb_("b1").rearrange("p (g h) -> p g h", g=32)
        b2raw = sb_("b2raw").rearrange("p (g h) -> p g h", g=32)
        c1raw = sb_("c1raw").rearrange("p (g h) -> p g h", g=32)
        c2 = sb_("c2").rearrange("p (g h) -> p g h", g=32)
        ts(E, b2.ap, b2raw, sgn, None, ALU.mult, None, [smB, smP], [b2])
        ts(E, c1.ap, c1raw, sgn, None, ALU.mult, None, [smB, smP], [c1])
        kreb = kre.ap[:, :, None].to_broadcast([128, 32, 16])
        kimb = kim.ap[:, :, None].to_broadcast([128, 32, 16])
        tt(E, Xb.ap, kreb, b1, ALU.mult, [kre, smB], [Xb])
        tt(E, tx.ap, kimb, b2.ap, ALU.mult, [kim, b2], [tx])
        tt(E, Xb.ap, Xb.ap, tx.ap, ALU.subtract, [Xb, tx], [Xb])
        tt(E, Yb.ap, kreb, b2.ap, ALU.mult, [kre, b2], [Yb])
        tt(E, tx.ap, kimb, b1, ALU.mult, [kim, smB], [tx])
        tt(E, Yb.ap, Yb.ap, tx.ap, ALU.add, [Yb, tx], [Yb])

        GB = 8
        AB1f = ar.alloc([GB, 8, 16], F32)
        L1f = ar.alloc([GB, 8, 16], F32)
        WSf = ar.alloc([GB, 8, 16], F32)
        mtmp = [ar.alloc([128], F32) for _ in range(2)]
        bigAs = [ar.alloc([GB, 8, 16], F32)]
        bigBs = [ar.alloc([GB, 8, 16], F32)]
        ar.seek(192 * KB, T1)
        bigAs.append(ar.alloc([GB, 8, 16], F32))
        bigBs.append(ar.alloc([GB, 8, 16], F32))
        obi = [0]

        def outer2(tabA, kA, vecA, tabB, kB, vecB, g0, readsA, readsB):
            bigA, bigB = bigAs[obi[0] % 2], bigBs[obi[0] % 2]
            obi[0] += 1
            for dst, tab, k0, vec, eng, rd in ((bigA, tabA, kA, vecA, "dve", readsA),
                                               (bigB, tabB, kB, vecB, "pool", readsB)):
                a = tab[:, g0:g0 + GB, k0:k0 + 8][:, :, :, None].to_broadcast([128, GB, 8, 16])
                b = vec[:, g0:g0 + GB, None, :].to_broadcast([128, GB, 8, 16])
                tt(eng, dst.ap, a, b, ALU.mult, rd, [dst])
            return bigA, bigB

        pbi = 0
        g3 = "p g s h -> p g (s h)"
        for gb in range(32 // GB):
            g0 = gb * GB
            bigA, bigB = outer2(Nc.ap, 0, Xb.ap, Ns.ap, 0, Yb.ap, g0, [Nc, Xb], [Ns, Yb])
            tt("dve", AB1f.ap, bigA.ap, bigB.ap, ALU.add, [bigA, bigB], [AB1f])
            bigA, bigB = outer2(Pc.ap, 0, c1.ap, Ps.ap, 0, c2, g0, [Pc, c1], [Ps, smB])
            tt("dve", L1f.ap, bigA.ap, bigB.ap, ALU.subtract, [bigA, bigB], [L1f])
            cp("act", L1b.ap[:, g0:g0 + GB, :], L1f.ap.rearrange(g3), [L1f], [L1b])
            for g in range(GB):
                bk = banks[pbi % 2]
                mt = mtmp[pbi % 2]
                pbi += 1
                mm(bk.ap[:, 0:128], AB1f.ap[:, g].rearrange("p s h -> p (s h)"),
                   L1f.ap[:, g].rearrange("p s h -> p (s h)"), True, True, [AB1f, L1f], [bk])
                tt("dve", mt.ap, bk.ap[:, 0:128], sp_("maskM"), ALU.mult, [bk, smP], [mt])
                o_, _n = SB_COLS["drep"]
                stt(MT.ap[:, g0 + g, :], ident_f, smB.ap[:, o_ + g0 + g:o_ + g0 + g + 1], mt.ap,
                    ALU.mult, ALU.add, [smP, smB, mt], [MT])
            bigA, bigB = outer2(Ps.ap, 0, c1.ap, Pc.ap, 0, c2, g0, [Ps, c1], [Pc, smB])
            stt(L2b.ap[:, g0:g0 + GB, :], bigA.ap.rearrange(g3), -1.0, bigB.ap.rearrange(g3),
                ALU.mult, ALU.subtract, [bigA, bigB], [L2b])
            for which in range(2):
                if which == 0:
                    bigA, bigB = outer2(Pc.ap, 8, Xb.ap, Ps.ap, 8, Yb.ap, g0, [Pc, Xb], [Ps, Yb])
                    tt("dve", WSf.ap, bigA.ap, bigB.ap, ALU.subtract, [bigA, bigB], [WSf])
                else:
                    bigA, bigB = outer2(Ps.ap, 8, Xb.ap, Pc.ap, 8, Yb.ap, g0, [Ps, Xb], [Pc, Yb])
                    tt("dve", WSf.ap, bigA.ap, bigB.ap, ALU.add, [bigA, bigB], [WSf])
                dstT = WS1T if which == 0 else WS2T
                for g4 in range(GB // 4):
                    bk = banks[pbi % 2]
                    pbi += 1
                    for j in range(4):
                        g = g4 * 4 + j
                        tr(bk.ap[:, j * 128:(j + 1) * 128], WSf.ap[:, g].rearrange("p s h -> p (s h)"),
                           ident_f, [WSf, smP], [bk])
                    cp("act", dstT.ap[:, g0 + g4 * 4:g0 + g4 * 4 + 4, :],
                       bk.ap.rearrange("p (j q) -> p j q", j=4), [bk], [dstT])
        if "ssmw" in debug:
            dump("MT", MT.ap, [128, 32, 128], BF16, [MT])
            dump("L1b", L1b.ap, [128, 32, 128], BF16, [L1b])
            dump("L2b", L2b.ap, [128, 32, 128], BF16, [L2b])
            dump("WS1T", WS1T.ap, [128, 32, 128], BF16, [WS1T])
            dump("WS2T", WS2T.ap, [128, 32, 128], BF16, [WS2T])
            dump("rp", rp.ap, [128, 32], F32, [rp])
            dump("f8p", f8p.ap, [128, 32], F32, [f8p])

        ar.seek(161 * KB, T1)
        tabb = [ar.alloc([2, 512], F32) for _ in range(3)]
        ar.seek(173 * KB, T1)
        YT = ar.alloc([4, L], BF16)
        ar.seek(40 * KB, 72 * KB)
        t1s = [ar.alloc([512], F32) for _ in range(2)]
        t2s = [ar.alloc([512], F32) for _ in range(2)]
        cms = [ar.alloc([512], F32) for _ in range(2)]
        wss = [ar.alloc([512], F32) for _ in range(2)]
        V1s = [ar.alloc([520], BF16) for _ in range(2)]
        V2s = [ar.alloc([520], BF16) for _ in range(2)]
        for v in V1s + V2s:
            memset("pool", v.ap[:, 0:8], 0.0, [v])
        zb = [banks[0], banks[1], banks[2], banks[3]]
        yb = [banks[4], banks[5]]

        def load_tab(g):
            dma("sp", tabb[g % 3].ap, tabs_d[g], [tabs_res.res[g // 2]], [tabb[g % 3]])

        def z_part(g):
            z1, z2 = zb[(g * 2) % 4], zb[(g * 2 + 1) % 4]
            mm(z1.ap, WS1T.ap[:, g, :], Ug.ap[:, g, :], True, True, [WS1T, Ug.res[g]], [z1])
            mm(z2.ap, WS2T.ap[:, g, :], Ug.ap[:, g, :], True, True, [WS2T, Ug.res[g]], [z2])

        for g in range(2):
            load_tab(g)
        wg = ar.alloc([4, 1024], BF16)
        dma("pool", wg.ap, w_glu_d.rearrange("(kc p) j -> p kc j", p=128), [], [wg])
        tpi = [0]

        def yt_transposes(j):
            for cb in range(4):
                b = 6 + tpi[0] % 2
                tpi[0] += 1
                bkb = bank_bf(b)
                for t in range(8):
                    tr(bkb[:, t * 128:(t + 1) * 128], Yc.ap[:, cb, t, j * 128:(j + 1) * 128], identb.ap,
                       [Yc.res[cb], identb], [banks[b]])
                dst = YT.ap[:, j, cb * 1024:(cb + 1) * 1024].rearrange("p (c t) -> p t c", t=8)
                cp("act", dst, bkb.rearrange("p (t c) -> p t c", t=8), [banks[b]], [YT])

        z_part(0)
        for g in range(32):
            if g + 2 < 32:
                load_tab(g + 2)
            if g + 1 < 32:
                z_part(g + 1)
            cosT, sinT = tabb[g % 3].ap[:, 0, :], tabb[g % 3].ap[:, 1, :]
            cosR = sinR = tabb[g % 3]
            z1, z2 = zb[(g * 2) % 4], zb[(g * 2 + 1) % 4]
            t1, t2, cm, ws = t1s[g % 2], t2s[g % 2], cms[g % 2], wss[g % 2]
            v1, v2 = V1s[g % 2], V2s[g % 2]
            tt("dve", t1.ap, z1.ap, cosT, ALU.mult, [z1, cosR], [t1])
            tt("dve", t2.ap, z2.ap, sinT, ALU.mult, [z2, sinR], [t2])
            tt("dve", cm.ap, t1.ap, t2.ap, ALU.add, [t1, t2], [cm])
            scan(ws, rp.ap[:, g:g + 1].to_broadcast([128, 512]), cm, [rp])
            tt("pool", v1.ap[:, 8:520], ws.ap, cosT, ALU.mult, [ws, cosR], [v1])
            tt("pool", v2.ap[:, 8:520], ws.ap, sinT, ALU.mult, [ws, sinR], [v2])
            for cb in range(4):
                ybk = yb[cb // 2]
                c0 = (cb % 2) * 256 + (g % 2) * 128
                o = ybk.ap[:, c0:c0 + 128]
                mm(o, Ug.ap[:, g, cb * 128:(cb + 1) * 128], MT.ap[:, g, :], True, False,
                   [Ug.res[g], MT], [ybk])
                mm(o, v1.ap[:, 7 + cb * 128:7 + (cb + 1) * 128], L1b.ap[:, g, :], False, False,
                   [v1, L1b], [ybk])
                mm(o, v2.ap[:, 7 + cb * 128:7 + (cb + 1) * 128], L2b.ap[:, g, :], False, True,
                   [v2, L2b], [ybk])
            if g % 2 == 1:
                g2 = g - 1
                for cb in range(4):
                    ybk = yb[cb // 2]
                    src = ybk.ap[:, (cb % 2) * 256:(cb % 2) * 256 + 256]
                    dst = Yc.ap[:, cb, :, g2 * 16:(g2 + 2) * 16].rearrange("p t (g h) -> p g t h", g=2)
                    act(dst, src.rearrange("p (g t h) -> p g t h", g=2, t=8), AF.Gelu_apprx_tanh,
                        [ybk], [Yc.res[cb]])
            if g % 8 == 7:
                yt_transposes(g // 8)
        if "Yc" in debug:
            dump("Yc", Yc.ap, [128, 4, 8, 512], BF16, [Yc])
        ar.seek(8 * KB, 40 * KB)
        Y2T = ar.alloc([4, L], BF16)
        ar.seek(T0, 161 * KB)
        sgs = [ar.alloc([512], F32) for _ in range(4)]
        gi_ = 0
        for m in range(4):
            for nt in range(8):
                ba, bb_ = banks[(gi_ * 2) % 8], banks[(gi_ * 2 + 1) % 8]
                sgb = sgs[gi_ % 4]
                gi_ += 1
                for kc in range(4):
                    mm(ba.ap, wg.ap[:, kc, m * 128:(m + 1) * 128], YT.ap[:, kc, nt * 512:(nt + 1) * 512],
                       kc == 0, kc == 3, [wg, YT], [ba])
                for kc in range(4):
                    mm(bb_.ap, wg.ap[:, kc, 512 + m * 128:512 + (m + 1) * 128],
                       YT.ap[:, kc, nt * 512:(nt + 1) * 512], kc == 0, kc == 3, [wg, YT], [bb_])
                act(sgb.ap, bb_.ap, AF.Sigmoid, [bb_], [sgb])
                tt("dve", Y2T.ap[:, m, nt * 512:(nt + 1) * 512], ba.ap, sgb.ap, ALU.mult, [ba, sgb], [Y2T])
        if "Y2T" in debug:
            dump("Y2T", Y2T.ap, [128, 4, L], BF16, [Y2T])

    if "P3" in stages:
        ar.seek(40 * KB, 104 * KB)
        xs3 = [ar.alloc([D], F32) for _ in range(8)]
        ring = [ar.alloc([4096], BF16) for _ in range(4)]
        ar.seek(T0, T1)
        lnr = [ar.alloc([D], F32) for _ in range(2)]
        xTb = ar.alloc([8, 1024], BF16, nres=8)
        aT = ar.alloc([16, 1024], BF16)
        mixin = Buf(aT.ap[:, 0:8, :], 1)
        mixin.res = aT.res
        wsp = ar.alloc([4, 1024], BF16)
        wap = ar.alloc([2, 1024], BF16)
        dma("pool", wsp.ap, wsp_d.rearrange("(kc p) j -> p kc j", p=128), [], [wsp])
        dma("pool", wap.ap, wap_d.rearrange("(kc p) j -> p kc j", p=128), [], [wap])
        gsb = [ar.alloc([512], F32) for _ in range(2)]
        gab = [ar.alloc([512], F32) for _ in range(2)]
        tm1 = [ar.alloc([512], F32) for _ in range(2)]
        tm2 = [ar.alloc([512], F32) for _ in range(2)]
        sqb = tm1
        st6 = ar.alloc([12], F32)
        mv = ar.alloc([2], F32)
        rstd = ar.alloc([1], F32)
        bgate = sp_("bgate")
        ri = 0
        bi = 0

        def nb():
            nonlocal bi
            b = banks[bi % 8]
            bi += 1
            return b

        def load_piece(src_aps):
            nonlocal ri
            slot = ring[ri % 4]
            ri += 1
            for (v, src) in src_aps:
                dma("pool", v(slot), src, [], [slot])
            return slot

        def layer_norm(xb, which):
            P.add("dve", lambda e: e.bn_stats(st6.ap[:, 0:6], xb.ap[:, 0:512]), R(xb), R(st6))
            P.add("dve", lambda e: e.bn_stats(st6.ap[:, 6:12], xb.ap[:, 512:1024]), R(xb, st6), R(st6))
            P.add("dve", lambda e: e.bn_aggr(mv.ap, st6.ap), R(st6), R(mv))
            act(rstd.ap, mv.ap[:, 1:2], AF.Sqrt, [mv], [rstd], bias=LN_EPS)
            P.add("dve", lambda e: e.reciprocal(rstd.ap, rstd.ap), R(rstd), R(rstd))
            stt(xb.ap, xb.ap, mv.ap[:, 0:1], lnr[0].ap, ALU.subtract, ALU.mult, [xb, mv, lnr[0]], [xb])
            stt(xb.ap, xb.ap, rstd.ap[:, 0:1], lnr[1].ap, ALU.mult, ALU.add, [xb, rstd, lnr[1]], [xb])

        def to_featmajor(xb, t):
            for half in range(2):
                bk = nb()
                for j in range(4):
                    kc = half * 4 + j
                    tr(bk.ap[:, j * 128:(j + 1) * 128], xb.ap[:, kc * 128:(kc + 1) * 128], ident_f,
                       [xb, smP], [bk])
                cp("act", xTb.ap[:, half * 4:half * 4 + 4, t * 128:(t + 1) * 128],
                   bk.ap.rearrange("p (j t) -> p j t", j=4), [bk], [xTb.res[t]])

        v3 = lambda kcn, c0, nc_: (lambda slot: slot.ap.rearrange("p (k c) -> p k c", k=kcn)[:, :, c0:c0 + nc_])
        for blk in range(4):
            t0 = blk * 1024
            dma("sp", xTb.ap, xTs_d[:, :, t0:t0 + 1024], [xTs_res.res[blk]], [xTb])
            for t in range(8):
                dma("sp", xs3[t].ap, x_d[t0 + t * 128:t0 + (t + 1) * 128, :], [], [xs3[t]])
            for mq in range(4):
                gcol = 2816 + mq * 256
                piece = load_piece([
                    (v3(8, 0, 256), w_in_d[:, gcol:gcol + 256].rearrange("(kc p) j -> p kc j", p=128)),
                    (v3(8, 256, 256), w_in_d[:, gcol + 1024:gcol + 1280].rearrange("(kc p) j -> p kc j", p=128)),
                ])
                pv = piece.ap.rearrange("p (k c) -> p k c", k=8)
                for m2 in range(2):
                    m = mq * 2 + m2
                    for nh in range(2):
                        cs = slice(nh * 512, (nh + 1) * 512)
                        gs_, ga_, t1_, t2_ = gsb[nh], gab[nh], tm1[nh], tm2[nh]
                        b_gs, b_ga, b_ps, b_pa = nb(), nb(), nb(), nb()
                        for kc in range(8):
                            mm(b_gs.ap, pv[:, kc, m2 * 128:(m2 + 1) * 128], xTb.ap[:, kc, cs], kc == 0, kc == 7,
                               [piece, xTb], [b_gs])
                        for kc in range(8):
                            mm(b_ga.ap, pv[:, kc, 256 + m2 * 128:256 + (m2 + 1) * 128], xTb.ap[:, kc, cs],
                               kc == 0, kc == 7, [piece, xTb], [b_ga])
                        for kc in range(4):
                            mm(b_ps.ap, wsp.ap[:, kc, m * 128:(m + 1) * 128],
                               Y2T.ap[:, kc, t0 + nh * 512:t0 + (nh + 1) * 512], kc == 0, kc == 3,
                               [wsp, Y2T], [b_ps])
                        for kc in range(2):
                            mm(b_pa.ap, wap.ap[:, kc, m * 128:(m + 1) * 128],
                               yattnT.ap[:, kc, t0 + nh * 512:t0 + (nh + 1) * 512], kc == 0, kc == 1,
                               [wap, yattnT], [b_pa])
                        act(gs_.ap, b_gs.ap, AF.Sigmoid, [b_gs, smP], [gs_], bias=bgate[:, m:m + 1])
                        act(ga_.ap, b_ga.ap, AF.Sigmoid, [b_ga, smP], [ga_], bias=bgate[:, 8 + m:9 + m])
                        tt("dve", t1_.ap, b_ps.ap, gs_.ap, ALU.mult, [b_ps, gs_], [t1_])
                        tt("dve", t2_.ap, b_pa.ap, ga_.ap, ALU.mult, [b_pa, ga_], [t2_])
                        tt("dve", mixin.ap[:, m, cs], t1_.ap, t2_.ap, ALU.add, [t1_, t2_], [mixin])
            wo = [load_piece([(v3(8, 0, 512), wout_d[:, oh * 512:(oh + 1) * 512]
                               .rearrange("(kc p) j -> p kc j", p=128))]) for oh in range(2)]
            for i, r_ in enumerate(lnr):
                dma("sp", r_.ap, lnrows_d[i], [], [r_])
            for t in range(8):
                for oh in range(2):
                    bk = nb()
                    for kc in range(8):
                        mm(bk.ap, mixin.ap[:, kc, t * 128:(t + 1) * 128],
                           wo[oh].ap.rearrange("p (k c) -> p k c", k=8)[:, kc, :],
                           kc == 0, kc == 7, [mixin, wo[oh]], [bk])
                    xo = xs3[t].ap[:, oh * 512:(oh + 1) * 512]
                    stt(xo, xo, ALPHA, bk.ap, ALU.mult, ALU.add, [xs3[t], bk], [xs3[t]])
                layer_norm(xs3[t], 0)
                if t >= 1:
                    to_featmajor(xs3[t - 1], t - 1)
            to_featmajor(xs3[7], 7)
            for h2 in range(2):
                for up in range(4):
                    c0 = h2 * 2048 + up * 512
                    piece = load_piece([(v3(8, 0, 512),
                                         wup_d[:, c0:c0 + 512].rearrange("(kc p) j -> p kc j", p=128))])
                    pv = piece.ap.rearrange("p (k c) -> p k c", k=8)
                    for m4 in range(4):
                        mloc = up * 4 + m4
                        for nh in range(2):
                            cs = slice(nh * 512, (nh + 1) * 512)
                            bk = nb()
                            for kc in range(8):
                                mm(bk.ap, pv[:, kc, m4 * 128:(m4 + 1) * 128], xTb.ap[:, kc, cs], kc == 0, kc == 7,
                                   [piece, xTb], [bk])
                            sq_ = sqb[nh]
                            act(sq_.ap, bk.ap, AF.Square, [bk], [sq_])
                            stt(aT.ap[:, mloc, cs], bk.ap, 0.0, sq_.ap, ALU.is_gt, ALU.mult, [bk, sq_], [aT])
                def dn_piece(oh, kh):
                    r0 = h2 * 2048 + kh * 1024
                    return load_piece([(v3(8, 0, 512), wdn_d[r0:r0 + 1024, oh * 512:(oh + 1) * 512]
                                        .rearrange("(kc p) j -> p kc j", p=128))])

                def dn_tile(t, oh, pcs):
                    bk = nb()
                    for kc in range(16):
                        pc_ = pcs[kc // 8]
                        mm(bk.ap, aT.ap[:, kc, t * 128:(t + 1) * 128],
                           pc_.ap.rearrange("p (k c) -> p k c", k=8)[:, kc % 8, :], kc == 0, kc == 15,
                           [aT, pc_], [bk])
                    xo = xs3[t].ap[:, oh * 512:(oh + 1) * 512]
                    if h2 == 0:
                        stt(xo, xo, ALPHA, bk.ap, ALU.mult, ALU.add, [xs3[t], bk], [xs3[t]])
                    else:
                        tt("dve", xo, xo, bk.ap, ALU.add, [xs3[t], bk], [xs3[t]])

                if h2 == 0:
                    for oh in range(2):
                        pcs = [dn_piece(oh, kh) for kh in range(2)]
                        for t in range(8):
                            dn_tile(t, oh, pcs)
                else:
                    pcs2 = [[dn_piece(oh, kh) for kh in range(2)] for oh in range(2)]
                    for i, r_ in enumerate(lnr):
                        dma("sp", r_.ap, lnrows_d[2 + i], [], [r_])
                    for t in range(8):
                        for oh in range(2):
                            dn_tile(t, oh, pcs2[oh])
                        layer_norm(xs3[t], 1)
                        dma("sp", out_d[t0 + t * 128:t0 + (t + 1) * 128, :], xs3[t].ap, [xs3[t]], [])

    P.finalize(nc, es)
    with nc.Block() as block:
        @block.tensor
        def _(e):
            P.emit_engine("pe", e)

        @block.scalar
        def _(e):
            P.emit_engine("act", e)

        @block.vector
        def _(e):
            P.emit_engine("dve", e)

        @block.gpsimd
        def _(e):
            P.emit_engine("pool", e)

        @block.sync
        def _(e):
            P.emit_engine("sp", e)
    es.close()
    nops = {k: len(v) for k, v in P.ops.items()}
    return nc, dbg, nops


def make_in_maps(inputs):
    g = lambda k: np.asarray(inputs[k], np.float32)
    smP = build_smalls_p(g("b_gate")[0], g("lambda_re")[0], g("lambda_im")[0], g("log_dt")[0])
    smB = build_smalls_b(g("lambda_re")[0], g("lambda_im")[0], g("log_dt")[0], g("ssm_b_re")[0],
                         g("ssm_b_im")[0], g("ssm_c_re")[0], g("ssm_c_im")[0], g("ssm_d")[0])
    biasT = build_bias_tables(g("rel_bias"))
    lnrows = np.stack([np.broadcast_to(g(k)[0][None, :], (128, D)) for k in ("ln1_g", "ln1_b", "ln2_g", "ln2_b")])
    lnrows = np.ascontiguousarray(lnrows, np.float32)
    shared = {
        "w_in": np.ascontiguousarray(g("w_in")[0]), "smalls_p": smP, "smalls_b": smB,
        "w_glu": np.ascontiguousarray(g("w_glu")[0]), "bias_t": biasT,
        "w_ssm_proj": np.ascontiguousarray(g("w_ssm_proj")[0]),
        "w_attn_proj": np.ascontiguousarray(g("w_attn_proj")[0]),
        "w_out": np.ascontiguousarray(g("w_out")[0]), "w_up": np.ascontiguousarray(g("w_up")[0]),
        "w_down": np.ascontiguousarray(g("w_down")[0]), "ln_rows": lnrows,
    }
    x = g("x")
    return [dict(shared, x=np.ascontiguousarray(x[b])) for b in range(x.shape[0])]


_CACHE = {}


def kernel(**inputs):
    in_maps = make_in_maps(inputs)
    if "nc" not in _CACHE:
        _CACHE["nc"] = build_program()[0]
    nc = _CACHE["nc"]
    res = run_bass_kernel_spmd(nc, in_maps, core_ids=list(range(NCORES)))
    return np.stack([np.asarray(r["out"], np.float32) for r in res.results], 0)
```

```python
import math
from contextlib import ExitStack

import numpy as np
import concourse.bass as bass
import concourse.mybir as mybir
from concourse.bass_utils import run_bass_kernel_spmd

F32 = mybir.dt.float32
BF16 = mybir.dt.bfloat16
I32 = mybir.dt.int32
AF = mybir.ActivationFunctionType
ALU = mybir.AluOpType

D = 1024
L = 4096
NCORES = 8
DFF = 4096
INW = 4864
ALPHA = 2.0 ** 0.25
LN_EPS = 1e-5
NEG = -30000.0
TWO_PI = 6.283185
HALF_PI = 1.5707963
MAGIC = 12582912.0
DIL = (1, 4, 16)


class Res:
    __slots__ = ("w", "r", "rd", "excl")

    def __init__(self, excl=False):
        self.excl = excl
        self.w = None
        self.r = {}
        self.rd = []


class Op:
    __slots__ = ("eng", "emit", "deps", "dma", "sem", "val", "prev", "used", "seq")
    _seq = 0

    def __init__(self, eng, emit, dma):
        Op._seq += 1
        self.seq = Op._seq
        self.eng = eng
        self.emit = emit
        self.dma = dma
        self.deps = []
        self.sem = None
        self.val = 0
        self.prev = 0
        self.used = False


class Prog:
    ENGS = ("pe", "act", "dve", "pool", "sp")
    EPOCH = 20000

    def __init__(self):
        self.ops = {e: [] for e in self.ENGS}

    def add(self, eng, emit, reads=(), writes=(), dma=False):
        op = Op(eng, emit, dma)
        deps = {}
        ex = [r for r in reads if r.excl]
        if ex:
            writes = list(writes) + [r for r in ex if r not in writes]
            reads = [r for r in reads if not r.excl]

        def dep(o):
            if o is not None and o is not op:
                deps[id(o)] = o

        for r in reads:
            dep(r.w)
        for w in writes:
            if w.w is not None and (dma or w.w.dma or w.w.eng != eng or eng != "pe"):
                dep(w.w)
            for e, o in w.r.items():
                if dma or e != eng or eng != "pe":
                    dep(o)
            for o in w.rd:
                dep(o)
        for r in reads:
            if dma:
                r.rd.append(op)
            else:
                r.r[eng] = op
        for w in writes:
            w.w = op
            w.r = {}
            w.rd = []
        op.deps = list(deps.values())
        for o in op.deps:
            o.used = True
        self.ops[eng].append(op)
        return op

    def finalize(self, nc, es, ndma=12):
        self.sems = {}
        for e in self.ENGS:
            cnt = 0
            dcnt = 0
            dsems = []
            esems = []
            for op in self.ops[e]:
                if op.dma:
                    if len(dsems) < ndma:
                        dsems.append([es.enter_context(nc.semaphore(f"d_{e}_{len(dsems)}")), 0])
                    slot = dsems[dcnt % ndma]
                    dcnt += 1
                    op.prev = slot[1]
                    slot[1] += 16
                    op.sem = slot[0]
                    op.val = slot[1]
                elif op.used:
                    ep = cnt // self.EPOCH
                    if ep >= len(esems):
                        esems.append(es.enter_context(nc.semaphore(f"c_{e}_{ep}")))
                    op.sem = esems[ep]
                    op.val = cnt % self.EPOCH + 1
                    cnt += 1

    def emit_engine(self, e, eng):
        waited = {}

        def wait(sem, val):
            k = id(sem)
            if waited.get(k, 0) < val:
                eng.wait_ge(sem, val)
                waited[k] = val

        for op in self.ops[e]:
            for d in op.deps:
                wait(d.sem, d.val)
            if op.dma and op.prev > 0:
                wait(op.sem, op.prev)
            ins = op.emit(eng)
            if op.sem is not None:
                ins.then_inc(op.sem, 16 if op.dma else 1)
        last = {}
        for op in self.ops[e]:
            if op.dma:
                last[id(op.sem)] = (op.sem, op.val)
        for sem, val in last.values():
            wait(sem, val)


class Buf:
    def __init__(self, ap, nres=1, excl=False):
        self.ap = ap
        self.res = [Res(excl) for _ in range(nres)]

    @property
    def r0(self):
        return self.res[0]


class Arena:
    def __init__(self, nc, es, nbytes):
        self.t = es.enter_context(nc.sbuf_tensor("arena", [128, nbytes // 4], F32))
        self.cap = nbytes
        self.off = 0
        self.limit = nbytes
        self.hist = []

    def seek(self, off, limit=None):
        self.off = off
        self.limit = self.cap if limit is None else limit

    def alloc(self, shape, dtype, nres=1):
        n = int(np.prod(shape))
        nb = n * (2 if dtype == BF16 else 4)
        off = (self.off + 63) // 64 * 64
        assert off + nb <= self.limit, f"SBUF arena overflow {off + nb} > {self.limit}"
        self.off = off + nb
        v = self.t[:, off // 4:(off + nb + 3) // 4]
        if dtype != F32:
            v = v.bitcast(dtype)[:, 0:n]
        if len(shape) > 1:
            names = " ".join(f"a{i}" for i in range(len(shape)))
            kw = {f"a{i}": int(s) for i, s in enumerate(shape)}
            v = v.rearrange(f"p ({names}) -> p {names}", **kw)
        b = Buf(v, nres)
        seed_r, seed_rd = {}, []
        for (s0, s1, ob) in self.hist:
            if s0 < off + nb and off < s1:
                for rs in ob.res:
                    cands = list(rs.r.values()) + ([rs.w] if rs.w is not None else [])
                    for o in cands:
                        if o.dma:
                            seed_rd.append(o)
                        elif o.eng not in seed_r or seed_r[o.eng].seq < o.seq:
                            seed_r[o.eng] = o
                    seed_rd.extend(rs.rd)
        for rs in b.res:
            rs.r = dict(seed_r)
            rs.rd = list(seed_rd)
        self.hist.append((off, off + nb, b))
        return b


def _t5_bucket_np(dist):
    max_exact = 16
    d = np.maximum(dist, 1).astype(np.float32)
    large = max_exact + (np.log(d / np.float32(max_exact)) / np.float32(math.log(2048 / max_exact))
                         * np.float32(32 - max_exact)).astype(np.int32)
    large = np.minimum(large, 31)
    return np.where(dist < max_exact, dist, large)


def _layout(items):
    cols, off = {}, 0
    for name, n in items:
        cols[name] = (off, n)
        off += n
    return cols, off


SP_COLS, SP_N = _layout((("ident", 128), ("iota", 512), ("kvals", 16), ("sgn", 1), ("maskM", 128),
                         ("bgate", 16), ("lamre", 32), ("lamim", 32), ("logdt", 32)))
SB_COLS, SB_N = _layout((("lamre", 32), ("lamim", 32), ("logdt", 32), ("drep", 32),
                         ("b1", 512), ("b2raw", 512), ("c1raw", 512), ("c2", 512)))


def _put(a, cols, name, v):
    o, n = cols[name]
    a[:, o:o + n] = np.asarray(v, np.float32).reshape(128, n)


def build_smalls_p(b_gate, lambda_re, lambda_im, log_dt):
    a = np.zeros((128, SP_N), np.float32)
    _put(a, SP_COLS, "ident", np.eye(128, dtype=np.float32))
    _put(a, SP_COLS, "iota", np.broadcast_to(np.arange(512, dtype=np.float32)[None, :], (128, 512)))
    _put(a, SP_COLS, "kvals", np.broadcast_to(
        np.array([1, 2, 3, 4, 5, 6, 7, 8, 7, 6, 5, 4, 3, 2, 1, 0], np.float32)[None, :], (128, 16)))
    _put(a, SP_COLS, "sgn", np.concatenate([np.ones(64), -np.ones(64)])[:, None])
    s_idx = np.arange(128) // 16
    _put(a, SP_COLS, "maskM", (s_idx[None, :] >= s_idx[:, None]).astype(np.float32))
    _put(a, SP_COLS, "bgate", b_gate.reshape(16, 128).T)
    lr, li = lambda_re.T, lambda_im.T
    _put(a, SP_COLS, "lamre", np.concatenate([lr, lr], 0))
    _put(a, SP_COLS, "lamim", np.concatenate([li, li], 0))
    _put(a, SP_COLS, "logdt", np.broadcast_to(log_dt[None, :], (128, 32)))
    return a


def build_smalls_b(lambda_re, lambda_im, log_dt, b_re, b_im, c_re, c_im, d_skip):
    a = np.zeros((128, SB_N), np.float32)
    lr, li = lambda_re.T, lambda_im.T
    _put(a, SB_COLS, "lamre", np.concatenate([lr, lr], 0))
    _put(a, SB_COLS, "lamim", np.concatenate([li, li], 0))
    _put(a, SB_COLS, "logdt", np.broadcast_to(log_dt[None, :], (128, 32)))
    _put(a, SB_COLS, "drep", np.tile(d_skip.T, (8, 1)))
    bre, bim = b_re.transpose(1, 0, 2), b_im.transpose(1, 0, 2)
    _put(a, SB_COLS, "b1", np.concatenate([bre, bim], 0))
    _put(a, SB_COLS, "b2raw", np.concatenate([bim, bre], 0))
    cre, cim = c_re.transpose(2, 0, 1), c_im.transpose(2, 0, 1)
    _put(a, SB_COLS, "c1raw", np.concatenate([cre, cim], 0))
    _put(a, SB_COLS, "c2", np.concatenate([cim, cre], 0))
    return a


def build_bias_tables(rel_bias):
    out = np.full((6, 128, 2, 512), NEG, np.float32)
    j = np.arange(128)[:, None]
    a = np.arange(128)[None, :]
    for hp in range(2):
        for g in range(3):
            rnd = hp * 3 + g
            for h2 in range(2):
                head = g * 4 + hp * 2 + h2
                rel_prev = a + 128 - j
                rel_cur = a - j
                for kb, rel in ((0, rel_prev), (1, rel_cur)):
                    valid = (rel >= 0) & (rel <= 128)
                    idx = _t5_bucket_np(np.maximum(rel, 0) * DIL[g])
                    vals = np.where(valid, rel_bias[idx, head], np.float32(NEG))
                    c0 = h2 * 256 + kb * 128
                    out[rnd, :, 0, c0:c0 + 128] = vals
                    if kb == 1:
                        out[rnd, :, 1, c0:c0 + 128] = vals
    return out


KB = 1024


def build_program(debug=(), stages=("A", "U", "ATT", "SSM", "P3")):
    nc = bass.Bass("TRN2", target_bir_lowering=False)
    es = ExitStack()
    P = Prog()

    def dram_in(name, shape, dt=F32):
        return nc.dram_tensor(name, list(shape), dt, kind="ExternalInput").ap()

    x_d = dram_in("x", [L, D])
    w_in_d = dram_in("w_in", [D, INW])
    smP_d = dram_in("smalls_p", [128, SP_N])
    smB_d = dram_in("smalls_b", [128, SB_N])
    w_glu_d = dram_in("w_glu", [512, 1024])
    biasT_d = dram_in("bias_t", [6, 128, 2, 512])
    wsp_d = dram_in("w_ssm_proj", [512, D])
    wap_d = dram_in("w_attn_proj", [256, D])
    wout_d = dram_in("w_out", [D, D])
    wup_d = dram_in("w_up", [D, DFF])
    wdn_d = dram_in("w_down", [DFF, D])
    lnrows_d = dram_in("ln_rows", [4, 128, D])
    out_d = nc.dram_tensor("out", [L, D], F32, kind="ExternalOutput").ap()
    dbg = {}

    ar = Arena(nc, es, 206 * KB)
    ps_all = es.enter_context(nc.psum_tensor("ps", [128, 4096], F32))
    banks = [Buf(ps_all[:, b * 512:(b + 1) * 512], excl=True) for b in range(8)]

    def bank_bf(b):
        return banks[b].ap.bitcast(BF16)

    def R(*bufs):
        out = []
        for b in bufs:
            if isinstance(b, Buf):
                out.extend(b.res)
            elif isinstance(b, Res):
                out.append(b)
            else:
                out.extend(R(*b))
        return out

    def dma(eng, out, in_, reads, writes):
        return P.add(eng, lambda e: e.dma_start(out=out, in_=in_), R(*reads), R(*writes), dma=True)

    def mm(out, lhsT, rhs, start, stop, reads, writes, **kw):
        return P.add("pe", lambda e: e.matmul(out, lhsT, rhs, start=start, stop=stop, **kw),
                     R(*reads), R(*writes))

    def tr(out, in_, ident, reads, writes):
        return P.add("pe", lambda e: e.transpose(out, in_, ident), R(*reads), R(*writes))

    def act(out, in_, func, reads, writes, bias=0.0, scale=1.0):
        return P.add("act", lambda e: e.activation(out, in_, func, bias=bias, scale=scale),
                     R(*reads), R(*writes))

    def tt(eng, out, in0, in1, op, reads, writes):
        return P.add(eng, lambda e: e.tensor_tensor(out, in0, in1, op), R(*reads), R(*writes))

    def ts(eng, out, in0, s1, s2, op0, op1, reads, writes):
        if s2 is None:
            return P.add(eng, lambda e: e.tensor_scalar(out, in0, s1, None, op0), R(*reads), R(*writes))
        return P.add(eng, lambda e: e.tensor_scalar(out, in0, s1, s2, op0, op1), R(*reads), R(*writes))

    def stt(out, in0, scalar, in1, op0, op1, reads, writes):
        return P.add("dve", lambda e: e.scalar_tensor_tensor(out, in0, scalar, in1, op0, op1),
                     R(*reads), R(*writes))

    def cp(eng, out, in_, reads, writes):
        if eng == "act":
            return P.add("act", lambda e: e.copy(out, in_), R(*reads), R(*writes))
        return P.add(eng, lambda e: e.tensor_copy(out, in_), R(*reads), R(*writes))

    def scan(dst, d0, src, extra_reads):
        return P.add("dve", lambda e: e.tensor_tensor_scan(dst.ap, d0, src.ap, 0.0, ALU.mult, ALU.add),
                     R(src, *extra_reads), R(dst))

    def memset(eng, out, val, writes):
        return P.add(eng, lambda e: e.memset(out, val), (), R(*writes))

    def dump(name, buf_ap, shape, dt, reads):
        d = nc.dram_tensor("dbg_" + name, list(shape), dt, kind="ExternalOutput").ap()
        dbg[name] = d
        dma("sp", d, buf_ap, reads, [])

    ar.seek(0, 8 * KB)
    smP = ar.alloc([SP_N], F32)
    dma("sp", smP.ap, smP_d, [], [smP])

    def sp_(name):
        o, n = SP_COLS[name]
        return smP.ap[:, o:o + n]

    ident_f = sp_("ident")
    identb = ar.alloc([128], BF16)
    cp("dve", identb.ap, ident_f, [smP], [identb])
    onesAB = ar.alloc([2, 128], BF16)
    memset("pool", onesAB.ap, 0.0, [onesAB])
    memset("pool", onesAB.ap[:, 0, 0:64], 1.0, [onesAB])
    memset("pool", onesAB.ap[:, 1, 64:128], 1.0, [onesAB])

    rp = ar.alloc([32], F32)
    f8p = ar.alloc([32], F32)
    e_dt = ar.alloc([32], F32)
    e_a = ar.alloc([32], F32)
    e_b = ar.alloc([32], F32)
    act(e_dt.ap, sp_("logdt"), AF.Exp, [smP], [e_dt])
    tt("dve", e_a.ap, sp_("lamre"), e_dt.ap, ALU.mult, [smP, e_dt], [e_a])
    act(rp.ap, e_a.ap, AF.Exp, [e_a], [rp], scale=8.0)
    tt("dve", e_a.ap, sp_("lamim"), e_dt.ap, ALU.mult, [smP, e_dt], [e_a])
    ts("dve", e_a.ap, e_a.ap, 8.0, 1.0 / (2.0 * math.pi), ALU.mult, ALU.mult, [e_a], [e_a])
    ts("dve", e_b.ap, e_a.ap, MAGIC, -MAGIC, ALU.add, ALU.add, [e_a], [e_b])
    tt("dve", e_b.ap, e_a.ap, e_b.ap, ALU.subtract, [e_a, e_b], [e_b])
    stt(f8p.ap, e_b.ap, 0.0, e_b.ap, ALU.is_lt, ALU.add, [e_b], [f8p])
    tabs_d = nc.dram_tensor("tabs_scratch", [32, 128, 2, 512], F32, kind="Internal").ap()
    tabs_res = Buf(None, 16)
    xTs_d = nc.dram_tensor("xT_scratch", [128, 8, L], BF16, kind="Internal").ap()
    xTs_res = Buf(None, 4)

    ar.seek(8 * KB, 72 * KB)
    xT = ar.alloc([8, L], BF16, nres=32)
    ar.seek(72 * KB, 104 * KB)
    Ug = ar.alloc([32, 512], BF16, nres=32)
    ar.seek(104 * KB, 120 * KB)
    yattnT = ar.alloc([2, L], BF16, nres=2)
    T0, T1 = 120 * KB, 206 * KB

    ar.seek(160 * KB, T1)
    xs = [ar.alloc([D], BF16) for _ in range(4)]
    pi = 0
    for tt_i in range(32):
        xb = xs[tt_i % 4]
        dma("pool", xb.ap, x_d[tt_i * 128:(tt_i + 1) * 128, :], [], [xb])
        b = pi % 2
        pi += 1
        bkb = bank_bf(b)
        for kc in range(8):
            tr(bkb[:, kc * 128:(kc + 1) * 128], xb.ap[:, kc * 128:(kc + 1) * 128], identb.ap,
               [xb, identb], [banks[b]])
        dst = xT.ap[:, :, tt_i * 128:(tt_i + 1) * 128]
        cp("dve" if tt_i % 2 == 0 else "act", dst, bkb.rearrange("p (j t) -> p j t", j=8),
           [banks[b]], [xT.res[tt_i]])
        if tt_i % 8 == 7:
            b4 = tt_i // 8
            dma("sp", xTs_d[:, :, b4 * 1024:(b4 + 1) * 1024], xT.ap[:, :, b4 * 1024:(b4 + 1) * 1024],
                [xT.res[b4 * 8 + i] for i in range(8)], [xTs_res.res[b4]])
    if "xT" in debug:
        dump("xT", xT.ap, [128, 8, L], BF16, [xT])

    if "U" in stages:
        ar.seek(T0, T1)
        wu = ar.alloc([8, 512], BF16)
        dma("pool", wu.ap, w_in_d[:, 2304:2816].rearrange("(kc p) j -> p kc j", p=128), [], [wu])
        Uc = ar.alloc([4, 32, 128], BF16, nres=4)
        xT5 = xT.ap.rearrange("p k (b c s) -> p k b c s", b=4, s=8)
        ar.seek(172 * KB, T1)
        gcs = [ar.alloc([2, 2, 512], F32) for _ in range(2)]
        gAs = [ar.alloc([2, 512], F32) for _ in range(2)]
        gFs = [ar.alloc([2, 512], F32) for _ in range(2)]
        iota = sp_("iota")

        def gen_tables(bt):
            cs_, tA, tF = gcs[bt % 2], gAs[bt % 2], gFs[bt % 2]
            tt("dve", tA.ap, f8p.ap[:, 2 * bt:2 * bt + 2, None].to_broadcast([128, 2, 512]),
               iota[:, None, :].to_broadcast([128, 2, 512]), ALU.mult, [f8p, smP], [tA])
            ts("dve", tF.ap, tA.ap, MAGIC, -MAGIC, ALU.add, ALU.add, [tA], [tF])
            tt("dve", tF.ap, tA.ap, tF.ap, ALU.subtract, [tA, tF], [tF])
            act(cs_.ap[:, :, 1, :], tF.ap, AF.Sin, [tF], [cs_], scale=TWO_PI)
            act(cs_.ap[:, :, 0, :], tF.ap, AF.Sin, [tF], [cs_], scale=TWO_PI / 2.0)
            act(cs_.ap[:, :, 0, :], cs_.ap[:, :, 0, :], AF.Square, [cs_], [cs_])
            ts("pool", cs_.ap[:, :, 0, :], cs_.ap[:, :, 0, :], -2.0, 1.0, ALU.mult, ALU.add, [cs_], [cs_])
            dma("sp", tabs_d[2 * bt:2 * bt + 2].rearrange("g p t c -> p g t c"), cs_.ap, [cs_],
                [tabs_res.res[bt]])

        for blk in range(4):
            for s in range(8):
                if s % 2 == 0:
                    gen_tables(blk * 4 + s // 2)
                bk = banks[pi % 2]
                pi += 1
                for kc in range(8):
                    mm(bk.ap, xT5[:, kc, blk, :, s], wu.ap[:, kc, :], kc == 0, kc == 7,
                       [xT.res[blk * 8 + i] for i in range(8)] + [wu], [bk])
                cp("act" if s % 2 else "dve", Uc.ap[:, blk, :, s * 16:(s + 1) * 16],
                   bk.ap.rearrange("p (g h) -> p g h", g=32), [bk], [Uc.res[blk]])
        for g in range(32):
            b = 2 + pi % 2
            pi += 1
            bkb = bank_bf(b)
            for blk in range(4):
                tr(bkb[:, blk * 128:(blk + 1) * 128], Uc.ap[:, blk, g, :], identb.ap,
                   [Uc.res[blk], identb], [banks[b]])
            cp("act" if g % 2 else "dve", Ug.ap[:, g, :], bkb[:, 0:512], [banks[b]], [Ug.res[g]])
        if "Ug" in debug:
            dump("Ug", Ug.ap, [128, 32, 512], BF16, [Ug])

    if "ATT" in stages:
        ar.seek(T0, T1)
        biasb = [ar.alloc([2, 512], BF16) for _ in range(1)]
        wq = [ar.alloc([8, 384], BF16) for _ in range(1)]
        qT = ar.alloc([2, L], BF16)
        kT = ar.alloc([L], BF16)
        memset("dve", qT.ap, 0.0, [qT])
        V = ar.alloc([32, 2, 128], BF16, nres=32)
        acc = ar.alloc([2, L], F32)
        PT = [ar.alloc([512], BF16) for _ in range(2)]
        memset("dve", V.ap, 0.0, [V])
        pti = 0
        si = 0
        for hp in range(2):
            for g in range(3):
                rnd = hp * 3 + g
                dil = DIL[g]
                nblk = 32 // dil
                bb = biasb[0]
                wb = wq[0]
                dma("pool", bb.ap, biasT_d[rnd], [], [bb])
                for j, c0 in enumerate((0, 768, 1536)):
                    col = c0 + g * 256 + hp * 128
                    dma("pool", wb.ap[:, :, j * 128:(j + 1) * 128],
                        w_in_d[:, col:col + 128].rearrange("(kc p) j -> p kc j", p=128), [], [wb])
                for which, dstT in ((0, qT), (1, kT)):
                    for nt in range(8):
                        bk = banks[pi % 2]
                        pi += 1
                        for kc in range(8):
                            mm(bk.ap, wb.ap[:, kc, which * 128:(which + 1) * 128],
                               xT.ap[:, kc, nt * 512:(nt + 1) * 512], kc == 0, kc == 7,
                               [wb] + [xT.res[nt * 4 + i] for i in range(4)], [bk])
                        cs_ = slice(nt * 512, (nt + 1) * 512)
                        if which == 0:
                            ts("dve", qT.ap[0:64, 0, cs_], bk.ap[0:64, :], 0.125, None, ALU.mult, None, [bk], [qT])
                            ts("dve", qT.ap[64:128, 1, cs_], bk.ap[64:128, :], 0.125, None, ALU.mult, None,
                               [bk], [qT])
                        else:
                            cp("dve", kT.ap[:, cs_], bk.ap, [bk], [kT])
                xTd = xT.ap.rearrange("p k (i a d) -> p k d i a", a=128, d=dil)
                for t4 in range(8):
                    b = 6 + t4 % 2
                    bk = banks[b]
                    for j in range(4):
                        tile = t4 * 4 + j
                        r, i = tile // nblk, tile % nblk
                        for kc in range(8):
                            mm(bk.ap[:, j * 128:(j + 1) * 128], xTd[:, kc, r, i, :], wb.ap[:, kc, 256:384],
                               kc == 0, kc == 7, [xT, wb], [bk])
                    src = bk.ap.rearrange("p (t c) -> p t c", t=4)
                    cp("dve", V.ap[:, t4 * 4:t4 * 4 + 4, 0, 0:64], src[:, :, 0:64], [bk],
                       [V.res[t4 * 4 + j] for j in range(4)])
                    cp("dve", V.ap[:, t4 * 4:t4 * 4 + 4, 1, 64:128], src[:, :, 64:128], [bk],
                       [V.res[t4 * 4 + j] for j in range(4)])
                qTd = qT.ap.rearrange("p h (i a d) -> p h d i a", a=128, d=dil)
                kTd = kT.ap.rearrange("p (i a d) -> p d i a", a=128, d=dil)
                accd = acc.ap.rearrange("p c (i a d) -> p c d i a", a=128, d=dil)
                tiles = [(r, i) for r in range(dil) for i in range(nblk)]

                def s_part(r, i, sb, pt):
                    first = (i == 0)
                    mm(sb.ap, identb.ap, bb.ap[:, 1 if first else 0, :], True, False, [identb, bb], [sb])
                    combos = [(h2, kb) for h2 in range(2) for kb in range(2) if not (first and kb == 0)]
                    for n_, (h2, kb) in enumerate(combos):
                        c0 = h2 * 256 + kb * 128
                        mm(sb.ap[:, c0:c0 + 128], kTd[:, r, i - 1 + kb, :], qTd[:, h2, r, i, :],
                           False, n_ == len(combos) - 1, [kT, qT], [sb])
                    act(pt.ap, sb.ap, AF.Exp, [sb], [pt])

                def pv_part(r, i, pb, pt):
                    first = (i == 0)
                    combos = [(h2, kb) for h2 in range(2) for kb in range(2) if not (first and kb == 0)]
                    for part in range(2):
                        for n_, (h2, kb) in enumerate(combos):
                            tl = r * nblk + i - 1 + kb
                            c0 = h2 * 256 + kb * 128
                            lhs = V.ap[:, tl, h2, :] if part == 0 else onesAB.ap[:, h2, :]
                            mm(pb.ap[:, part * 128:(part + 1) * 128], lhs, pt.ap[:, c0:c0 + 128],
                               n_ == 0, n_ == len(combos) - 1, [V.res[tl], onesAB, pt], [pb])
                    dsta = accd[:, :, r, i, :]
                    src = pb.ap[:, 0:256].rearrange("p (c a) -> p c a", c=2)
                    if g == 0:
                        cp("dve", dsta, src, [pb], [acc])
                    else:
                        tt("dve", dsta, dsta, src, ALU.add, [acc, pb], [acc])

                prev = None
                for (r, i) in tiles:
                    sb = banks[2 + si % 2]
                    pb = banks[4 + si % 2]
                    pt = PT[si % 2]
                    si += 1
                    s_part(r, i, sb, pt)
                    if prev is not None:
                        pv_part(*prev)
                    prev = (r, i, pb, pt)
                pv_part(*prev)
            P.add("dve", lambda e: e.reciprocal(acc.ap[:, 1, :], acc.ap[:, 1, :]), R(acc), R(acc))
            tt("dve", yattnT.ap[:, hp, :], acc.ap[:, 0, :], acc.ap[:, 1, :], ALU.mult, [acc], [yattnT.res[hp]])
        if "yattnT" in debug:
            dump("yattnT", yattnT.ap, [128, 2, L], BF16, [yattnT])

    if "SSM" in stages:
        ar.seek(8 * KB, 40 * KB)
        Yc = ar.alloc([4, 8, 512], BF16, nres=4)
        ar.seek(T0, 161 * KB)
        MT = ar.alloc([32, 128], BF16)
        L1b = ar.alloc([32, 128], BF16)
        L2b = ar.alloc([32, 128], BF16)
        WS1T = ar.alloc([32, 128], BF16)
        WS2T = ar.alloc([32, 128], BF16)
        ar.seek(161 * KB, T1)
        dt_ = ar.alloc([32], F32)
        lrdt = ar.alloc([32], F32)
        th = ar.alloc([32], F32)
        den = ar.alloc([32], F32)
        t32a = ar.alloc([32], F32)
        t32b = ar.alloc([32], F32)
        kre = ar.alloc([32], F32)
        kim = ar.alloc([32], F32)
        argm = ar.alloc([32, 16], F32)
        magp = ar.alloc([32, 16], F32)
        magn = ar.alloc([32, 16], F32)
        turns = ar.alloc([32, 16], F32)
        ti = ar.alloc([32, 16], I32)
        tf = argm
        fr = ar.alloc([32, 16], F32)
        msf = ar.alloc([32, 16], F32)
        frc = turns
        msc = msf
        c16 = ar.alloc([32, 16], F32)
        s16 = ar.alloc([32, 16], F32)
        Pc = ar.alloc([32, 16], F32)
        Ps = ar.alloc([32, 16], F32)
        Nc = ar.alloc([32, 16], F32)
        Ns = ar.alloc([32, 16], F32)
        b2 = c16
        c1 = s16
        Xb = ar.alloc([32, 16], F32)
        Yb = ar.alloc([32, 16], F32)
        tx = magn
        ar.seek(40 * KB, 72 * KB)
        smB = ar.alloc([SB_N], F32)
        dma("sp", smB.ap, smB_d, [], [smB])

        def sb_(name):
            o, n = SB_COLS[name]
            return smB.ap[:, o:o + n]

        E = "dve"
        lamre, lamim = sb_("lamre"), sb_("lamim")
        act(dt_.ap, sb_("logdt"), AF.Exp, [smB], [dt_])
        tt(E, lrdt.ap, lamre, dt_.ap, ALU.mult, [smB, dt_], [lrdt])
        tt(E, th.ap, lamim, dt_.ap, ALU.mult, [smB, dt_], [th])
        tt(E, t32a.ap, lamre, lamre, ALU.mult, [smB], [t32a])
        tt(E, t32b.ap, lamim, lamim, ALU.mult, [smB], [t32b])
        tt(E, den.ap, t32a.ap, t32b.ap, ALU.add, [t32a, t32b], [den])
        P.add("dve", lambda e: e.reciprocal(den.ap, den.ap), R(den), R(den))
        kv = sp_("kvals")
        kvb = kv[:, None, :].to_broadcast([128, 32, 16])
        tt(E, argm.ap, lrdt.ap[:, :, None].to_broadcast([128, 32, 16]), kvb, ALU.mult, [lrdt, smP], [argm])
        act(magp.ap, argm.ap, AF.Exp, [argm], [magp])
        act(magn.ap, argm.ap, AF.Exp, [argm], [magn], scale=-1.0)
        tt(E, turns.ap, th.ap[:, :, None].to_broadcast([128, 32, 16]), kvb, ALU.mult, [th, smP], [turns])
        ts(E, turns.ap, turns.ap, 1.0 / (2.0 * math.pi), None, ALU.mult, None, [turns], [turns])
        ts(E, tf.ap, turns.ap, MAGIC, -MAGIC, ALU.add, ALU.add, [turns], [tf])
        tt(E, fr.ap, turns.ap, tf.ap, ALU.subtract, [turns, tf], [fr])
        stt(msf.ap, fr.ap, 0.5, fr.ap, ALU.is_gt, ALU.subtract, [fr], [msf])
        act(s16.ap, msf.ap, AF.Sin, [msf], [s16], scale=-TWO_PI)
        ts(E, frc.ap, fr.ap, 0.25, None, ALU.add, None, [fr], [frc])
        stt(msc.ap, frc.ap, 0.5, frc.ap, ALU.is_gt, ALU.subtract, [frc], [msc])
        act(c16.ap, msc.ap, AF.Sin, [msc], [c16], scale=-TWO_PI)
        tt(E, Pc.ap, magp.ap, c16.ap, ALU.mult, [magp, c16], [Pc])
        tt(E, Ps.ap, magp.ap, s16.ap, ALU.mult, [magp, s16], [Ps])
        tt(E, Nc.ap, magn.ap, c16.ap, ALU.mult, [magn, c16], [Nc])
        tt(E, Ns.ap, magn.ap, s16.ap, ALU.mult, [magn, s16], [Ns])
        abre, abim = Pc.ap[:, :, 0], Ps.ap[:, :, 0]
        nr = t32a
        ts(E, nr.ap, abre, -1.0, None, ALU.add, None, [Pc], [nr])
        tt(E, t32b.ap, nr.ap, lamre, ALU.mult, [nr, smB], [t32b])
        tt(E, kre.ap, abim, lamim, ALU.mult, [Ps, smB], [kre])
        tt(E, kre.ap, kre.ap, t32b.ap, ALU.add, [kre, t32b], [kre])
        tt(E, kre.ap, kre.ap, den.ap, ALU.mult, [kre, den], [kre])
        tt(E, t32b.ap, nr.ap, lamim, ALU.mult, [nr, smB], [t32b])
        tt(E, kim.ap, abim, lamre, ALU.mult, [Ps, smB], [kim])
        tt(E, kim.ap, kim.ap, t32b.ap, ALU.subtract, [kim, t32b], [kim])
        tt(E, kim.ap, kim.ap, den.ap, ALU.mult, [kim, den], [kim])
        sgn = sp_("sgn")
        b1 = sb_("b1").rearrange("p (g h) -> p g h", g=32)
        b2raw = sb_("b2raw").rearrange("p (g h) -> p g h", g=32)
        c1raw = sb_("c1raw").rearrange("p (g h) -> p g h", g=32)
        c2 = sb_("c2").rearrange("p (g h) -> p g h", g=32)
        ts(E, b2.ap, b2raw, sgn, None, ALU.mult, None, [smB, smP], [b2])
        ts(E, c1.ap, c1raw, sgn, None, ALU.mult, None, [smB, smP], [c1])
        kreb = kre.ap[:, :, None].to_broadcast([128, 32, 16])
        kimb = kim.ap[:, :, None].to_broadcast([128, 32, 16])
        tt(E, Xb.ap, kreb, b1, ALU.mult, [kre, smB], [Xb])
        tt(E, tx.ap, kimb, b2.ap, ALU.mult, [kim, b2], [tx])
        tt(E, Xb.ap, Xb.ap, tx.ap, ALU.subtract, [Xb, tx], [Xb])
        tt(E, Yb.ap, kreb, b2.ap, ALU.mult, [kre, b2], [Yb])
        tt(E, tx.ap, kimb, b1, ALU.mult, [kim, smB], [tx])
        tt(E, Yb.ap, Yb.ap, tx.ap, ALU.add, [Yb, tx], [Yb])

        GB = 8
        AB1f = ar.alloc([GB, 8, 16], F32)
        L1f = ar.alloc([GB, 8, 16], F32)
        WSf = ar.alloc([GB, 8, 16], F32)
        mtmp = [ar.alloc([128], F32) for _ in range(2)]
        bigAs = [ar.alloc([GB, 8, 16], F32)]
        bigBs = [ar.alloc([GB, 8, 16], F32)]
        ar.seek(192 * KB, T1)
        bigAs.append(ar.alloc([GB, 8, 16], F32))
        bigBs.append(ar.alloc([GB, 8, 16], F32))
        obi = [0]

        def outer2(tabA, kA, vecA, tabB, kB, vecB, g0, readsA, readsB):
            bigA, bigB = bigAs[obi[0] % 2], bigBs[obi[0] % 2]
            obi[0] += 1
            for dst, tab, k0, vec, eng, rd in ((bigA, tabA, kA, vecA, "dve", readsA),
                                               (bigB, tabB, kB, vecB, "pool", readsB)):
                a = tab[:, g0:g0 + GB, k0:k0 + 8][:, :, :, None].to_broadcast([128, GB, 8, 16])
                b = vec[:, g0:g0 + GB, None, :].to_broadcast([128, GB, 8, 16])
                tt(eng, dst.ap, a, b, ALU.mult, rd, [dst])
            return bigA, bigB

        pbi = 0
        g3 = "p g s h -> p g (s h)"
        for gb in range(32 // GB):
            g0 = gb * GB
            bigA, bigB = outer2(Nc.ap, 0, Xb.ap, Ns.ap, 0, Yb.ap, g0, [Nc, Xb], [Ns, Yb])
            tt("dve", AB1f.ap, bigA.ap, bigB.ap, ALU.add, [bigA, bigB], [AB1f])
            bigA, bigB = outer2(Pc.ap, 0, c1.ap, Ps.ap, 0, c2, g0, [Pc, c1], [Ps, smB])
            tt("dve", L1f.ap, bigA.ap, bigB.ap, ALU.subtract, [bigA, bigB], [L1f])
            cp("act", L1b.ap[:, g0:g0 + GB, :], L1f.ap.rearrange(g3), [L1f], [L1b])
            for g in range(GB):
                bk = banks[pbi % 2]
                mt = mtmp[pbi % 2]
                pbi += 1
                mm(bk.ap[:, 0:128], AB1f.ap[:, g].rearrange("p s h -> p (s h)"),
                   L1f.ap[:, g].rearrange("p s h -> p (s h)"), True, True, [AB1f, L1f], [bk])
                tt("dve", mt.ap, bk.ap[:, 0:128], sp_("maskM"), ALU.mult, [bk, smP], [mt])
                o_, _n = SB_COLS["drep"]
                stt(MT.ap[:, g0 + g, :], ident_f, smB.ap[:, o_ + g0 + g:o_ + g0 + g + 1], mt.ap,
                    ALU.mult, ALU.add, [smP, smB, mt], [MT])
            bigA, bigB = outer2(Ps.ap, 0, c1.ap, Pc.ap, 0, c2, g0, [Ps, c1], [Pc, smB])
            stt(L2b.ap[:, g0:g0 + GB, :], bigA.ap.rearrange(g3), -1.0, bigB.ap.rearrange(g3),
                ALU.mult, ALU.subtract, [bigA, bigB], [L2b])
            for which in range(2):
                if which == 0:
                    bigA, bigB = outer2(Pc.ap, 8, Xb.ap, Ps.ap, 8, Yb.ap, g0, [Pc, Xb], [Ps, Yb])
                    tt("dve", WSf.ap, bigA.ap, bigB.ap, ALU.subtract, [bigA, bigB], [WSf])
                else:
                    bigA, bigB = outer2(Ps.ap, 8, Xb.ap, Pc.ap, 8, Yb.ap, g0, [Ps, Xb], [Pc, Yb])
                    tt("dve", WSf.ap, bigA.ap, bigB.ap, ALU.add, [bigA, bigB], [WSf])
                dstT = WS1T if which == 0 else WS2T
                for g4 in range(GB // 4):
                    bk = banks[pbi % 2]
                    pbi += 1
                    for j in range(4):
                        g = g4 * 4 + j
                        tr(bk.ap[:, j * 128:(j + 1) * 128], WSf.ap[:, g].rearrange("p s h -> p (s h)"),
                           ident_f, [WSf, smP], [bk])
                    cp("act", dstT.ap[:, g0 + g4 * 4:g0 + g4 * 4 + 4, :],
                       bk.ap.rearrange("p (j q) -> p j q", j=4), [bk], [dstT])
        if "ssmw" in debug:
            dump("MT", MT.ap, [128, 32, 128], BF16, [MT])
            dump("L1b", L1b.ap, [128, 32, 128], BF16, [L1b])
            dump("L2b", L2b.ap, [128, 32, 128], BF16, [L2b])
            dump("WS1T", WS1T.ap, [128, 32, 128], BF16, [WS1T])
            dump("WS2T", WS2T.ap, [128, 32, 128], BF16, [WS2T])
            dump("rp", rp.ap, [128, 32], F32, [rp])
            dump("f8p", f8p.ap, [128, 32], F32, [f8p])

        ar.seek(161 * KB, T1)
        tabb = [ar.alloc([2, 512], F32) for _ in range(3)]
        ar.seek(173 * KB, T1)
        YT = ar.alloc([4, L], BF16)
        ar.seek(40 * KB, 72 * KB)
        t1s = [ar.alloc([512], F32) for _ in range(2)]
        t2s = [ar.alloc([512], F32) for _ in range(2)]
        cms = [ar.alloc([512], F32) for _ in range(2)]
        wss = [ar.alloc([512], F32) for _ in range(2)]
        V1s = [ar.alloc([520], BF16) for _ in range(2)]
        V2s = [ar.alloc([520], BF16) for _ in range(2)]
        for v in V1s + V2s:
            memset("pool", v.ap[:, 0:8], 0.0, [v])
        zb = [banks[0], banks[1], banks[2], banks[3]]
        yb = [banks[4], banks[5]]

        def load_tab(g):
            dma("sp", tabb[g % 3].ap, tabs_d[g], [tabs_res.res[g // 2]], [tabb[g % 3]])

        def z_part(g):
            z1, z2 = zb[(g * 2) % 4], zb[(g * 2 + 1) % 4]
            mm(z1.ap, WS1T.ap[:, g, :], Ug.ap[:, g, :], True, True, [WS1T, Ug.res[g]], [z1])
            mm(z2.ap, WS2T.ap[:, g, :], Ug.ap[:, g, :], True, True, [WS2T, Ug.res[g]], [z2])

        for g in range(2):
            load_tab(g)
        wg = ar.alloc([4, 1024], BF16)
        dma("pool", wg.ap, w_glu_d.rearrange("(kc p) j -> p kc j", p=128), [], [wg])
        tpi = [0]

        def yt_transposes(j):
            for cb in range(4):
                b = 6 + tpi[0] % 2
                tpi[0] += 1
                bkb = bank_bf(b)
                for t in range(8):
                    tr(bkb[:, t * 128:(t + 1) * 128], Yc.ap[:, cb, t, j * 128:(j + 1) * 128], identb.ap,
                       [Yc.res[cb], identb], [banks[b]])
                dst = YT.ap[:, j, cb * 1024:(cb + 1) * 1024].rearrange("p (c t) -> p t c", t=8)
                cp("act", dst, bkb.rearrange("p (t c) -> p t c", t=8), [banks[b]], [YT])

        z_part(0)
        for g in range(32):
            if g + 2 < 32:
                load_tab(g + 2)
            if g + 1 < 32:
                z_part(g + 1)
            cosT, sinT = tabb[g % 3].ap[:, 0, :], tabb[g % 3].ap[:, 1, :]
            cosR = sinR = tabb[g % 3]
            z1, z2 = zb[(g * 2) % 4], zb[(g * 2 + 1) % 4]
            t1, t2, cm, ws = t1s[g % 2], t2s[g % 2], cms[g % 2], wss[g % 2]
            v1, v2 = V1s[g % 2], V2s[g % 2]
            tt("dve", t1.ap, z1.ap, cosT, ALU.mult, [z1, cosR], [t1])
            tt("dve", t2.ap, z2.ap, sinT, ALU.mult, [z2, sinR], [t2])
            tt("dve", cm.ap, t1.ap, t2.ap, ALU.add, [t1, t2], [cm])
            scan(ws, rp.ap[:, g:g + 1].to_broadcast([128, 512]), cm, [rp])
            tt("pool", v1.ap[:, 8:520], ws.ap, cosT, ALU.mult, [ws, cosR], [v1])
            tt("pool", v2.ap[:, 8:520], ws.ap, sinT, ALU.mult, [ws, sinR], [v2])
            for cb in range(4):
                ybk = yb[cb // 2]
                c0 = (cb % 2) * 256 + (g % 2) * 128
                o = ybk.ap[:, c0:c0 + 128]
                mm(o, Ug.ap[:, g, cb * 128:(cb + 1) * 128], MT.ap[:, g, :], True, False,
                   [Ug.res[g], MT], [ybk])
                mm(o, v1.ap[:, 7 + cb * 128:7 + (cb + 1) * 128], L1b.ap[:, g, :], False, False,
                   [v1, L1b], [ybk])
                mm(o, v2.ap[:, 7 + cb * 128:7 + (cb + 1) * 128], L2b.ap[:, g, :], False, True,
                   [v2, L2b], [ybk])
            if g % 2 == 1:
                g2 = g - 1
                for cb in range(4):
                    ybk = yb[cb // 2]
                    src = ybk.ap[:, (cb % 2) * 256:(cb % 2) * 256 + 256]
                    dst = Yc.ap[:, cb, :, g2 * 16:(g2 + 2) * 16].rearrange("p t (g h) -> p g t h", g=2)
                    act(dst, src.rearrange("p (g t h) -> p g t h", g=2, t=8), AF.Gelu_apprx_tanh,
                        [ybk], [Yc.res[cb]])
            if g % 8 == 7:
                yt_transposes(g // 8)
        if "Yc" in debug:
            dump("Yc", Yc.ap, [128, 4, 8, 512], BF16, [Yc])
        ar.seek(8 * KB, 40 * KB)
        Y2T = ar.alloc([4, L], BF16)
        ar.seek(T0, 161 * KB)
        sgs = [ar.alloc([512], F32) for _ in range(4)]
        gi_ = 0
        for m in range(4):
            for nt in range(8):
                ba, bb_ = banks[(gi_ * 2) % 8], banks[(gi_ * 2 + 1) % 8]
                sgb = sgs[gi_ % 4]
                gi_ += 1
                for kc in range(4):
                    mm(ba.ap, wg.ap[:, kc, m * 128:(m + 1) * 128], YT.ap[:, kc, nt * 512:(nt + 1) * 512],
                       kc == 0, kc == 3, [wg, YT], [ba])
                for kc in range(4):
                    mm(bb_.ap, wg.ap[:, kc, 512 + m * 128:512 + (m + 1) * 128],
                       YT.ap[:, kc, nt * 512:(nt + 1) * 512], kc == 0, kc == 3, [wg, YT], [bb_])
                act(sgb.ap, bb_.ap, AF.Sigmoid, [bb_], [sgb])
                tt("dve", Y2T.ap[:, m, nt * 512:(nt + 1) * 512], ba.ap, sgb.ap, ALU.mult, [ba, sgb], [Y2T])
        if "Y2T" in debug:
            dump("Y2T", Y2T.ap, [128, 4, L], BF16, [Y2T])

    if "P3" in stages:
        ar.seek(40 * KB, 104 * KB)
        xs3 = [ar.alloc([D], F32) for _ in range(8)]
        ring = [ar.alloc([4096], BF16) for _ in range(4)]
        ar.seek(T0, T1)
        lnr = [ar.alloc([D], F32) for _ in range(2)]
        xTb = ar.alloc([8, 1024], BF16, nres=8)
        aT = ar.alloc([16, 1024], BF16)
        mixin = Buf(aT.ap[:, 0:8, :], 1)
        mixin.res = aT.res
        wsp = ar.alloc([4, 1024], BF16)
        wap = ar.alloc([2, 1024], BF16)
        dma("pool", wsp.ap, wsp_d.rearrange("(kc p) j -> p kc j", p=128), [], [wsp])
        dma("pool", wap.ap, wap_d.rearrange("(kc p) j -> p kc j", p=128), [], [wap])
        gsb = [ar.alloc([512], F32) for _ in range(2)]
        gab = [ar.alloc([512], F32) for _ in range(2)]
        tm1 = [ar.alloc([512], F32) for _ in range(2)]
        tm2 = [ar.alloc([512], F32) for _ in range(2)]
        sqb = tm1
        st6 = ar.alloc([12], F32)
        mv = ar.alloc([2], F32)
        rstd = ar.alloc([1], F32)
        bgate = sp_("bgate")
        ri = 0
        bi = 0

        def nb():
            nonlocal bi
            b = banks[bi % 8]
            bi += 1
            return b

        def load_piece(src_aps):
            nonlocal ri
            slot = ring[ri % 4]
            ri += 1
            for (v, src) in src_aps:
                dma("pool", v(slot), src, [], [slot])
            return slot

        def layer_norm(xb, which):
            P.add("dve", lambda e: e.bn_stats(st6.ap[:, 0:6], xb.ap[:, 0:512]), R(xb), R(st6))
            P.add("dve", lambda e: e.bn_stats(st6.ap[:, 6:12], xb.ap[:, 512:1024]), R(xb, st6), R(st6))
            P.add("dve", lambda e: e.bn_aggr(mv.ap, st6.ap), R(st6), R(mv))
            act(rstd.ap, mv.ap[:, 1:2], AF.Sqrt, [mv], [rstd], bias=LN_EPS)
            P.add("dve", lambda e: e.reciprocal(rstd.ap, rstd.ap), R(rstd), R(rstd))
            stt(xb.ap, xb.ap, mv.ap[:, 0:1], lnr[0].ap, ALU.subtract, ALU.mult, [xb, mv, lnr[0]], [xb])
            stt(xb.ap, xb.ap, rstd.ap[:, 0:1], lnr[1].ap, ALU.mult, ALU.add, [xb, rstd, lnr[1]], [xb])

        def to_featmajor(xb, t):
            for half in range(2):
                bk = nb()
                for j in range(4):
                    kc = half * 4 + j
                    tr(bk.ap[:, j * 128:(j + 1) * 128], xb.ap[:, kc * 128:(kc + 1) * 128], ident_f,
                       [xb, smP], [bk])
                cp("act", xTb.ap[:, half * 4:half * 4 + 4, t * 128:(t + 1) * 128],
                   bk.ap.rearrange("p (j t) -> p j t", j=4), [bk], [xTb.res[t]])

        v3 = lambda kcn, c0, nc_: (lambda slot: slot.ap.rearrange("p (k c) -> p k c", k=kcn)[:, :, c0:c0 + nc_])
        for blk in range(4):
            t0 = blk * 1024
            dma("sp", xTb.ap, xTs_d[:, :, t0:t0 + 1024], [xTs_res.res[blk]], [xTb])
            for t in range(8):
                dma("sp", xs3[t].ap, x_d[t0 + t * 128:t0 + (t + 1) * 128, :], [], [xs3[t]])
            for mq in range(4):
                gcol = 2816 + mq * 256
                piece = load_piece([
                    (v3(8, 0, 256), w_in_d[:, gcol:gcol + 256].rearrange("(kc p) j -> p kc j", p=128)),
                    (v3(8, 256, 256), w_in_d[:, gcol + 1024:gcol + 1280].rearrange("(kc p) j -> p kc j", p=128)),
                ])
                pv = piece.ap.rearrange("p (k c) -> p k c", k=8)
                for m2 in range(2):
                    m = mq * 2 + m2
                    for nh in range(2):
                        cs = slice(nh * 512, (nh + 1) * 512)
                        gs_, ga_, t1_, t2_ = gsb[nh], gab[nh], tm1[nh], tm2[nh]
                        b_gs, b_ga, b_ps, b_pa = nb(), nb(), nb(), nb()
                        for kc in range(8):
                            mm(b_gs.ap, pv[:, kc, m2 * 128:(m2 + 1) * 128], xTb.ap[:, kc, cs], kc == 0, kc == 7,
                               [piece, xTb], [b_gs])
                        for kc in range(8):
                            mm(b_ga.ap, pv[:, kc, 256 + m2 * 128:256 + (m2 + 1) * 128], xTb.ap[:, kc, cs],
                               kc == 0, kc == 7, [piece, xTb], [b_ga])
                        for kc in range(4):
                            mm(b_ps.ap, wsp.ap[:, kc, m * 128:(m + 1) * 128],
                               Y2T.ap[:, kc, t0 + nh * 512:t0 + (nh + 1) * 512], kc == 0, kc == 3,
                               [wsp, Y2T], [b_ps])
                        for kc in range(2):
                            mm(b_pa.ap, wap.ap[:, kc, m * 128:(m + 1) * 128],
                               yattnT.ap[:, kc, t0 + nh * 512:t0 + (nh + 1) * 512], kc == 0, kc == 1,
                               [wap, yattnT], [b_pa])
                        act(gs_.ap, b_gs.ap, AF.Sigmoid, [b_gs, smP], [gs_], bias=bgate[:, m:m + 1])
                        act(ga_.ap, b_ga.ap, AF.Sigmoid, [b_ga, smP], [ga_], bias=bgate[:, 8 + m:9 + m])
                        tt("dve", t1_.ap, b_ps.ap, gs_.ap, ALU.mult, [b_ps, gs_], [t1_])
                        tt("dve", t2_.ap, b_pa.ap, ga_.ap, ALU.mult, [b_pa, ga_], [t2_])
                        tt("dve", mixin.ap[:, m, cs], t1_.ap, t2_.ap, ALU.add, [t1_, t2_], [mixin])
            wo = [load_piece([(v3(8, 0, 512), wout_d[:, oh * 512:(oh + 1) * 512]
                               .rearrange("(kc p) j -> p kc j", p=128))]) for oh in range(2)]
            for i, r_ in enumerate(lnr):
                dma("sp", r_.ap, lnrows_d[i], [], [r_])
            for t in range(8):
                for oh in range(2):
                    bk = nb()
                    for kc in range(8):
                        mm(bk.ap, mixin.ap[:, kc, t * 128:(t + 1) * 128],
                           wo[oh].ap.rearrange("p (k c) -> p k c", k=8)[:, kc, :],
                           kc == 0, kc == 7, [mixin, wo[oh]], [bk])
                    xo = xs3[t].ap[:, oh * 512:(oh + 1) * 512]
                    stt(xo, xo, ALPHA, bk.ap, ALU.mult, ALU.add, [xs3[t], bk], [xs3[t]])
                layer_norm(xs3[t], 0)
                if t >= 1:
                    to_featmajor(xs3[t - 1], t - 1)
            to_featmajor(xs3[7], 7)
            for h2 in range(2):
                for up in range(4):
                    c0 = h2 * 2048 + up * 512
                    piece = load_piece([(v3(8, 0, 512),
                                         wup_d[:, c0:c0 + 512].rearrange("(kc p) j -> p kc j", p=128))])
                    pv = piece.ap.rearrange("p (k c) -> p k c", k=8)
                    for m4 in range(4):
                        mloc = up * 4 + m4
                        for nh in range(2):
                            cs = slice(nh * 512, (nh + 1) * 512)
                            bk = nb()
                            for kc in range(8):
                                mm(bk.ap, pv[:, kc, m4 * 128:(m4 + 1) * 128], xTb.ap[:, kc, cs], kc == 0, kc == 7,
                                   [piece, xTb], [bk])
                            sq_ = sqb[nh]
                            act(sq_.ap, bk.ap, AF.Square, [bk], [sq_])
                            stt(aT.ap[:, mloc, cs], bk.ap, 0.0, sq_.ap, ALU.is_gt, ALU.mult, [bk, sq_], [aT])
                def dn_piece(oh, kh):
                    r0 = h2 * 2048 + kh * 1024
                    return load_piece([(v3(8, 0, 512), wdn_d[r0:r0 + 1024, oh * 512:(oh + 1) * 512]
                                        .rearrange("(kc p) j -> p kc j", p=128))])

                def dn_tile(t, oh, pcs):
                    bk = nb()
                    for kc in range(16):
                        pc_ = pcs[kc // 8]
                        mm(bk.ap, aT.ap[:, kc, t * 128:(t + 1) * 128],
                           pc_.ap.rearrange("p (k c) -> p k c", k=8)[:, kc % 8, :], kc == 0, kc == 15,
                           [aT, pc_], [bk])
                    xo = xs3[t].ap[:, oh * 512:(oh + 1) * 512]
                    if h2 == 0:
                        stt(xo, xo, ALPHA, bk.ap, ALU.mult, ALU.add, [xs3[t], bk], [xs3[t]])
                    else:
                        tt("dve", xo, xo, bk.ap, ALU.add, [xs3[t], bk], [xs3[t]])

                if h2 == 0:
                    for oh in range(2):
                        pcs = [dn_piece(oh, kh) for kh in range(2)]
                        for t in range(8):
                            dn_tile(t, oh, pcs)
                else:
                    pcs2 = [[dn_piece(oh, kh) for kh in range(2)] for oh in range(2)]
                    for i, r_ in enumerate(lnr):
                        dma("sp", r_.ap, lnrows_d[2 + i], [], [r_])
                    for t in range(8):
                        for oh in range(2):
                            dn_tile(t, oh, pcs2[oh])
                        layer_norm(xs3[t], 1)
                        dma("sp", out_d[t0 + t * 128:t0 + (t + 1) * 128, :], xs3[t].ap, [xs3[t]], [])

    P.finalize(nc, es)
    with nc.Block() as block:
        @block.tensor
        def _(e):
            P.emit_engine("pe", e)

        @block.scalar
        def _(e):
            P.emit_engine("act", e)

        @block.vector
        def _(e):
            P.emit_engine("dve", e)

        @block.gpsimd
        def _(e):
            P.emit_engine("pool", e)

        @block.sync
        def _(e):
            P.emit_engine("sp", e)
    es.close()
    nops = {k: len(v) for k, v in P.ops.items()}
    return nc, dbg, nops


def make_in_maps(inputs):
    g = lambda k: np.asarray(inputs[k], np.float32)
    smP = build_smalls_p(g("b_gate")[0], g("lambda_re")[0], g("lambda_im")[0], g("log_dt")[0])
    smB = build_smalls_b(g("lambda_re")[0], g("lambda_im")[0], g("log_dt")[0], g("ssm_b_re")[0],
                         g("ssm_b_im")[0], g("ssm_c_re")[0], g("ssm_c_im")[0], g("ssm_d")[0])
    biasT = build_bias_tables(g("rel_bias"))
    lnrows = np.stack([np.broadcast_to(g(k)[0][None, :], (128, D)) for k in ("ln1_g", "ln1_b", "ln2_g", "ln2_b")])
    lnrows = np.ascontiguousarray(lnrows, np.float32)
    shared = {
        "w_in": np.ascontiguousarray(g("w_in")[0]), "smalls_p": smP, "smalls_b": smB,
        "w_glu": np.ascontiguousarray(g("w_glu")[0]), "bias_t": biasT,
        "w_ssm_proj": np.ascontiguousarray(g("w_ssm_proj")[0]),
        "w_attn_proj": np.ascontiguousarray(g("w_attn_proj")[0]),
        "w_out": np.ascontiguousarray(g("w_out")[0]), "w_up": np.ascontiguousarray(g("w_up")[0]),
        "w_down": np.ascontiguousarray(g("w_down")[0]), "ln_rows": lnrows,
    }
    x = g("x")
    return [dict(shared, x=np.ascontiguousarray(x[b])) for b in range(x.shape[0])]


_CACHE = {}


def kernel(**inputs):
    in_maps = make_in_maps(inputs)
    if "nc" not in _CACHE:
        _CACHE["nc"] = build_program()[0]
    nc = _CACHE["nc"]
    res = run_bass_kernel_spmd(nc, in_maps, core_ids=list(range(NCORES)))
    return np.stack([np.asarray(r["out"], np.float32) for r in res.results], 0)
```
